# Optimizing a Trainium2 kernel written in Bass

```python
import math
import jax, jax.numpy as jnp
from jax import lax
import numpy as np

D_MODEL = 1024
BATCH = 4
SEQ = 8192
DEPTH = 2

ATT_HEADS = 4
ATT_HEAD_DIM = 64
ATT_V_DIM = 2 * ATT_HEAD_DIM
ATT_QK = ATT_HEADS * 2 * ATT_HEAD_DIM
ATT_WIDTH = ATT_HEADS * ATT_V_DIM
ROPE_THETA = 500000.0
ROPE_DIM = ATT_HEAD_DIM // 4
Q_BLOCK = 128

SSM_HEAD_DIM = 64
SSM_WIDTH = D_MODEL
SSM_HEADS = SSM_WIDTH // SSM_HEAD_DIM
SSM_GROUPS = 2
SSM_STATE = 128
SSM_CONV = 4
SSM_CHUNK = 128
CONV_CH = SSM_WIDTH + 2 * SSM_GROUPS * SSM_STATE

EPS = 1e-5

_SIZES = [ATT_QK, ATT_QK, ATT_WIDTH, ATT_WIDTH,
          SSM_WIDTH, CONV_CH, SSM_HEADS,
          D_MODEL, D_MODEL]
IN_COLS = sum(_SIZES)
_OFFS = [int(v) for v in np.cumsum([0] + _SIZES)]

kernel_name = "hybrid_diffattn_ssd_gated_merge"


def rms_norm(x, w):
    xf = x.astype(jnp.float32)
    y = xf * lax.rsqrt(jnp.mean(xf * xf, axis=-1, keepdims=True) + EPS)
    return (y * w.astype(jnp.float32)).astype(x.dtype)


def rope_partial(t, cos, sin):
    half = ROPE_DIM // 2
    tr = t[..., :ROPE_DIM].astype(jnp.float32)
    t1, t2 = tr[..., :half], tr[..., half:]
    cc = cos[:, :, None, None, :]
    ss = sin[:, :, None, None, :]
    rot = jnp.concatenate([t1 * cc - t2 * ss, t2 * cc + t1 * ss], axis=-1)
    return jnp.concatenate([rot.astype(t.dtype), t[..., ROPE_DIM:]], axis=-1)


def diff_attention(q, k, v, lam):
    bsz, seq = q.shape[0], q.shape[1]
    nb = seq // Q_BLOCK
    scale = ATT_HEAD_DIM ** -0.5
    kf = k.astype(jnp.float32)
    vf = v.astype(jnp.float32)
    lamf = lam.astype(jnp.float32)
    qb = q.astype(jnp.float32).reshape(bsz, nb, Q_BLOCK, ATT_HEADS, 2, ATT_HEAD_DIM)
    qb = qb.transpose(1, 0, 2, 3, 4, 5)
    key_idx = jnp.arange(seq)

    def block(args):
        qi, bi = args
        s = jnp.einsum('bqhtd,bkhtd->bhtqk', qi, kf) * scale
        q_idx = bi * Q_BLOCK + jnp.arange(Q_BLOCK)
        mask = key_idx[None, :] <= q_idx[:, None]
        s = jnp.where(mask, s, -jnp.inf)
        p = jax.nn.softmax(s, axis=-1)
        a = p[:, :, 0] - lamf * p[:, :, 1]
        return jnp.einsum('bhqk,bkhe->bqhe', a, vf)

    out = lax.map(block, (qb, jnp.arange(nb)))
    return out.transpose(1, 0, 2, 3, 4).reshape(bsz, seq, ATT_HEADS, ATT_V_DIM)


def causal_dwconv(u, w, b):
    out = lax.conv_general_dilated(
        u, w[:, None, :], window_strides=(1,), padding=[(SSM_CONV - 1, 0)],
        dimension_numbers=('NWC', 'WIO', 'NWC'), feature_group_count=u.shape[-1])
    return out + b


def ssd_chunked(x, dt, a, b_mat, c_mat):
    bsz, seq = x.shape[0], x.shape[1]
    L = SSM_CHUNK
    nc = seq // L
    G, J, P, N = SSM_GROUPS, SSM_HEADS // SSM_GROUPS, SSM_HEAD_DIM, SSM_STATE
    xd = (x.astype(jnp.float32) * dt[..., None]).reshape(bsz, nc, L, G, J, P)
    adt = (dt * a).reshape(bsz, nc, L, G, J).transpose(0, 3, 4, 1, 2)
    acs = jnp.cumsum(adt, axis=-1)
    bc = b_mat.astype(jnp.float32).reshape(bsz, nc, L, G, N)
    cc = c_mat.astype(jnp.float32).reshape(bsz, nc, L, G, N)
    causal = jnp.tril(jnp.ones((L, L), dtype=bool))
    seg = acs[..., :, None] - acs[..., None, :]
    lmat = jnp.exp(jnp.where(causal, seg, -jnp.inf))
    cb = jnp.einsum('bclgn,bcsgn->bgcls', cc, bc)
    y_diag = jnp.einsum('bgcls,bgjcls,bcsgjp->bclgjp', cb, lmat, xd)
    decay = jnp.exp(acs[..., -1:] - acs)
    states = jnp.einsum('bclgn,bgjcl,bclgjp->bcgjpn', bc, decay, xd)
    chunk_decay = jnp.exp(acs[..., -1])

    def step(h, inp):
        st, dec = inp
        return h * dec[..., None, None] + st, h

    h0 = jnp.zeros((bsz, G, J, P, N), jnp.float32)
    _, prev = lax.scan(step, h0, (states.transpose(1, 0, 2, 3, 4, 5),
                                  chunk_decay.transpose(3, 0, 1, 2)))
    prev = prev.transpose(1, 0, 2, 3, 4, 5)
    y_off = jnp.einsum('bclgn,bcgjpn,bgjcl->bclgjp', cc, prev, jnp.exp(acs))
    return (y_diag + y_off).reshape(bsz, seq, SSM_HEADS, P)


def hybrid_layer(x, c, cos, sin, lambda_init, w_ada, b_ada, norm_w, w_in,
                 lambda_q1, lambda_k1, lambda_q2, lambda_k2, attn_subln_w, w_att_branch,
                 conv_w, conv_b, dt_bias, a_log, d_skip, ssm_norm_w, w_ssm_branch, w_out):
    bsz, seq = x.shape[0], x.shape[1]
    mod = c @ w_ada + b_ada
    shift, scale, gate = jnp.split(mod, 3, axis=-1)
    h = rms_norm(x, norm_w) * (1 + scale[:, None, :]) + shift[:, None, :]

    proj = h @ w_in
    q, k, v, att_g, z, xbc, dt_raw, mg_att, mg_ssm = [
        proj[..., _OFFS[i]:_OFFS[i + 1]] for i in range(len(_SIZES))]

    q = rope_partial(q.reshape(bsz, seq, ATT_HEADS, 2, ATT_HEAD_DIM), cos, sin)
    k = rope_partial(k.reshape(bsz, seq, ATT_HEADS, 2, ATT_HEAD_DIM), cos, sin)
    v = v.reshape(bsz, seq, ATT_HEADS, ATT_V_DIM)
    lam = (jnp.exp(jnp.sum(lambda_q1 * lambda_k1)) - jnp.exp(jnp.sum(lambda_q2 * lambda_k2))
           + lambda_init)
    o = diff_attention(q, k, v, lam)
    o = rms_norm(o, attn_subln_w) * (1.0 - lambda_init)
    o = o.reshape(bsz, seq, ATT_WIDTH).astype(x.dtype) * jax.nn.silu(att_g)
    y_att = o @ w_att_branch

    xbc = jax.nn.silu(causal_dwconv(xbc, conv_w, conv_b))
    xs = xbc[..., :SSM_WIDTH]
    bm = xbc[..., SSM_WIDTH:SSM_WIDTH + SSM_GROUPS * SSM_STATE].reshape(bsz, seq, SSM_GROUPS, SSM_STATE)
    cm = xbc[..., SSM_WIDTH + SSM_GROUPS * SSM_STATE:].reshape(bsz, seq, SSM_GROUPS, SSM_STATE)
    xs = xs.reshape(bsz, seq, SSM_HEADS, SSM_HEAD_DIM)
    dt = jax.nn.softplus(dt_raw.astype(jnp.float32) + dt_bias.astype(jnp.float32))
    a = -jnp.exp(a_log.astype(jnp.float32))
    y = ssd_chunked(xs, dt, a, bm, cm)
    y = y + d_skip.astype(jnp.float32)[:, None] * xs.astype(jnp.float32)
    y = y.reshape(bsz, seq, SSM_WIDTH) * jax.nn.silu(z.astype(jnp.float32))
    y = rms_norm(y.reshape(bsz, seq, SSM_GROUPS, SSM_WIDTH // SSM_GROUPS),
                 ssm_norm_w.reshape(SSM_GROUPS, SSM_WIDTH // SSM_GROUPS))
    y = y.reshape(bsz, seq, SSM_WIDTH).astype(x.dtype)
    y_ssm = y @ w_ssm_branch

    merged = jax.nn.sigmoid(mg_att) * y_att + jax.nn.sigmoid(mg_ssm) * y_ssm
    return x + gate[:, None, :] * (merged @ w_out)


def setup_inputs(seed: int = 0) -> dict:
    key = jax.random.key(seed)
    ks = jax.random.split(key, 24)
    f32 = jnp.float32

    def nrm(k, shape, s):
        return jax.random.normal(k, shape, f32) * s

    x = nrm(ks[0], (BATCH, SEQ, D_MODEL), 1.0)
    c = nrm(ks[1], (BATCH, D_MODEL), 1.0)
    offsets = jax.random.randint(ks[2], (BATCH, 1), 0, 4096, dtype=jnp.int32)
    positions = offsets + jnp.arange(SEQ, dtype=jnp.int32)[None, :]
    w_ada = nrm(ks[3], (DEPTH, D_MODEL, 3 * D_MODEL), 0.5 * D_MODEL ** -0.5)
    b_ada = nrm(ks[4], (DEPTH, 3 * D_MODEL), 0.02)
    norm_w = 1.0 + nrm(ks[5], (DEPTH, D_MODEL), 0.02)
    w_in = nrm(ks[6], (DEPTH, D_MODEL, IN_COLS), D_MODEL ** -0.5)
    lambda_q1 = nrm(ks[7], (DEPTH, ATT_HEAD_DIM), 0.1)
    lambda_k1 = nrm(ks[8], (DEPTH, ATT_HEAD_DIM), 0.1)
    lambda_q2 = nrm(ks[9], (DEPTH, ATT_HEAD_DIM), 0.1)
    lambda_k2 = nrm(ks[10], (DEPTH, ATT_HEAD_DIM), 0.1)
    attn_subln_w = 1.0 + nrm(ks[11], (DEPTH, ATT_V_DIM), 0.02)
    w_att_branch = nrm(ks[12], (DEPTH, ATT_WIDTH, D_MODEL), ATT_WIDTH ** -0.5)
    conv_w = nrm(ks[13], (DEPTH, SSM_CONV, CONV_CH), SSM_CONV ** -0.5)
    conv_b = nrm(ks[14], (DEPTH, CONV_CH), 0.02)
    dt0 = jnp.exp(jax.random.uniform(ks[15], (DEPTH, SSM_HEADS), f32,
                                     math.log(1e-3), math.log(1e-1)))
    dt_bias = dt0 + jnp.log(-jnp.expm1(-dt0))
    a_log = jnp.log(jax.random.uniform(ks[16], (DEPTH, SSM_HEADS), f32, 1.0, 16.0))
    d_skip = 1.0 + nrm(ks[17], (DEPTH, SSM_HEADS), 0.1)
    ssm_norm_w = 1.0 + nrm(ks[18], (DEPTH, SSM_WIDTH), 0.02)
    w_ssm_branch = nrm(ks[19], (DEPTH, SSM_WIDTH, D_MODEL), SSM_WIDTH ** -0.5)
    w_out = nrm(ks[20], (DEPTH, D_MODEL, D_MODEL), D_MODEL ** -0.5)
    final_norm_w = 1.0 + nrm(ks[21], (D_MODEL,), 0.02)
    return {"x": x, "c": c, "positions": positions, "w_ada": w_ada, "b_ada": b_ada,
            "norm_w": norm_w, "w_in": w_in, "lambda_q1": lambda_q1, "lambda_k1": lambda_k1,
            "lambda_q2": lambda_q2, "lambda_k2": lambda_k2, "attn_subln_w": attn_subln_w,
            "w_att_branch": w_att_branch, "conv_w": conv_w, "conv_b": conv_b,
            "dt_bias": dt_bias, "a_log": a_log, "d_skip": d_skip, "ssm_norm_w": ssm_norm_w,
            "w_ssm_branch": w_ssm_branch, "w_out": w_out, "final_norm_w": final_norm_w}


def reference(x, c, positions, w_ada, b_ada, norm_w, w_in, lambda_q1, lambda_k1,
              lambda_q2, lambda_k2, attn_subln_w, w_att_branch, conv_w, conv_b,
              dt_bias, a_log, d_skip, ssm_norm_w, w_ssm_branch, w_out, final_norm_w):
    inv_freq = ROPE_THETA ** (-jnp.arange(0, ROPE_DIM, 2, dtype=jnp.float32) / ROPE_DIM)
    ang = positions.astype(jnp.float32)[..., None] * inv_freq
    cos, sin = jnp.cos(ang), jnp.sin(ang)
    for l in range(DEPTH):
        lambda_init = 0.8 - 0.6 * math.exp(-0.3 * l)
        x = hybrid_layer(x, c, cos, sin, lambda_init, w_ada[l], b_ada[l], norm_w[l], w_in[l],
                         lambda_q1[l], lambda_k1[l], lambda_q2[l], lambda_k2[l],
                         attn_subln_w[l], w_att_branch[l], conv_w[l], conv_b[l],
                         dt_bias[l], a_log[l], d_skip[l], ssm_norm_w[l],
                         w_ssm_branch[l], w_out[l])
    return rms_norm(x, final_norm_w)
```

```python
import contextlib
import math

import numpy as np

import concourse.bass as bass
import concourse.mybir as mybir
from concourse.alu_op_type import AluOpType as ALU
from concourse.bass_utils import run_bass_kernel_spmd

F32 = mybir.dt.float32
BF16 = mybir.dt.bfloat16
I32 = mybir.dt.int32
AF = mybir.ActivationFunctionType

D = 1024
S = 8192
NT = S // 128
NG = S // 512
import os as _os
NGL = int(_os.environ.get('KDBG_NG', NG))
DEPTH = 2
IN_COLS = 6672
O_Q, O_K, O_V, O_G, O_Z, O_X, O_DT, O_MA, O_MS = 0, 512, 1024, 1536, 2048, 3072, 4608, 4624, 5648
EPS = 1e-5
ROPE_THETA = 500000.0

SM = {}
_o = 0
for _n, _w in [("bada", 16), ("nw", 8), ("dtb", 16), ("alog", 16), ("dsk", 16), ("cw", 48), ("cb", 12),
               ("subw", 128), ("lq1", 64), ("lk1", 64), ("lq2", 64), ("lk2", 64), ("bgate", 1024),
               ("ssmw", 1024)]:
    SM[_n] = (_o, _o + _w)
    _o += _w
NS = _o


class _Stop(Exception):
    pass


KSTOP = int(_os.environ.get('KDBG_STOP', -1))


_DEAD = [False]


def dbg_stop(k):
    if k == KSTOP:
        _DEAD[0] = True


class Buf:
    __slots__ = ("w", "r", "rd", "name", "psum")

    def __init__(self, name="", psum=False):
        self.psum = psum
        self.w = None
        self.r = {}
        self.rd = []
        self.name = name


_G = {}
INTERLEAVE_BC = _os.environ.get('KDBG_NOBC') is None
BC_RATIO = float(_os.environ.get('KDBG_BCR', 1.0))


def emit_rstd(nc, sc, out, tmp, in_, scale, n, reads, writes):
    V, P = nc.vector, nc.gpsimd
    sc.op("dve", lambda: V.tensor_scalar(out=tmp, in0=in_, scalar1=scale, scalar2=EPS, op0=ALU.mult, op1=ALU.add),
          reads=reads, writes=writes)
    sc.op("pool", lambda: P.tensor_tensor(out=out, in0=tmp, in1=_G["negh"][:, 0:n], op=ALU.pow),
          reads=list(writes) + [_G["b_negh"]], writes=writes)


class Sched:
    LIM = 30000
    NQ = 16

    def __init__(self, nc, es):
        self.nc = nc
        self.es = es
        self.engs = {"pe": nc.tensor, "act": nc.scalar, "dve": nc.vector, "pool": nc.gpsimd, "sp": nc.sync}
        self.cnt = {"pe": 0, "act": 0, "dve": 0, "pool": 0}
        self.csem = {e: [] for e in self.cnt}
        self.dcnt = {"sp": 0, "pool": 0}
        self.dsem = {q: [es.enter_context(nc.semaphore(f"d_{q}_{i}")) for i in range(self.NQ)] for q in self.dcnt}
        self.seen = {e: {} for e in self.engs}

    def _csem(self, e, gen):
        while len(self.csem[e]) <= gen:
            self.csem[e].append(self.es.enter_context(self.nc.semaphore(f"c_{e}_{len(self.csem[e])}")))
        return self.csem[e][gen]

    def _wait(self, E, ev):
        kind, src, n = ev
        if kind == "c":
            if src == "pe" and E == "pe":
                return
            key = ("c", src)
            if self.seen[E].get(key, 0) >= n:
                return
            gen, val = (n - 1) // self.LIM, (n - 1) % self.LIM + 1
            self.engs[E].wait_ge(self._csem(src, gen), val)
            self.seen[E][key] = n
        else:
            slot, val = n % self.NQ, 16 * (n // self.NQ + 1)
            key = ("d", src, slot)
            if self.seen[E].get(key, 0) >= val:
                return
            self.engs[E].wait_ge(self.dsem[src][slot], val)
            self.seen[E][key] = val

    def _deps(self, E, reads, writes):
        for b in reads:
            if b.w is not None:
                self._wait(E, b.w)
            if b.psum:
                for e2, ev in list(b.r.items()):
                    if e2 != E:
                        self._wait(E, ev)
        for b in writes:
            if b.w is not None:
                self._wait(E, b.w)
            for ev in b.r.values():
                self._wait(E, ev)
            for ev in b.rd:
                self._wait(E, ev)

    def op(self, E, inst_fn, reads=(), writes=()):
        if _DEAD[0]:
            return None
        self._deps(E, reads, writes)
        inst = inst_fn()
        self.cnt[E] += 1
        n = self.cnt[E]
        inst.then_inc(self._csem(E, (n - 1) // self.LIM), 1)
        ev = ("c", E, n)
        for b in reads:
            b.r[E] = ev
        for b in writes:
            b.w = ev
            b.r = {}
            b.rd = []
        return ev

    def dma(self, q, out, in_, reads=(), writes=()):
        if _DEAD[0]:
            return None
        n = self.dcnt[q]
        if n >= self.NQ:
            self._wait(q, ("d", q, n - self.NQ))
        self._deps(q, reads, writes)
        inst = self.engs[q].dma_start(out=out, in_=in_)
        inst.then_inc(self.dsem[q][n % self.NQ], 16)
        ev = ("d", q, n)
        self.dcnt[q] += 1
        for b in reads:
            b.rd.append(ev)
        for b in writes:
            b.w = ev
            b.r = {}
            b.rd = []
        return ev

    def wait_all(self, E, bufs):
        for b in bufs:
            if b.w is not None:
                self._wait(E, b.w)


_DUMPS = []


def dbg_dump(nc, sc, name, ap, shape, dt, reads):
    d = nc.dram_tensor("dbg_" + name, list(shape), dt, kind="ExternalOutput").ap()
    b = Buf()
    sc.dma("pool", d, ap, reads=reads, writes=[b])
    _DUMPS.append(b)


def build_program(n_layers=DEPTH, phases="ABCD", debug=False):
    nc = bass.Bass("TRN2", target_bir_lowering=False)
    es = contextlib.ExitStack()
    with es:
        _emit(nc, es, n_layers, phases, debug)
    return nc


def _emit(nc, es, n_layers, phases, debug):
    sc = Sched(nc, es)
    del _DUMPS[:]
    V, A, T, P = nc.vector, nc.scalar, nc.tensor, nc.gpsimd

    def din(name, shape, dt=F32):
        return nc.dram_tensor(name, list(shape), dt, kind="ExternalInput").ap()

    def dscr(name, shape, dt):
        kind = "ExternalOutput" if debug else "Internal"
        return nc.dram_tensor(name, list(shape), dt, kind=kind).ap()

    x_in = din("x", [S, D])
    pos_in = din("pos", [128, NT], I32)
    invf_in = din("invf", [128, 8])
    cst_in = din("cst", [128, 4, 128])
    cT2_in = din("cT2", [128, 8, 2])
    cB_in = din("cB", [128, 8, 128])
    sm_in = din("smalls", [DEPTH, 128, NS])
    fnw_in = din("fnw", [128, D])
    wada_in = din("w_ada", [DEPTH, D, 3 * D])
    win_in = din("w_in", [DEPTH, D, IN_COLS])
    watt_in = din("w_att", [DEPTH, 512, D])
    wssm_in = din("w_ssm", [DEPTH, D, D])
    wout_in = din("w_out", [DEPTH, D, D])
    y_out = nc.dram_tensor("y", [S, D], F32, kind="ExternalOutput").ap()

    QTd = dscr("QTd", [4, 128, S], BF16)
    KTd = dscr("KTd", [4, 128, S], BF16)
    Vd = dscr("Vd", [S, 512], BF16)
    Gd = dscr("Gd", [S, 512], F32)
    Zd = dscr("Zd", [S, D], F32)
    DTd = dscr("DTd", [S, 16], F32)
    UTd = dscr("UTd", [1536, S], F32)
    MGd = dscr("MGd", [2048, S], F32)
    OGTd = dscr("OGTd", [512, S], BF16)
    YNTd = dscr("YNTd", [D, S], BF16)
    X1d = dscr("X1d", [S, D], F32)

    b_QK = [Buf() for _ in range(NG)]
    b_V = [Buf() for _ in range(NT)]
    b_G = [Buf() for _ in range(NT)]
    b_Z = [Buf() for _ in range(NT)]
    b_DT = [Buf() for _ in range(NT)]
    b_UT = [Buf() for _ in range(NG)]
    b_MG = [Buf() for _ in range(NG)]
    b_OGT = [[Buf() for _ in range(NG)] for _ in range(4)]
    b_YNT = [Buf() for _ in range(NG)]
    b_X1 = [Buf() for _ in range(NT)]
    b_Y = [Buf() for _ in range(NT)]

    _uid = [0]

    def sb(name, shape, dt):
        _uid[0] += 1
        return es2.enter_context(nc.sbuf_tensor(f"sb_{name}_{_uid[0]}", list(shape), dt))

    ps = [es.enter_context(nc.psum_tensor(f"ps{i}", [128, 512], F32)) for i in range(8)]
    psb = [Buf(f"ps{i}", psum=True) for i in range(8)]

    def psbf(i):
        return ps[i][:].bitcast(BF16)

    es2 = es
    cst = sb("cst", [128, 4, 128], F32)
    identb = sb("identb", [128, 128], BF16)
    maskb = sb("maskb", [128, 128], BF16)
    cosT = sb("cosT", [128, NT, 8], F32)
    sinT = sb("sinT", [128, NT, 8], F32)
    cT2 = sb("cT2", [128, 8, 2], F32)
    cB = sb("cB", [128, 8, 128], F32)
    sm = sb("sm", [128, NS], F32)
    A_pk = sb("A_pk", [128, 8], F32)
    Sh_pk = sb("Sh_pk", [128, 8], F32)
    gate_b = sb("gate_b", [128, D], F32)
    b_cst, b_identb, b_maskb, b_cs, b_cT, b_sm, b_mod, b_gate = (Buf() for _ in range(8))
    negh = sb("negh", [128, 8], F32)
    b_negh = Buf()
    sc.op("dve", lambda: V.memset(negh[:], -0.5), writes=[b_negh])
    _G["negh"] = negh
    _G["b_negh"] = b_negh

    ident_f = cst[:, 0, :]
    tri_f = cst[:, 1, :]
    negm_f = cst[:, 2, :]
    ones_f = cst[:, 3, :]

    sc.dma("sp", cst[:], cst_in[:, :, :], writes=[b_cst])
    sc.dma("sp", cT2[:], cT2_in[:, :, :], writes=[b_cT])
    sc.dma("sp", cB[:], cB_in[:, :, :], writes=[b_cT])
    sc.op("dve", lambda: V.tensor_copy(out=identb[:], in_=ident_f), reads=[b_cst], writes=[b_identb])
    sc.op("dve", lambda: V.tensor_copy(out=maskb[:], in_=tri_f), reads=[b_cst], writes=[b_maskb])

    with contextlib.ExitStack() as es2:
        posi = sb("posi", [128, NT], I32)
        posf = sb("posf", [128, NT], F32)
        invf = sb("invf", [128, 8], F32)
        ang = sb("ang", [128, NT, 8], F32)
        kk = sb("kk", [128, NT, 8], F32)
        ki = sb("ki", [128, NT, 8], I32)
        r1 = sb("r1", [128, NT, 8], F32)
        r2 = sb("r2", [128, NT, 8], F32)
        bt = Buf()
        TWO_PI = 2.0 * math.pi
        sc.dma("sp", posi[:], pos_in[:, :], writes=[bt])
        sc.dma("sp", invf[:], invf_in[:, :], writes=[bt])
        sc.op("dve", lambda: V.tensor_copy(out=posf[:], in_=posi[:]), reads=[bt], writes=[bt])
        sc.op("dve", lambda: V.tensor_tensor(out=ang[:], in0=posf[:].unsqueeze(2).to_broadcast([128, NT, 8]),
                                             in1=invf[:].unsqueeze(1).to_broadcast([128, NT, 8]), op=ALU.mult),
              reads=[bt], writes=[bt])

        def reduce_to_pi(src, dst):
            sc.op("dve", lambda: V.tensor_scalar(out=kk[:], in0=src[:], scalar1=1.0 / TWO_PI, scalar2=None,
                                                 op0=ALU.mult), reads=[bt], writes=[bt])
            sc.op("dve", lambda: V.tensor_copy(out=ki[:], in_=kk[:]), reads=[bt], writes=[bt])
            sc.op("dve", lambda: V.tensor_copy(out=kk[:], in_=ki[:]), reads=[bt], writes=[bt])
            sc.op("dve", lambda: V.scalar_tensor_tensor(out=dst[:], in0=kk[:], scalar=-TWO_PI, in1=src[:],
                                                        op0=ALU.mult, op1=ALU.add), reads=[bt], writes=[bt])
            sc.op("dve", lambda: V.tensor_scalar(out=kk[:], in0=dst[:], scalar1=math.pi, scalar2=-TWO_PI,
                                                 op0=ALU.is_gt, op1=ALU.mult), reads=[bt], writes=[bt])
            sc.op("dve", lambda: V.tensor_tensor(out=dst[:], in0=dst[:], in1=kk[:], op=ALU.add),
                  reads=[bt], writes=[bt])
            sc.op("dve", lambda: V.tensor_scalar(out=kk[:], in0=dst[:], scalar1=-math.pi, scalar2=TWO_PI,
                                                 op0=ALU.is_lt, op1=ALU.mult), reads=[bt], writes=[bt])
            sc.op("dve", lambda: V.tensor_tensor(out=dst[:], in0=dst[:], in1=kk[:], op=ALU.add),
                  reads=[bt], writes=[bt])
            sc.op("dve", lambda: V.tensor_scalar(out=dst[:], in0=dst[:], scalar1=math.pi, scalar2=-math.pi,
                                                 op0=ALU.min, op1=ALU.max), reads=[bt], writes=[bt])

        if debug:
            dbg_dump(nc, sc, "invf", invf[:], [128, 8], F32, [bt])
            dbg_dump(nc, sc, "cB", cB[:], [128, 8, 128], F32, [b_cT])
            dbg_dump(nc, sc, "cT2", cT2[:], [128, 8, 2], F32, [b_cT])
            dbg_dump(nc, sc, "posf", posf[:], [128, NT], F32, [bt])
            dbg_dump(nc, sc, "ang", ang[:], [128, NT, 8], F32, [bt])
        reduce_to_pi(ang, r1)
        if debug:
            dbg_dump(nc, sc, "r1", r1[:], [128, NT, 8], F32, [bt])
            dbg_dump(nc, sc, "kk", kk[:], [128, NT, 8], F32, [bt])
        sc.op("act", lambda: A.activation(out=sinT[:], in_=r1[:], func=AF.Sin), reads=[bt], writes=[b_cs])
        sc.op("dve", lambda: V.tensor_scalar(out=r2[:], in0=r1[:], scalar1=math.pi / 2, scalar2=None, op0=ALU.add),
              reads=[bt], writes=[bt])
        reduce_to_pi(r2, r1)
        sc.op("act", lambda: A.activation(out=cosT[:], in_=r1[:], func=AF.Sin), reads=[bt], writes=[b_cs])
        _fence(nc, sc)

    def smv(name):
        a, b = SM[name]
        return sm[:, a:b]

    for l in range(n_layers):
        lambda_init = 0.8 - 0.6 * math.exp(-0.3 * l)
        last = (l == n_layers - 1)
        x_src = x_in if l == 0 else X1d

        sc.dma("sp", sm[:], sm_in[l, :, :], writes=[b_sm])
        with contextlib.ExitStack() as es2:
            wa = sb("wa", [128, 8, 3 * D], F32)
            modsb = sb("modsb", [128, 16], F32)
            b_wa = [Buf() for _ in range(8)]
            bm = Buf()
            for kc in range(8):
                sc.dma("sp", wa[:, kc, :], wada_in[l, kc * 128:(kc + 1) * 128, :], writes=[b_wa[kc]])
            for j in range(16):
                for kc in range(8):
                    sc.op("pe", lambda j=j, kc=kc: T.matmul(ps[0][:, 2 * j:2 * j + 2],
                                                            lhsT=wa[:, kc, j * 128:(j + 1) * 128],
                                                            rhs=cT2[:, kc, :], start=(kc == 0), stop=(kc == 7)),
                          reads=[b_wa[kc], b_cT], writes=[psb[0]])
            for n in range(2):
                for kc in range(8):
                    sc.op("pe", lambda n=n, kc=kc: T.matmul(ps[1 + n][:, :], lhsT=cB[:, kc, :],
                                                            rhs=wa[:, kc, 2 * D + n * 512:2 * D + (n + 1) * 512],
                                                            start=(kc == 0), stop=(kc == 7)),
                          reads=[b_wa[kc], b_cT], writes=[psb[1 + n]])
            sc.op("dve", lambda: V.tensor_tensor(out=modsb[:], in0=ps[0][:, 0:32].rearrange("p (j two) -> p j two", two=2)[:, :, 0], in1=smv("bada"), op=ALU.add),
                  reads=[psb[0], b_sm], writes=[bm])
            sc.op("dve", lambda: V.scalar_tensor_tensor(out=A_pk[:], in0=modsb[:, 8:16], scalar=1.0, in1=smv("nw"),
                                                        op0=ALU.add, op1=ALU.mult),
                  reads=[bm, b_sm], writes=[b_mod])
            sc.op("dve", lambda: V.tensor_copy(out=Sh_pk[:], in_=modsb[:, 0:8]), reads=[bm], writes=[b_mod])
            a0 = SM["bgate"][0]
            for n in range(2):
                sc.op("dve", lambda n=n: V.tensor_tensor(out=gate_b[:, n * 512:(n + 1) * 512], in0=ps[1 + n][:, :],
                                                         in1=sm[:, a0 + n * 512:a0 + (n + 1) * 512], op=ALU.add),
                      reads=[psb[1 + n], b_sm], writes=[b_gate])
            sc.wait_all("dve", [b_mod, b_gate])
            if debug and l == 0:
                dbg_dump(nc, sc, "A_pk", A_pk[:], [128, 8], F32, [b_mod])
                dbg_dump(nc, sc, "Sh_pk", Sh_pk[:], [128, 8], F32, [b_mod])
                dbg_dump(nc, sc, "gate_b", gate_b[:], [128, D], F32, [b_gate])
                dbg_dump(nc, sc, "cos", cosT[:], [128, NT, 8], F32, [b_cs])
                dbg_dump(nc, sc, "sin", sinT[:], [128, NT, 8], F32, [b_cs])
                dbg_dump(nc, sc, "modsb", modsb[:], [128, 16], F32, [bm])
            _fence(nc, sc)

        if "A" in phases:
          try:
            _phase_A(nc, sc, l, x_src, (b_X1 if l > 0 else None), win_in, ps, psb, psbf,
                     dict(QTd=QTd, KTd=KTd, Vd=Vd, Gd=Gd, Zd=Zd, DTd=DTd, UTd=UTd, MGd=MGd),
                     dict(QK=b_QK, V=b_V, G=b_G, Z=b_Z, DT=b_DT, UT=b_UT, MG=b_MG),
                     sm, b_sm, A_pk, Sh_pk, b_mod, identb, b_identb, cosT, sinT, b_cs)
          except _Stop:
            _fence(nc, sc)
        if INTERLEAVE_BC and "B" in phases and "C" in phases:
            _phase_BC(nc, sc, l, lambda_init, ps, psb, psbf, QTd, KTd, Vd, Gd, OGTd, b_QK, b_V, b_G, b_OGT,
                      UTd, DTd, Zd, YNTd, b_UT, b_DT, b_Z, b_YNT,
                      sm, b_sm, identb, b_identb, maskb, b_maskb, cst, b_cst)
        else:
            if "B" in phases:
                _phase_B(nc, sc, l, lambda_init, ps, psb, psbf, QTd, KTd, Vd, Gd, OGTd,
                         b_QK, b_V, b_G, b_OGT, sm, b_sm, identb, b_identb, maskb, b_maskb)
            if "C" in phases:
                _phase_C(nc, sc, l, ps, psb, psbf, UTd, DTd, Zd, YNTd, b_UT, b_DT, b_Z, b_YNT,
                         sm, b_sm, identb, b_identb, cst, b_cst)
        if "D" in phases:
            _phase_D(nc, sc, l, last, ps, psb, psbf, x_src, (b_X1 if l > 0 else None), OGTd, YNTd, MGd, X1d, y_out,
                     b_OGT, b_YNT, b_MG, b_X1, b_Y, watt_in, wssm_in, wout_in, gate_b, b_gate, fnw_in)

    allb = list(_DUMPS) + b_QK + b_V + b_G + b_Z + b_DT + b_UT + b_MG + b_YNT + b_X1 + b_Y
    for h in range(4):
        allb += b_OGT[h]
    sc.wait_all("sp", allb)
    for e in ("pe", "act", "dve", "pool"):
        if sc.cnt[e] > 0:
            sc._wait("sp", ("c", e, sc.cnt[e]))


def _phase_A(nc, sc, l, x_src, b_xsrc, win_in, ps, psb, psbf, dr, bd, sm, b_sm, A_pk, Sh_pk, b_mod,
             identb, b_identb, cosT, sinT, b_cs):
    V, A, T, P = nc.vector, nc.scalar, nc.tensor, nc.gpsimd
    with contextlib.ExitStack() as es2:
        def sb(name, shape, dt):
            return es2.enter_context(nc.sbuf_tensor(f"{name}_A{l}", list(shape), dt))

        Wb = sb("Wb", [128, 8, IN_COLS], BF16)
        b_W = [Buf() for _ in range(8)]
        with contextlib.ExitStack() as es3:
            wst = [es3.enter_context(nc.sbuf_tensor(f"wst{i}_A{l}", [128, IN_COLS], F32)) for i in range(2)]
            b_wst = [Buf(), Buf()]
            H1 = 3328
            for kc in range(8):
                i = kc % 2
                sc.dma("sp", wst[i][:], win_in[l, kc * 128:(kc + 1) * 128, :], writes=[b_wst[i]])
                sc.op("act", lambda kc=kc, i=i: A.copy(out=Wb[:, kc, 0:H1], in_=wst[i][:, 0:H1]),
                      reads=[b_wst[i]], writes=[b_W[kc]])
                sc.op("dve", lambda kc=kc, i=i: V.tensor_copy(out=Wb[:, kc, H1:IN_COLS], in_=wst[i][:, H1:IN_COLS]),
                      reads=[b_wst[i]], writes=[b_W[kc]])
            _fence(nc, sc)
        b_Wall = Buf()
        sc.op("act", lambda: A.copy(out=Wb[:, 0, 0:1], in_=Wb[:, 0, 0:1]), reads=b_W, writes=[b_Wall])
        sc.op("dve", lambda: V.tensor_copy(out=Wb[:, 0, 1:2], in_=Wb[:, 0, 1:2]), reads=b_W + [b_Wall],
              writes=[b_Wall])

        dbg_stop(1)
        xt = [sb(f"xt{i}", [128, D], F32) for i in range(2)]
        junk = sb("junk", [128, D], BF16)
        xn = [sb(f"xn{i}", [128, D], BF16) for i in range(2)]
        hT = [sb(f"hT{i}", [128, 8, 512], BF16) for i in range(2)]
        stat = [sb(f"stat{i}", [128, 4], F32) for i in range(2)]
        qk_sb = [sb(f"qksb{i}", [128, 1024], BF16) for i in range(2)]
        rp = [sb(f"rp{i}", [128, 4, 16, 8], F32) for i in range(2)]
        QKst = [sb("QKst0", [128, 8, 512], BF16)] * 2
        v_st = [sb(f"vst{i}", [128, 512], BF16) for i in range(2)]
        g_st = [sb(f"gst{i}", [128, 512], F32) for i in range(2)]
        z_st = [sb(f"zst{i}", [128, D], F32) for i in range(2)]
        dt4 = [sb(f"dt4{i}", [128, 4, 16], F32) for i in range(2)]
        dt4s = [sb(f"dt4s{i}", [128, 4, 16], F32) for i in range(2)]
        b_dt4, b_dt4s = [Buf(), Buf()], [Buf(), Buf()]
        fm_st = [sb(f"fmst{i}", [128, 512], F32) for i in range(3)]
        b_xt, b_xn, b_hT, b_stat, b_qk, b_rp, b_QKst_unused, b_vst, b_gst, b_zst, b_dtst, b_dttmp, b_hTa = (
            [Buf(), Buf()] for _ in range(13))
        b_junk = Buf()
        _bq = Buf()
        b_QKst = [_bq, _bq]
        b_fm = [Buf() for _ in range(3)]
        tm_banks = [2, 3, 6, 7]
        fm_banks = [4, 5]
        tmc = [0]
        fmc = [0]

        def a_sl(name):
            a, b = SM[name]
            return sm[:, a:b]

        for g in range(NGL):
            hb = g % 2
            for tt in range(4):
                t = 4 * g + tt
                i2 = t % 2
                rd = [b_xsrc[t]] if b_xsrc is not None else []
                sc.dma("sp", xt[i2][:], x_src[t * 128:(t + 1) * 128, :], reads=rd, writes=[b_xt[i2]])
                sc.op("dve", lambda i2=i2: V.scalar_tensor_tensor(out=junk[:], in0=xt[i2][:], scalar=1.0,
                                                                  in1=xt[i2][:], op0=ALU.mult, op1=ALU.mult,
                                                                  accum_out=stat[i2][:, 0:1]),
                      reads=[b_xt[i2]], writes=[b_junk, b_stat[i2]])
                emit_rstd(nc, sc, stat[i2][:, 2:3], stat[i2][:, 1:2], stat[i2][:, 0:1], 1.0 / D, 1,
                          [b_stat[i2]], [b_stat[i2]])
                sc.op("dve", lambda i2=i2: V.tensor_scalar(out=xn[i2][:], in0=xt[i2][:], scalar1=stat[i2][:, 2:3],
                                                           scalar2=None, op0=ALU.mult),
                      reads=[b_xt[i2], b_stat[i2]], writes=[b_xn[i2]])
                dbg_stop(11)
                pT = psbf(0)
                for kc in range(8):
                    sc.op("pe", lambda kc=kc, i2=i2: T.transpose(out=pT[:, kc * 128:(kc + 1) * 128],
                                                                 in_=xn[i2][:, kc * 128:(kc + 1) * 128],
                                                                 identity=identb[:]),
                          reads=[b_xn[i2], b_identb], writes=[psb[0]])
                dbg_stop(12)
                for kc in range(8):
                    eng = _os.environ.get("KDBG_EVAC") or ("dve" if kc % 2 == 0 else "act")
                    if eng == "dve":
                        sc.op("dve", lambda kc=kc, tt=tt, hb=hb: V.tensor_scalar(
                            out=hT[hb][:, kc, tt * 128:(tt + 1) * 128], in0=pT[:, kc * 128:(kc + 1) * 128],
                            scalar1=A_pk[:, kc:kc + 1], scalar2=Sh_pk[:, kc:kc + 1], op0=ALU.mult, op1=ALU.add),
                              reads=[psb[0], b_mod], writes=[b_hT[hb]])
                    else:
                        sc.op("act", lambda kc=kc, tt=tt, hb=hb: A.activation(
                            out=hT[hb][:, kc, tt * 128:(tt + 1) * 128], in_=pT[:, kc * 128:(kc + 1) * 128],
                            func=AF.Identity, scale=A_pk[:, kc:kc + 1], bias=Sh_pk[:, kc:kc + 1]),
                              reads=[psb[0], b_mod], writes=[b_hTa[hb]])

                dbg_stop(2)

                def tm_mm(c0, ncols):
                    bk = tm_banks[tmc[0] % 4]
                    tmc[0] += 1
                    for kc in range(8):
                        sc.op("pe", lambda kc=kc, bk=bk: T.matmul(ps[bk][:, 0:ncols],
                                                                  lhsT=hT[hb][:, kc, tt * 128:(tt + 1) * 128],
                                                                  rhs=Wb[:, kc, c0:c0 + ncols],
                                                                  start=(kc == 0), stop=(kc == 7)),
                              reads=[b_hT[hb], b_hTa[hb], b_Wall], writes=[psb[bk]])
                    return bk

                for half in range(2):
                    bk = tm_mm(half * 512, 512)
                    pv = ps[bk][:, :].rearrange("p (s d) -> p s d", d=64)
                    qv = qk_sb[i2][:, half * 512:(half + 1) * 512].rearrange("p (s d) -> p s d", d=64)
                    cb_ = cosT[:, t, :].unsqueeze(1).to_broadcast([128, 8, 8])
                    sb_ = sinT[:, t, :].unsqueeze(1).to_broadcast([128, 8, 8])
                    r = rp[i2]
                    o8 = half * 8
                    t1 = pv[:, :, 0:8]
                    t2 = pv[:, :, 8:16]
                    sc.op("dve", lambda: V.tensor_tensor(out=r[:, 0, o8:o8 + 8, :], in0=t1, in1=cb_, op=ALU.mult),
                          reads=[psb[bk], b_cs], writes=[b_rp[i2]])
                    sc.op("dve", lambda: V.tensor_tensor(out=r[:, 1, o8:o8 + 8, :], in0=t2, in1=sb_, op=ALU.mult),
                          reads=[psb[bk], b_cs], writes=[b_rp[i2]])
                    sc.op("dve", lambda: V.tensor_tensor(out=r[:, 2, o8:o8 + 8, :], in0=t2, in1=cb_, op=ALU.mult),
                          reads=[psb[bk], b_cs], writes=[b_rp[i2]])
                    sc.op("dve", lambda: V.tensor_tensor(out=r[:, 3, o8:o8 + 8, :], in0=t1, in1=sb_, op=ALU.mult),
                          reads=[psb[bk], b_cs], writes=[b_rp[i2]])
                    sc.op("dve", lambda: V.tensor_tensor(out=qv[:, :, 0:8], in0=r[:, 0, o8:o8 + 8, :],
                                                         in1=r[:, 1, o8:o8 + 8, :], op=ALU.subtract),
                          reads=[b_rp[i2]], writes=[b_qk[i2]])
                    sc.op("dve", lambda: V.tensor_tensor(out=qv[:, :, 8:16], in0=r[:, 2, o8:o8 + 8, :],
                                                         in1=r[:, 3, o8:o8 + 8, :], op=ALU.add),
                          reads=[b_rp[i2]], writes=[b_qk[i2]])
                    sc.op("act", lambda: A.copy(out=qv[:, :, 16:64], in_=pv[:, :, 16:64]),
                          reads=[psb[bk]], writes=[b_qk[i2]])
                    sc.op("dve", lambda: V.tensor_copy(out=r[:, 0, o8, 0:1], in_=r[:, 0, o8, 1:2]),
                          reads=[b_qk[i2], b_rp[i2]], writes=[b_qk[i2], b_rp[i2]])
                dbg_stop(4)
                bk = tm_mm(O_V, 512)
                sc.op("act", lambda bk=bk, i2=i2: A.copy(out=v_st[i2][:], in_=ps[bk][:, :]),
                      reads=[psb[bk]], writes=[b_vst[i2]])
                sc.dma("pool", dr["Vd"][t * 128:(t + 1) * 128, :], v_st[i2][:], reads=[b_vst[i2]], writes=[bd["V"][t]])
                dbg_stop(5)
                bk = tm_mm(O_G, 512)
                sc.op("act", lambda bk=bk, i2=i2: A.activation(out=g_st[i2][:], in_=ps[bk][:, :], func=AF.Silu),
                      reads=[psb[bk]], writes=[b_gst[i2]])
                sc.dma("pool", dr["Gd"][t * 128:(t + 1) * 128, :], g_st[i2][:], reads=[b_gst[i2]], writes=[bd["G"][t]])
                for half in range(2):
                    bk = tm_mm(O_Z + half * 512, 512)
                    sc.op("act", lambda bk=bk, i2=i2, half=half: A.activation(
                        out=z_st[i2][:, half * 512:(half + 1) * 512], in_=ps[bk][:, :], func=AF.Silu),
                          reads=[psb[bk]], writes=[b_zst[i2]])
                sc.dma("pool", dr["Zd"][t * 128:(t + 1) * 128, :], z_st[i2][:], reads=[b_zst[i2]], writes=[bd["Z"][t]])
                dbg_stop(6)
                bk = tm_mm(O_DT, 16)
                sc.op("dve", lambda bk=bk: V.tensor_tensor(out=dt4[hb][:, tt, :], in0=ps[bk][:, 0:16],
                                                           in1=a_sl("dtb"), op=ALU.add),
                      reads=[psb[bk], b_sm], writes=[b_dt4[hb]])
                dbg_stop(3)
                pT2 = psbf(1)
                for c in range(8):
                    sc.op("pe", lambda c=c, i2=i2: T.transpose(out=pT2[:, c * 128:(c + 1) * 128],
                                                               in_=qk_sb[i2][:, c * 128:(c + 1) * 128],
                                                               identity=identb[:]),
                          reads=[b_qk[i2], b_identb], writes=[psb[1]])
                sc.op("act", lambda hb=hb, tt=tt: A.copy(
                    out=QKst[hb][:, :, tt * 128:(tt + 1) * 128],
                    in_=pT2[:, :].rearrange("p (c s) -> p c s", c=8)),
                      reads=[psb[1]], writes=[b_QKst[hb]])
                dbg_stop(7)
            sc.op("act", lambda: A.activation(out=dt4[hb][:], in_=dt4[hb][:], func=AF.Exp),
                  reads=[b_dt4[hb]], writes=[b_dt4[hb]])
            sc.op("act", lambda: A.activation(out=dt4s[hb][:], in_=dt4[hb][:], func=AF.Ln, bias=1.0),
                  reads=[b_dt4[hb]], writes=[b_dt4s[hb]])
            sc.dma("pool", dr["DTd"][g * 512:(g + 1) * 512, :].rearrange("(i p) e -> p i e", p=128), dt4s[hb][:],
                   reads=[b_dt4s[hb]], writes=[bd["DT"][4 * g + i] for i in range(4)])
            sc.dma("pool", dr["QTd"][:, :, g * 512:(g + 1) * 512].rearrange("h p s -> p h s"), QKst[hb][:, 0:4, :],
                   reads=[b_QKst[hb]], writes=[bd["QK"][g]])
            sc.dma("pool", dr["KTd"][:, :, g * 512:(g + 1) * 512].rearrange("h p s -> p h s"), QKst[hb][:, 4:8, :],
                   reads=[b_QKst[hb]], writes=[bd["QK"][g]])
            dbg_stop(8)
            for cc in range(28):
                if cc < 12:
                    c0 = O_X + cc * 128
                else:
                    c0 = O_MA + (cc - 12) * 128
                bk = fm_banks[fmc[0] % 2]
                fi = fmc[0] % 3
                fmc[0] += 1
                for kc in range(8):
                    sc.op("pe", lambda kc=kc, bk=bk, c0=c0: T.matmul(ps[bk][:, :], lhsT=Wb[:, kc, c0:c0 + 128],
                                                                     rhs=hT[hb][:, kc, :],
                                                                     start=(kc == 0), stop=(kc == 7)),
                          reads=[b_hT[hb], b_hTa[hb], b_Wall], writes=[psb[bk]])
                if cc < 12:
                    sc.op("dve", lambda bk=bk, fi=fi: V.tensor_copy(out=fm_st[fi][:], in_=ps[bk][:, :]),
                          reads=[psb[bk]], writes=[b_fm[fi]])
                    sc.dma("pool", dr["UTd"][cc * 128:(cc + 1) * 128, g * 512:(g + 1) * 512], fm_st[fi][:],
                           reads=[b_fm[fi]], writes=[bd["UT"][g]] if cc == 11 else [])
                else:
                    sc.op("act", lambda bk=bk, fi=fi: A.activation(out=fm_st[fi][:], in_=ps[bk][:, :],
                                                                   func=AF.Sigmoid),
                          reads=[psb[bk]], writes=[b_fm[fi]])
                    sc.dma("pool", dr["MGd"][(cc - 12) * 128:(cc - 11) * 128, g * 512:(g + 1) * 512], fm_st[fi][:],
                           reads=[b_fm[fi]], writes=[bd["MG"][g]] if cc == 27 else [])
        _fence(nc, sc)


def _fence(nc, sc):
    _DEAD[0] = False
    for E in ("pe", "act", "dve", "pool", "sp"):
        for e in ("pe", "act", "dve", "pool"):
            if sc.cnt[e] > 0 and not (E == "pe" and e == "pe"):
                sc._wait(E, ("c", e, sc.cnt[e]))
        for q in ("sp", "pool"):
            n = sc.dcnt[q]
            for k in range(max(0, n - sc.NQ), n):
                sc._wait(E, ("d", q, k))


def _phase_B(nc, sc, l, lambda_init, ps, psb, psbf, QTd, KTd, Vd, Gd, OGTd, b_QK, b_V, b_G, b_OGT,
             sm, b_sm, identb, b_identb, maskb, b_maskb):
    V, A, T, P = nc.vector, nc.scalar, nc.tensor, nc.gpsimd
    with contextlib.ExitStack() as es2:
        def sb(name, shape, dt):
            return es2.enter_context(nc.sbuf_tensor(f"{name}_B{l}", list(shape), dt))

        KT = [sb(f"KT{i}", [128, S], BF16) for i in range(2)]
        Va = [sb(f"Va{i}", [128, NT, 130], BF16) for i in range(2)]
        QZ = [[sb(f"QZ{i}_{t}", [128, 512], BF16) for t in range(2)] for i in range(2)]
        Et = [sb(f"E{i}", [128, 512], BF16) for i in range(4)]
        Gt = [sb(f"Gt{i}", [128, 4, 128], F32) for i in range(2)]
        gw = [sb(f"gw{i}", [128, 4, 128], F32) for i in range(2)]
        o1 = sb("o1", [128, 4, 128], F32)
        od = [sb(f"od{i}", [128, 128], F32) for i in range(2)]
        junk = sb("junk", [128, 128], F32)
        st = [sb(f"st{i}", [128, 8], F32) for i in range(2)]
        ogb = [sb(f"ogb{i}", [128, 128], BF16) for i in range(2)]
        ogT = [sb(f"ogT{i}", [128, 512], BF16) for i in range(2)]
        wsub = sb("wsub", [128, 128], F32)
        lam = sb("lam", [128, 8], F32)
        ljunk = sb("ljunk", [128, 64], F32)
        b_KT, b_Va, b_QT_unused, b_Gt, b_gw, b_od, b_st, b_ogb, b_ogT = ([Buf(), Buf()] for _ in range(9))
        b_E = [Buf() for _ in range(4)]
        b_o1, b_junk, b_wsub, b_lam = Buf(), Buf(), Buf(), Buf()
        b_QZ = [[Buf(), Buf()], [Buf(), Buf()]]
        for i in range(2):
            sc.op("dve", lambda i=i: V.memset(QZ[i][0][64:128, :], 0.0), writes=[b_QZ[i][0]])
            sc.op("dve", lambda i=i: V.memset(QZ[i][1][0:64, :], 0.0), writes=[b_QZ[i][1]])

        def smv(name):
            a, b = SM[name]
            return sm[:, a:b]

        sc.op("dve", lambda: V.scalar_tensor_tensor(out=ljunk[:], in0=smv("lq1"), scalar=1.0, in1=smv("lk1"),
                                                    op0=ALU.mult, op1=ALU.mult, accum_out=lam[:, 0:1]),
              reads=[b_sm], writes=[b_lam])
        sc.op("dve", lambda: V.scalar_tensor_tensor(out=ljunk[:], in0=smv("lq2"), scalar=1.0, in1=smv("lk2"),
                                                    op0=ALU.mult, op1=ALU.mult, accum_out=lam[:, 1:2]),
              reads=[b_sm, b_lam], writes=[b_lam])
        sc.op("act", lambda: A.activation(out=lam[:, 2:4], in_=lam[:, 0:2], func=AF.Exp), reads=[b_lam],
              writes=[b_lam])
        sc.op("dve", lambda: V.tensor_tensor(out=lam[:, 4:5], in0=lam[:, 2:3], in1=lam[:, 3:4], op=ALU.subtract),
              reads=[b_lam], writes=[b_lam])
        sc.op("dve", lambda: V.tensor_scalar(out=lam[:, 5:6], in0=lam[:, 4:5], scalar1=lambda_init, scalar2=-1.0,
                                             op0=ALU.add, op1=ALU.mult), reads=[b_lam], writes=[b_lam])
        sc.op("dve", lambda: V.tensor_scalar(out=wsub[:], in0=smv("subw"), scalar1=(1.0 - lambda_init), scalar2=None,
                                             op0=ALU.mult), reads=[b_sm], writes=[b_wsub])
        for i in range(2):
            sc.op("dve", lambda i=i: V.memset(Va[i][:, :, 128:130], 1.0), writes=[b_Va[i]])

        sbanks = [0, 1, 7]
        obanks = [2, 3, 4, 5]
        scnt = [0]
        for h in range(4):
            hb = h % 2
            sc.dma("sp", KT[hb][:], KTd[h, :, :], reads=b_QK, writes=[b_KT[hb]])
            for q4 in range(4):
                sc.dma("sp", Va[hb][:, q4 * 16:(q4 + 1) * 16, 0:128],
                       Vd[q4 * 2048:(q4 + 1) * 2048, h * 128:(h + 1) * 128].rearrange("(t p) e -> p t e", p=128),
                       reads=b_V[q4 * 16:(q4 + 1) * 16], writes=[b_Va[hb]])
            for qg in range(NG):
                qb = qg % 2
                sc.dma("sp", QZ[qb][0][0:64, :], QTd[h, 0:64, qg * 512:(qg + 1) * 512], reads=[b_QK[qg]],
                       writes=[b_QZ[qb][0]])
                sc.dma("sp", QZ[qb][1][64:128, :], QTd[h, 64:128, qg * 512:(qg + 1) * 512], reads=[b_QK[qg]],
                       writes=[b_QZ[qb][1]])
                sc.dma("sp", Gt[qb][:], Gd[qg * 512:(qg + 1) * 512, h * 128:(h + 1) * 128].rearrange(
                    "(i p) e -> p i e", p=128), reads=b_G[qg * 4:qg * 4 + 4], writes=[b_Gt[qb]])
                sc.op("pool", lambda qb=qb: P.tensor_tensor(out=gw[qb][:], in0=Gt[qb][:],
                                                            in1=wsub[:].unsqueeze(1).to_broadcast([128, 4, 128]),
                                                            op=ALU.mult),
                      reads=[b_Gt[qb], b_wsub], writes=[b_gw[qb]])
                for t in range(2):
                    nj = 4 * qg + 4
                    slots = {}

                    def emit_score(j, t=t, qg=qg, qb=qb, hb=hb):
                        c0 = max(0, j - 4 * qg) * 128
                        sbk = sbanks[scnt[0] % 3]
                        ei = scnt[0] % 4
                        scnt[0] += 1
                        slots[j] = (c0, ei)
                        sc.op("pe", lambda: T.matmul(
                            ps[sbk][:, c0:512], lhsT=KT[hb][:, j * 128:(j + 1) * 128],
                            rhs=QZ[qb][t][:, c0:512], start=True, stop=True),
                              reads=[b_KT[hb], b_QZ[qb][t]], writes=[psb[sbk]])
                        sc.op("act", lambda: A.activation(
                            out=Et[ei][:, c0:512], in_=ps[sbk][:, c0:512], func=AF.Exp, scale=0.125),
                              reads=[psb[sbk]], writes=[b_E[ei]])
                        if j >= 4 * qg:
                            sc.op("dve", lambda: V.tensor_tensor(
                                out=Et[ei][:, c0:c0 + 128], in0=Et[ei][:, c0:c0 + 128], in1=maskb[:], op=ALU.mult),
                                  reads=[b_E[ei], b_maskb], writes=[b_E[ei]])

                    def emit_pv(j, qg=qg, hb=hb):
                        c0, ei = slots[j]
                        for i in range(max(j, 4 * qg), 4 * qg + 4):
                            ii = i - 4 * qg
                            ob = obanks[ii]
                            sc.op("pe", lambda: T.matmul(
                                ps[ob][:, 0:129], lhsT=Et[ei][:, ii * 128:(ii + 1) * 128],
                                rhs=Va[hb][:, j, 0:129], start=(j == 0), stop=(j == i)),
                                  reads=[b_E[ei], b_Va[hb]], writes=[psb[ob]])

                    LOOK = 2
                    for j in range(min(LOOK, nj)):
                        emit_score(j)
                    for j in range(nj):
                        if j + LOOK < nj:
                            emit_score(j + LOOK)
                        emit_pv(j)
                    for ii in range(4):
                        ob = obanks[ii]
                        s2 = st[ii % 2]
                        bs2 = b_st[ii % 2]
                        if t == 0:
                            sc.op("dve", lambda ob=ob, s2=s2: V.reciprocal(out=s2[:, 0:1], in_=ps[ob][:, 128:129]),
                                  reads=[psb[ob]], writes=[bs2])
                            sc.op("dve", lambda ob=ob, s2=s2, ii=ii: V.tensor_scalar(
                                out=o1[:, ii, :], in0=ps[ob][:, 0:128], scalar1=s2[:, 0:1], scalar2=None,
                                op0=ALU.mult), reads=[psb[ob], bs2], writes=[b_o1])
                        else:
                            oi = ii % 2
                            sc.op("dve", lambda ob=ob, s2=s2: V.reciprocal(out=s2[:, 0:1], in_=ps[ob][:, 128:129]),
                                  reads=[psb[ob]], writes=[bs2])
                            sc.op("dve", lambda s2=s2: V.tensor_tensor(out=s2[:, 1:2], in0=s2[:, 0:1],
                                                                       in1=lam[:, 5:6], op=ALU.mult),
                                  reads=[bs2, b_lam], writes=[bs2])
                            sc.op("dve", lambda ob=ob, s2=s2, ii=ii, oi=oi: V.scalar_tensor_tensor(
                                out=od[oi][:], in0=ps[ob][:, 0:128], scalar=s2[:, 1:2], in1=o1[:, ii, :],
                                op0=ALU.mult, op1=ALU.add), reads=[psb[ob], bs2, b_o1], writes=[b_od[oi]])
                            sc.op("dve", lambda s2=s2, oi=oi: V.scalar_tensor_tensor(
                                out=junk[:], in0=od[oi][:], scalar=1.0, in1=od[oi][:], op0=ALU.mult, op1=ALU.mult,
                                accum_out=s2[:, 2:3]), reads=[b_od[oi]], writes=[b_junk, bs2])
                            emit_rstd(nc, sc, s2[:, 4:5], s2[:, 3:4], s2[:, 2:3], 1.0 / 128, 1, [bs2], [bs2])
                            sc.op("dve", lambda s2=s2, oi=oi, ii=ii, qb=qb: V.scalar_tensor_tensor(
                                out=ogb[oi][:], in0=od[oi][:], scalar=s2[:, 4:5], in1=gw[qb][:, ii, :],
                                op0=ALU.mult, op1=ALU.mult), reads=[b_od[oi], bs2, b_gw[qb]], writes=[b_ogb[oi]])
                            pT = psbf(6)
                            sc.op("pe", lambda oi=oi, ii=ii: T.transpose(out=pT[:, ii * 128:(ii + 1) * 128],
                                                                         in_=ogb[oi][:], identity=identb[:]),
                                  reads=[b_ogb[oi], b_identb], writes=[psb[6]])
                    if t == 1:
                        pT = psbf(6)
                        sc.op("act", lambda qb=qb: A.copy(out=ogT[qb][:], in_=pT[:, 0:512]),
                              reads=[psb[6]], writes=[b_ogT[qb]])
                        sc.dma("pool", OGTd[h * 128:(h + 1) * 128, qg * 512:(qg + 1) * 512], ogT[qb][:],
                               reads=[b_ogT[qb]], writes=[b_OGT[h][qg]])
        _fence(nc, sc)


def _phase_C(nc, sc, l, ps, psb, psbf, UTd, DTd, Zd, YNTd, b_UT, b_DT, b_Z, b_YNT, sm, b_sm,
             identb, b_identb, cst, b_cst):
    V, A, T, P = nc.vector, nc.scalar, nc.tensor, nc.gpsimd
    ident_f = cst[:, 0, :]
    tri_f = cst[:, 1, :]
    negm_f = cst[:, 2, :]
    ones_f = cst[:, 3, :]
    with contextlib.ExitStack() as es2:
        def sb(name, shape, dt):
            return es2.enter_context(nc.sbuf_tensor(f"{name}_C{l}", list(shape), dt))

        U = [sb(f"U{i}", [128, 12, 515], F32) for i in range(2)]
        acc = [sb(f"acc{i}", [128, 512], F32) for i in range(2)]
        XC = [sb(f"XC{i}", [128, 12, 512], BF16) for i in range(2)]
        DTg = [sb(f"DTg{i}", [128, 4, 16], F32) for i in range(2)]
        Zt = [sb(f"Zt{i}", [128, D], F32) for i in range(2)]
        xd = [sb(f"xd{i}", [128, D], BF16) for i in range(2)]
        xdd = [sb(f"xdd{i}", [128, D], BF16) for i in range(2)]
        xsk = [sb(f"xsk{i}", [128, D], F32) for i in range(2)]
        Btm = [sb(f"Btm{i}", [128, 256], BF16) for i in range(2)]
        sml = [sb(f"sml{i}", [128, 8, 4, 16], F32) for i in range(2)]
        LT = [sb(f"LT{i}", [128, 128], F32) for i in range(8)]
        MT = [sb(f"MT{i}", [128, 128], BF16) for i in range(8)]
        t1 = [sb(f"t1{i}", [128, 512], F32) for i in range(2)]
        yy = [sb(f"yy{i}", [128, D], F32) for i in range(2)]
        Hs = sb("Hs", [128, 2, 512], F32)
        Hb = sb("Hb", [128, 2, 512], BF16)
        a_row = sb("a_row", [128, 16], F32)
        nst = [sb(f"nst{i}", [128, 8], F32) for i in range(2)]
        junk = sb("junk", [128, 512], F32)
        ynb = [sb(f"ynb{i}", [128, D], BF16) for i in range(2)]
        YNst = [sb(f"YNst{i}", [128, 8, 512], BF16) for i in range(2)]
        (b_U, b_acc, b_XC, b_DTg, b_Zt, b_xd, b_xdd, b_xsk, b_Btm, b_sml, b_t1, b_yy, b_nst, b_ynb,
         b_YNst) = ([Buf(), Buf()] for _ in range(15))
        b_LT = [Buf() for _ in range(8)]
        b_MT = [Buf() for _ in range(8)]
        b_Hs, b_Hb, b_arow, b_junk = Buf(), Buf(), Buf(), Buf()

        def smv(name):
            a, b = SM[name]
            return sm[:, a:b]

        sc.op("act", lambda: A.activation(out=a_row[:], in_=smv("alog"), func=AF.Exp), reads=[b_sm], writes=[b_arow])
        sc.op("dve", lambda: V.tensor_scalar(out=a_row[:], in0=a_row[:], scalar1=-1.0, scalar2=None, op0=ALU.mult),
              reads=[b_arow], writes=[b_arow])
        sc.op("dve", lambda: V.memset(Hs[:], 0.0), writes=[b_Hs])
        sc.op("dve", lambda: V.memset(Hb[:], 0.0), writes=[b_Hb])
        sc.op("dve", lambda: V.memset(U[0][:, :, 0:3], 0.0), writes=[b_U[0]])

        cw0 = SM["cw"][0]
        cb0 = SM["cb"][0]

        def load_group(g):
            ub = g % 2
            if g == 0:
                sc.dma("sp", U[ub][:, :, 3:515], UTd[:, 0:512].rearrange("(c p) s -> p c s", p=128),
                       reads=[b_UT[0]], writes=[b_U[ub]])
            else:
                sc.dma("sp", U[ub][:, :, :], UTd[:, g * 512 - 3:g * 512 + 512].rearrange("(c p) s -> p c s", p=128),
                       reads=[b_UT[g - 1], b_UT[g]], writes=[b_U[ub]])
            sc.dma("sp", DTg[ub][:], DTd[g * 512:(g + 1) * 512, :].rearrange("(i p) e -> p i e", p=128),
                   reads=b_DT[4 * g:4 * g + 4], writes=[b_DTg[ub]])

        def conv_piece(g, ccs):
            ub = g % 2
            for cc in ccs:
                ab = cc % 2
                w = lambda k, cc=cc: sm[:, cw0 + cc * 4 + k:cw0 + cc * 4 + k + 1]
                sc.op("dve", lambda: V.tensor_scalar(
                    out=acc[ab][:], in0=U[ub][:, cc, 3:515], scalar1=w(3), scalar2=sm[:, cb0 + cc:cb0 + cc + 1],
                    op0=ALU.mult, op1=ALU.add), reads=[b_U[ub], b_sm], writes=[b_acc[ab]])
                for k in (2, 1, 0):
                    sc.op("dve", lambda: V.scalar_tensor_tensor(
                        out=acc[ab][:], in0=U[ub][:, cc, k:k + 512], scalar=w(k), in1=acc[ab][:],
                        op0=ALU.mult, op1=ALU.add), reads=[b_U[ub], b_sm, b_acc[ab]], writes=[b_acc[ab]])
                sc.op("act", lambda: A.activation(out=XC[ub][:, cc, :], in_=acc[ab][:], func=AF.Silu),
                      reads=[b_acc[ab]], writes=[b_XC[ub]])

        load_group(0)
        conv_piece(0, range(12))
        for g in range(NG):
            ub = g % 2
            if g + 1 < NG:
                load_group(g + 1)
            s_ = sml[ub]
            bs = b_sml[ub]
            row = lambda r: s_[:, r, :, :]
            row2 = lambda r: s_[:, r, :, :].rearrange("p t h -> p (t h)")
            sc.op("dve", lambda: V.tensor_tensor(out=row(0), in0=DTg[ub][:],
                                                 in1=a_row[:].unsqueeze(1).to_broadcast([128, 4, 16]), op=ALU.mult),
                  reads=[b_DTg[ub], b_arow], writes=[bs])
            sc.op("pe", lambda: T.matmul(ps[1][:, 256:320], lhsT=tri_f, rhs=row2(0), start=True, stop=True),
                  reads=[bs, b_cst], writes=[psb[1]])
            sc.op("pe", lambda: T.matmul(ps[1][:, 320:384], lhsT=ones_f, rhs=row2(0), start=True, stop=True),
                  reads=[bs, b_cst], writes=[psb[1]])
            sc.op("dve", lambda: V.tensor_copy(out=row2(1), in_=ps[1][:, 256:320]), reads=[psb[1]], writes=[bs])
            sc.op("dve", lambda: V.tensor_scalar(out=row2(2), in0=ps[1][:, 256:320], scalar1=-1.0,
                                                 scalar2=None, op0=ALU.mult), reads=[psb[1]], writes=[bs])
            sc.op("dve", lambda: V.tensor_tensor(out=row2(3), in0=ps[1][:, 320:384], in1=row2(1),
                                                 op=ALU.subtract), reads=[psb[1], bs], writes=[bs])
            sc.op("act", lambda: A.activation(out=row2(4), in_=row2(3), func=AF.Exp), reads=[bs], writes=[bs])
            sc.op("act", lambda: A.activation(out=row2(5), in_=row2(1), func=AF.Exp), reads=[bs], writes=[bs])
            sc.op("act", lambda: A.activation(out=row2(6), in_=ps[1][:, 320:384], func=AF.Exp),
                  reads=[psb[1], bs], writes=[bs])
            sc.op("dve", lambda: V.tensor_copy(out=s_[:, 7, 0, 0:1], in_=s_[:, 6, 0, 0:1]), reads=[bs], writes=[bs])

            for tt in range(4):
                c = 4 * g + tt
                cb = c % 2
                tsl = slice(tt * 128, (tt + 1) * 128)
                sc.dma("sp", Zt[cb][:], Zd[c * 128:(c + 1) * 128, :], reads=[b_Z[c]], writes=[b_Zt[cb]])
                pX = psbf(0)
                for cc in range(8):
                    sc.op("pe", lambda cc=cc: T.transpose(out=pX[:, cc * 128:(cc + 1) * 128], in_=XC[ub][:, cc, tsl],
                                                          identity=identb[:]),
                          reads=[b_XC[ub], b_identb], writes=[psb[0]])
                pB = psbf(1)
                for k in range(2):
                    sc.op("pe", lambda k=k: T.transpose(out=pB[:, k * 128:(k + 1) * 128], in_=XC[ub][:, 8 + k, tsl],
                                                        identity=identb[:]),
                          reads=[b_XC[ub], b_identb], writes=[psb[1]])
                pX3 = pX[:, :].rearrange("p (h e) -> p h e", e=64)
                bc = lambda r: s_[:, r, tt, :].unsqueeze(2).to_broadcast([128, 16, 64])
                sc.op("dve", lambda: V.tensor_tensor(out=xd[cb][:].rearrange("p (h e) -> p h e", e=64), in0=pX3,
                                                     in1=DTg[ub][:, tt, :].unsqueeze(2).to_broadcast([128, 16, 64]),
                                                     op=ALU.mult), reads=[psb[0], b_DTg[ub]], writes=[b_xd[cb]])
                sc.op("dve", lambda: V.tensor_tensor(out=xdd[cb][:].rearrange("p (h e) -> p h e", e=64),
                                                     in0=xd[cb][:].rearrange("p (h e) -> p h e", e=64), in1=bc(4),
                                                     op=ALU.mult), reads=[b_xd[cb], bs], writes=[b_xdd[cb]])
                sc.op("dve", lambda: V.tensor_tensor(out=xsk[cb][:].rearrange("p (h e) -> p h e", e=64), in0=pX3,
                                                     in1=smv("dsk").unsqueeze(2).to_broadcast([128, 16, 64]),
                                                     op=ALU.mult), reads=[psb[0], b_sm], writes=[b_xsk[cb]])
                sc.op("act", lambda: A.copy(out=Btm[cb][:], in_=pB[:, 0:256]), reads=[psb[1]], writes=[b_Btm[cb]])
                for gg in range(2):
                    gs = slice(gg * 512, (gg + 1) * 512)
                    sc.op("pe", lambda: T.matmul(ps[2][:, 0:128], lhsT=XC[ub][:, 8 + gg, tsl],
                                                 rhs=XC[ub][:, 10 + gg, tsl], start=True, stop=True),
                          reads=[b_XC[ub]], writes=[psb[2]])
                    sc.op("pe", lambda: T.matmul(ps[3][:, :], lhsT=XC[ub][:, 10 + gg, tsl], rhs=Hb[:, gg, :],
                                                 start=True, stop=True),
                          reads=[b_XC[ub], b_Hb], writes=[psb[3]])
                    sc.op("pe", lambda: T.matmul(ps[4][:, :], lhsT=Btm[cb][:, gg * 128:(gg + 1) * 128],
                                                 rhs=xdd[cb][:, gs], start=True, stop=True),
                          reads=[b_Btm[cb], b_xdd[cb]], writes=[psb[4]])
                    for j in range(8):
                        hh = gg * 8 + j
                        pb = 6 + j // 4
                        cs = slice((j % 4) * 128, (j % 4 + 1) * 128)
                        sc.op("pe", lambda: T.matmul(
                            ps[pb][:, cs], lhsT=s_[:, 0, tt, hh:hh + 1].to_broadcast([128, 128]), rhs=tri_f,
                            start=True, stop=False), reads=[bs, b_cst], writes=[psb[pb]])
                        sc.op("pe", lambda: T.matmul(ps[pb][:, cs], lhsT=ident_f, rhs=negm_f,
                                                     start=False, stop=True),
                              reads=[b_cst], writes=[psb[pb]])
                    for j in range(8):
                        hh = gg * 8 + j
                        pb = 6 + j // 4
                        cs = slice((j % 4) * 128, (j % 4 + 1) * 128)
                        sc.op("act", lambda: A.activation(
                            out=LT[j][:], in_=ps[pb][:, cs], func=AF.Exp, bias=s_[:, 2, tt, hh:hh + 1], scale=1.0),
                              reads=[psb[pb], bs], writes=[b_LT[j]])
                        sc.op("dve", lambda: V.tensor_tensor(out=MT[j][:], in0=LT[j][:], in1=ps[2][:, 0:128],
                                                             op=ALU.mult),
                              reads=[b_LT[j], psb[2]], writes=[b_MT[j]])
                    for j in range(8):
                        hh = gg * 8 + j
                        sc.op("pe", lambda: T.matmul(
                            ps[5][:, j * 64:(j + 1) * 64], lhsT=MT[j][:], rhs=xd[cb][:, hh * 64:(hh + 1) * 64],
                            start=True, stop=True), reads=[b_MT[j], b_xd[cb]], writes=[psb[5]])
                    tb = gg
                    sc.op("dve", lambda: V.tensor_tensor(
                        out=t1[tb][:].rearrange("p (h e) -> p h e", e=64),
                        in0=ps[3][:, :].rearrange("p (h e) -> p h e", e=64),
                        in1=s_[:, 5, tt, gg * 8:(gg + 1) * 8].unsqueeze(2).to_broadcast([128, 8, 64]), op=ALU.mult),
                          reads=[psb[3], bs], writes=[b_t1[tb]])
                    sc.op("dve", lambda: V.tensor_tensor(out=t1[tb][:], in0=ps[5][:, :], in1=t1[tb][:],
                                                         op=ALU.add),
                          reads=[psb[5], b_t1[tb]], writes=[b_t1[tb]])
                    sc.op("pool", lambda: P.tensor_tensor(out=yy[cb][:, gs], in0=t1[tb][:],
                                                          in1=xsk[cb][:, gs], op=ALU.add),
                          reads=[b_t1[tb], b_xsk[cb]], writes=[b_yy[cb]])
                    sc.op("dve", lambda: V.tensor_tensor(
                        out=Hs[:, gg, :].rearrange("p (h e) -> p h e", e=64),
                        in0=Hs[:, gg, :].rearrange("p (h e) -> p h e", e=64),
                        in1=s_[:, 6, tt, gg * 8:(gg + 1) * 8].unsqueeze(2).to_broadcast([128, 8, 64]), op=ALU.mult),
                          reads=[b_Hs, bs], writes=[b_Hs])
                    sc.op("dve", lambda: V.tensor_tensor(out=Hs[:, gg, :], in0=ps[4][:, :], in1=Hs[:, gg, :],
                                                         op=ALU.add),
                          reads=[psb[4], b_Hs], writes=[b_Hs])
                    sc.op("act", lambda: A.copy(out=Hb[:, gg, :], in_=Hs[:, gg, :]), reads=[b_Hs],
                          writes=[b_Hb])
                n_ = nst[cb]
                bn = b_nst[cb]
                sc.op("dve", lambda: V.tensor_tensor(out=yy[cb][:], in0=yy[cb][:], in1=Zt[cb][:], op=ALU.mult),
                      reads=[b_yy[cb], b_Zt[cb]], writes=[b_yy[cb]])
                for gg in range(2):
                    gs = slice(gg * 512, (gg + 1) * 512)
                    sc.op("dve", lambda: V.scalar_tensor_tensor(
                        out=junk[:], in0=yy[cb][:, gs], scalar=1.0, in1=yy[cb][:, gs], op0=ALU.mult, op1=ALU.mult,
                        accum_out=n_[:, gg:gg + 1]), reads=[b_yy[cb]], writes=[b_junk, bn])
                emit_rstd(nc, sc, n_[:, 4:6], n_[:, 2:4], n_[:, 0:2], 1.0 / 512, 2, [bn], [bn])
                w0 = SM["ssmw"][0]
                for gg in range(2):
                    gs = slice(gg * 512, (gg + 1) * 512)
                    sc.op("dve", lambda: V.scalar_tensor_tensor(
                        out=ynb[cb][:, gs], in0=yy[cb][:, gs], scalar=n_[:, 4 + gg:5 + gg],
                        in1=sm[:, w0 + gg * 512:w0 + (gg + 1) * 512], op0=ALU.mult, op1=ALU.mult),
                          reads=[b_yy[cb], bn, b_sm], writes=[b_ynb[cb]])
                pT = psbf(0)
                for cc in range(8):
                    sc.op("pe", lambda cc=cc: T.transpose(out=pT[:, cc * 128:(cc + 1) * 128],
                                                          in_=ynb[cb][:, cc * 128:(cc + 1) * 128],
                                                          identity=identb[:]),
                          reads=[b_ynb[cb], b_identb], writes=[psb[0]])
                sc.op("act", lambda: A.copy(out=YNst[ub][:, :, tsl], in_=pT[:, :].rearrange("p (c s) -> p c s", c=8)),
                      reads=[psb[0]], writes=[b_YNst[ub]])
                if g + 1 < NG:
                    conv_piece(g + 1, range(3 * tt, 3 * tt + 3))
            sc.dma("pool", YNTd[:, g * 512:(g + 1) * 512].rearrange("(c p) s -> p c s", p=128), YNst[ub][:],
                   reads=[b_YNst[ub]], writes=[b_YNT[g]])
        _fence(nc, sc)


def _phase_BC(nc, sc, l, lambda_init, ps, psb, psbf, QTd, KTd, Vd, Gd, OGTd, b_QK, b_V, b_G, b_OGT,
              UTd, DTd, Zd, YNTd, b_UT, b_DT, b_Z, b_YNT,
              sm, b_sm, identb, b_identb, maskb, b_maskb, cst, b_cst):
    V, A, T, P = nc.vector, nc.scalar, nc.tensor, nc.gpsimd
    ident_f = cst[:, 0, :]
    tri_f = cst[:, 1, :]
    negm_f = cst[:, 2, :]
    ones_f = cst[:, 3, :]

    def smv(name):
        a, b = SM[name]
        return sm[:, a:b]

    with contextlib.ExitStack() as es2:
        def sb(name, shape, dt):
            return es2.enter_context(nc.sbuf_tensor(f"{name}_BC{l}", list(shape), dt))

        KT = [sb(f"KT{i}", [128, S], BF16) for i in range(2)]
        Va = sb("Va", [128, NT, 130], BF16)
        QZ = [[sb(f"QZ{i}_{t}", [128, 512], BF16) for t in range(2)] for i in range(2)]
        Et = [sb(f"E{i}", [128, 512], BF16) for i in range(4)]
        Gt = [sb(f"Gt{i}", [128, 4, 128], F32) for i in range(2)]
        gw = [sb(f"gw{i}", [128, 4, 128], F32) for i in range(2)]
        o1 = sb("o1", [128, 4, 128], F32)
        od = [sb(f"od{i}", [128, 128], F32) for i in range(2)]
        junkb = sb("junkb", [128, 128], F32)
        st = [sb(f"st{i}", [128, 8], F32) for i in range(2)]
        ogb = [sb(f"ogb{i}", [128, 128], BF16) for i in range(4)]
        ogT = [sb(f"ogT{i}", [128, 512], BF16) for i in range(2)]
        wsub = sb("wsub", [128, 128], F32)
        lam = sb("lam", [128, 8], F32)
        ljunk = sb("ljunk", [128, 64], F32)
        b_KT, b_Gt, b_gw, b_od, b_st, b_ogT = ([Buf(), Buf()] for _ in range(6))
        b_Va = Buf()
        b_ogb = [Buf() for _ in range(4)]
        b_E = [Buf() for _ in range(4)]
        b_o1, b_junkb, b_wsub, b_lam = Buf(), Buf(), Buf(), Buf()
        b_QZ = [[Buf(), Buf()], [Buf(), Buf()]]
        U = sb("U", [128, 12, 515], F32)
        acc = [sb(f"acc{i}", [128, 512], F32) for i in range(2)]
        XC = [sb(f"XC{i}", [128, 12, 512], BF16) for i in range(2)]
        DTg = [sb(f"DTg{i}", [128, 4, 16], F32) for i in range(2)]
        Zt = sb("Zt", [128, D], F32)
        xd = sb("xd", [128, D], BF16)
        xdd = sb("xdd", [128, D], BF16)
        xsk = sb("xsk", [128, D], F32)
        Btm = sb("Btm", [128, 256], BF16)
        sml = [sb(f"sml{i}", [128, 8, 4, 16], F32) for i in range(2)]
        LT = [sb(f"LT{i}", [128, 128], F32) for i in range(8)]
        MT = [sb(f"MT{i}", [128, 128], BF16) for i in range(8)]
        t1 = [sb(f"t1{i}", [128, 512], F32) for i in range(2)]
        yy = sb("yy", [128, D], F32)
        Hs = sb("Hs", [128, 2, 512], F32)
        Hb = sb("Hb", [128, 2, 512], BF16)
        a_row = sb("a_row", [128, 16], F32)
        nst = [sb(f"nst{i}", [128, 8], F32) for i in range(2)]
        junk = sb("junk", [128, 512], F32)
        ynb = sb("ynb", [128, D], BF16)
        YNst = sb("YNst", [128, 8, 512], BF16)
        b_acc, b_XC, b_DTg, b_sml, b_t1, b_nst = ([Buf(), Buf()] for _ in range(6))
        b_U, b_Zt, b_xd, b_xdd, b_xsk, b_Btm, b_yy, b_ynb, b_YNst = (Buf() for _ in range(9))
        b_LT = [Buf() for _ in range(8)]
        b_MT = [Buf() for _ in range(8)]
        b_Hs, b_Hb, b_arow, b_junk = Buf(), Buf(), Buf(), Buf()

        def gen_B():
            sc.op("dve", lambda: V.scalar_tensor_tensor(out=ljunk[:], in0=smv("lq1"), scalar=1.0, in1=smv("lk1"),
                                                        op0=ALU.mult, op1=ALU.mult, accum_out=lam[:, 0:1]),
                  reads=[b_sm], writes=[b_lam])
            sc.op("dve", lambda: V.scalar_tensor_tensor(out=ljunk[:], in0=smv("lq2"), scalar=1.0, in1=smv("lk2"),
                                                        op0=ALU.mult, op1=ALU.mult, accum_out=lam[:, 1:2]),
                  reads=[b_sm, b_lam], writes=[b_lam])
            sc.op("act", lambda: A.activation(out=lam[:, 2:4], in_=lam[:, 0:2], func=AF.Exp), reads=[b_lam],
                  writes=[b_lam])
            sc.op("dve", lambda: V.tensor_tensor(out=lam[:, 4:5], in0=lam[:, 2:3], in1=lam[:, 3:4], op=ALU.subtract),
                  reads=[b_lam], writes=[b_lam])
            sc.op("dve", lambda: V.tensor_scalar(out=lam[:, 5:6], in0=lam[:, 4:5], scalar1=lambda_init, scalar2=-1.0,
                                                 op0=ALU.add, op1=ALU.mult), reads=[b_lam], writes=[b_lam])
            sc.op("dve", lambda: V.tensor_scalar(out=wsub[:], in0=smv("subw"), scalar1=(1.0 - lambda_init),
                                                 scalar2=None, op0=ALU.mult), reads=[b_sm], writes=[b_wsub])
            sc.op("dve", lambda: V.memset(Va[:, :, 128:130], 1.0), writes=[b_Va])
            for i in range(2):
                sc.op("dve", lambda i=i: V.memset(QZ[i][0][64:128, :], 0.0), writes=[b_QZ[i][0]])
                sc.op("dve", lambda i=i: V.memset(QZ[i][1][0:64, :], 0.0), writes=[b_QZ[i][1]])
            sbanks = [0, 1]
            scnt = [0]
            for h in range(4):
                hb = h % 2
                sc.dma("sp", KT[hb][:], KTd[h, :, :], reads=b_QK, writes=[b_KT[hb]])
                for q4 in range(4):
                    sc.dma("sp", Va[:, q4 * 16:(q4 + 1) * 16, 0:128],
                           Vd[q4 * 2048:(q4 + 1) * 2048, h * 128:(h + 1) * 128].rearrange("(t p) e -> p t e", p=128),
                           reads=b_V[q4 * 16:(q4 + 1) * 16], writes=[b_Va])
                for qg in range(NG):
                    qb = qg % 2
                    sc.dma("sp", QZ[qb][0][0:64, :], QTd[h, 0:64, qg * 512:(qg + 1) * 512], reads=[b_QK[qg]],
                           writes=[b_QZ[qb][0]])
                    sc.dma("sp", QZ[qb][1][64:128, :], QTd[h, 64:128, qg * 512:(qg + 1) * 512], reads=[b_QK[qg]],
                           writes=[b_QZ[qb][1]])
                    sc.dma("sp", Gt[qb][:], Gd[qg * 512:(qg + 1) * 512, h * 128:(h + 1) * 128].rearrange(
                        "(i p) e -> p i e", p=128), reads=b_G[qg * 4:qg * 4 + 4], writes=[b_Gt[qb]])
                    sc.op("pool", lambda: P.tensor_tensor(out=gw[qb][:], in0=Gt[qb][:],
                                                          in1=wsub[:].unsqueeze(1).to_broadcast([128, 4, 128]),
                                                          op=ALU.mult),
                          reads=[b_Gt[qb], b_wsub], writes=[b_gw[qb]])
                    for t in range(2):
                        nj = 4 * qg + 4
                        slots = {}

                        def emit_score(j):
                            c0 = max(0, j - 4 * qg) * 128
                            sbk = sbanks[scnt[0] % 2]
                            ei = scnt[0] % 4
                            scnt[0] += 1
                            slots[j] = (c0, ei)
                            sc.op("pe", lambda: T.matmul(
                                ps[sbk][:, c0:512], lhsT=KT[hb][:, j * 128:(j + 1) * 128],
                                rhs=QZ[qb][t][:, c0:512], start=True, stop=True),
                                  reads=[b_KT[hb], b_QZ[qb][t]], writes=[psb[sbk]])
                            sc.op("act", lambda: A.activation(
                                out=Et[ei][:, c0:512], in_=ps[sbk][:, c0:512], func=AF.Exp, scale=0.125),
                                  reads=[psb[sbk]], writes=[b_E[ei]])
                            if j >= 4 * qg:
                                sc.op("dve", lambda: V.tensor_tensor(
                                    out=Et[ei][:, c0:c0 + 128], in0=Et[ei][:, c0:c0 + 128], in1=maskb[:],
                                    op=ALU.mult), reads=[b_E[ei], b_maskb], writes=[b_E[ei]])

                        def emit_pv(j):
                            c0, ei = slots[j]
                            for i in range(max(j, 4 * qg), 4 * qg + 4):
                                ii = i - 4 * qg
                                ob = 2 + ii // 2
                                oc = (ii % 2) * 256
                                sc.op("pe", lambda: T.matmul(
                                    ps[ob][:, oc:oc + 129], lhsT=Et[ei][:, ii * 128:(ii + 1) * 128],
                                    rhs=Va[:, j, 0:129], start=(j == 0 and ii % 2 == 0), stop=(j == i),
                                    skip_group_check=True),
                                      reads=[b_E[ei], b_Va], writes=[psb[ob]])

                        emit_score(0)
                        for j in range(nj):
                            if j + 1 < nj:
                                emit_score(j + 1)
                            emit_pv(j)
                            yield
                        for ii in range(4):
                            ob = 2 + ii // 2
                            oc = (ii % 2) * 256
                            s2 = st[ii % 2]
                            bs2 = b_st[ii % 2]
                            sc.op("dve", lambda: V.reciprocal(out=s2[:, 0:1], in_=ps[ob][:, oc + 128:oc + 129]),
                                  reads=[psb[ob]], writes=[bs2])
                            if t == 0:
                                sc.op("dve", lambda: V.tensor_scalar(
                                    out=o1[:, ii, :], in0=ps[ob][:, oc:oc + 128], scalar1=s2[:, 0:1], scalar2=None,
                                    op0=ALU.mult), reads=[psb[ob], bs2], writes=[b_o1])
                            else:
                                oi = ii % 2
                                sc.op("dve", lambda: V.tensor_tensor(out=s2[:, 1:2], in0=s2[:, 0:1],
                                                                     in1=lam[:, 5:6], op=ALU.mult),
                                      reads=[bs2, b_lam], writes=[bs2])
                                sc.op("dve", lambda: V.scalar_tensor_tensor(
                                    out=od[oi][:], in0=ps[ob][:, oc:oc + 128], scalar=s2[:, 1:2], in1=o1[:, ii, :],
                                    op0=ALU.mult, op1=ALU.add), reads=[psb[ob], bs2, b_o1], writes=[b_od[oi]])
                                sc.op("dve", lambda: V.scalar_tensor_tensor(
                                    out=junkb[:], in0=od[oi][:], scalar=1.0, in1=od[oi][:], op0=ALU.mult,
                                    op1=ALU.mult, accum_out=s2[:, 2:3]), reads=[b_od[oi]], writes=[b_junkb, bs2])
                                emit_rstd(nc, sc, s2[:, 4:5], s2[:, 3:4], s2[:, 2:3], 1.0 / 128, 1, [bs2], [bs2])
                                sc.op("dve", lambda: V.scalar_tensor_tensor(
                                    out=ogb[ii][:], in0=od[oi][:], scalar=s2[:, 4:5], in1=gw[qb][:, ii, :],
                                    op0=ALU.mult, op1=ALU.mult), reads=[b_od[oi], bs2, b_gw[qb]],
                                      writes=[b_ogb[ii]])
                        if t == 1:
                            sbk = sbanks[scnt[0] % 2]
                            scnt[0] += 1
                            pT = psbf(sbk)
                            for ii in range(4):
                                sc.op("pe", lambda: T.transpose(out=pT[:, ii * 128:(ii + 1) * 128],
                                                                in_=ogb[ii][:], identity=identb[:]),
                                      reads=[b_ogb[ii], b_identb], writes=[psb[sbk]])
                            sc.op("act", lambda: A.copy(out=ogT[qb][:], in_=pT[:, 0:512]),
                                  reads=[psb[sbk]], writes=[b_ogT[qb]])
                            sc.dma("pool", OGTd[h * 128:(h + 1) * 128, qg * 512:(qg + 1) * 512], ogT[qb][:],
                                   reads=[b_ogT[qb]], writes=[b_OGT[h][qg]])
                        yield

        cw0 = SM["cw"][0]
        cb0 = SM["cb"][0]
        PB_, PX_, PM_, PY_ = 4, 5, 6, 7

        def load_group(g):
            ub = g % 2
            if g == 0:
                sc.op("dve", lambda: V.memset(U[:, :, 0:3], 0.0), writes=[b_U])
                sc.dma("sp", U[:, :, 3:515], UTd[:, 0:512].rearrange("(c p) s -> p c s", p=128),
                       reads=[b_UT[0]], writes=[b_U])
            else:
                sc.dma("sp", U[:, :, :], UTd[:, g * 512 - 3:g * 512 + 512].rearrange("(c p) s -> p c s", p=128),
                       reads=[b_UT[g - 1], b_UT[g]], writes=[b_U])
            sc.dma("sp", DTg[ub][:], DTd[g * 512:(g + 1) * 512, :].rearrange("(i p) e -> p i e", p=128),
                   reads=b_DT[4 * g:4 * g + 4], writes=[b_DTg[ub]])

        def conv_piece(g, ccs):
            ub = g % 2
            for cc in ccs:
                ab = cc % 2
                w = lambda k, cc=cc: sm[:, cw0 + cc * 4 + k:cw0 + cc * 4 + k + 1]
                sc.op("dve", lambda: V.tensor_scalar(
                    out=acc[ab][:], in0=U[:, cc, 3:515], scalar1=w(3), scalar2=sm[:, cb0 + cc:cb0 + cc + 1],
                    op0=ALU.mult, op1=ALU.add), reads=[b_U, b_sm], writes=[b_acc[ab]])
                for k in (2, 1, 0):
                    sc.op("dve", lambda: V.scalar_tensor_tensor(
                        out=acc[ab][:], in0=U[:, cc, k:k + 512], scalar=w(k), in1=acc[ab][:],
                        op0=ALU.mult, op1=ALU.add), reads=[b_U, b_sm, b_acc[ab]], writes=[b_acc[ab]])
                sc.op("act", lambda: A.activation(out=XC[ub][:, cc, :], in_=acc[ab][:], func=AF.Silu),
                      reads=[b_acc[ab]], writes=[b_XC[ub]])

        def gen_C():
            sc.op("act", lambda: A.activation(out=a_row[:], in_=smv("alog"), func=AF.Exp), reads=[b_sm],
                  writes=[b_arow])
            sc.op("dve", lambda: V.tensor_scalar(out=a_row[:], in0=a_row[:], scalar1=-1.0, scalar2=None,
                                                 op0=ALU.mult), reads=[b_arow], writes=[b_arow])
            sc.op("dve", lambda: V.memset(Hs[:], 0.0), writes=[b_Hs])
            sc.op("dve", lambda: V.memset(Hb[:], 0.0), writes=[b_Hb])
            load_group(0)
            for q in range(4):
                conv_piece(0, range(3 * q, 3 * q + 3))
                yield
            for g in range(NG):
                ub = g % 2
                if g + 1 < NG:
                    load_group(g + 1)
                s_ = sml[ub]
                bs = b_sml[ub]
                row = lambda r: s_[:, r, :, :]
                row2 = lambda r: s_[:, r, :, :].rearrange("p t h -> p (t h)")
                sc.op("dve", lambda: V.tensor_tensor(out=row(0), in0=DTg[ub][:],
                                                     in1=a_row[:].unsqueeze(1).to_broadcast([128, 4, 16]),
                                                     op=ALU.mult), reads=[b_DTg[ub], b_arow], writes=[bs])
                sc.op("pe", lambda: T.matmul(ps[PM_][:, 128:192], lhsT=tri_f, rhs=row2(0), start=True, stop=True),
                      reads=[bs, b_cst], writes=[psb[PM_]])
                sc.op("pe", lambda: T.matmul(ps[PM_][:, 192:256], lhsT=ones_f, rhs=row2(0), start=True, stop=True),
                      reads=[bs, b_cst], writes=[psb[PM_]])
                sc.op("dve", lambda: V.tensor_copy(out=row2(1), in_=ps[PM_][:, 128:192]), reads=[psb[PM_]],
                      writes=[bs])
                sc.op("dve", lambda: V.tensor_scalar(out=row2(2), in0=ps[PM_][:, 128:192], scalar1=-1.0,
                                                     scalar2=None, op0=ALU.mult), reads=[psb[PM_]], writes=[bs])
                sc.op("dve", lambda: V.tensor_tensor(out=row2(3), in0=ps[PM_][:, 192:256], in1=row2(1),
                                                     op=ALU.subtract), reads=[psb[PM_], bs], writes=[bs])
                sc.op("act", lambda: A.activation(out=row2(4), in_=row2(3), func=AF.Exp), reads=[bs], writes=[bs])
                sc.op("act", lambda: A.activation(out=row2(5), in_=row2(1), func=AF.Exp), reads=[bs], writes=[bs])
                sc.op("act", lambda: A.activation(out=row2(6), in_=ps[PM_][:, 192:256], func=AF.Exp),
                      reads=[psb[PM_], bs], writes=[bs])
                sc.op("dve", lambda: V.tensor_copy(out=s_[:, 7, 0, 0:1], in_=s_[:, 6, 0, 0:1]), reads=[bs],
                      writes=[bs])
                yield
                for tt in range(4):
                    c = 4 * g + tt
                    tsl = slice(tt * 128, (tt + 1) * 128)
                    sc.dma("sp", Zt[:], Zd[c * 128:(c + 1) * 128, :], reads=[b_Z[c]], writes=[b_Zt])
                    pX = psbf(PX_)
                    for cc in range(8):
                        sc.op("pe", lambda: T.transpose(out=pX[:, cc * 128:(cc + 1) * 128], in_=XC[ub][:, cc, tsl],
                                                        identity=identb[:]),
                              reads=[b_XC[ub], b_identb], writes=[psb[PX_]])
                    pB = psbf(PM_)
                    for k in range(2):
                        sc.op("pe", lambda: T.transpose(out=pB[:, k * 128:(k + 1) * 128], in_=XC[ub][:, 8 + k, tsl],
                                                        identity=identb[:]),
                              reads=[b_XC[ub], b_identb], writes=[psb[PM_]])
                    yield
                    pX3 = pX[:, :].rearrange("p (h e) -> p h e", e=64)
                    bc = lambda r: s_[:, r, tt, :].unsqueeze(2).to_broadcast([128, 16, 64])
                    v3 = lambda tl: tl[:].rearrange("p (h e) -> p h e", e=64)
                    sc.op("dve", lambda: V.tensor_tensor(out=v3(xd), in0=pX3,
                                                         in1=DTg[ub][:, tt, :].unsqueeze(2).to_broadcast(
                                                             [128, 16, 64]), op=ALU.mult),
                          reads=[psb[PX_], b_DTg[ub]], writes=[b_xd])
                    sc.op("dve", lambda: V.tensor_tensor(out=v3(xsk), in0=pX3,
                                                         in1=smv("dsk").unsqueeze(2).to_broadcast([128, 16, 64]),
                                                         op=ALU.mult), reads=[psb[PX_], b_sm], writes=[b_xsk])
                    sc.op("dve", lambda: V.tensor_tensor(out=v3(xdd), in0=v3(xd), in1=bc(4), op=ALU.mult),
                          reads=[b_xd, bs], writes=[b_xdd])
                    sc.op("act", lambda: A.copy(out=Btm[:], in_=pB[:, 0:256]), reads=[psb[PM_]], writes=[b_Btm])
                    yield
                    for gg in range(2):
                        gs = slice(gg * 512, (gg + 1) * 512)
                        sc.op("pe", lambda: T.matmul(ps[PM_][:, 256:384], lhsT=XC[ub][:, 8 + gg, tsl],
                                                     rhs=XC[ub][:, 10 + gg, tsl], start=True, stop=True),
                              reads=[b_XC[ub]], writes=[psb[PM_]])
                        sc.op("pe", lambda: T.matmul(ps[PY_][:, :], lhsT=XC[ub][:, 10 + gg, tsl], rhs=Hb[:, gg, :],
                                                     start=True, stop=True),
                              reads=[b_XC[ub], b_Hb], writes=[psb[PY_]])
                        sc.op("pe", lambda: T.matmul(ps[PX_][:, :], lhsT=Btm[:, gg * 128:(gg + 1) * 128],
                                                     rhs=xdd[:, gs], start=True, stop=True),
                              reads=[b_Btm, b_xdd], writes=[psb[PX_]])
                        tb = gg
                        sc.op("dve", lambda: V.tensor_tensor(
                            out=t1[tb][:].rearrange("p (h e) -> p h e", e=64),
                            in0=ps[PY_][:, :].rearrange("p (h e) -> p h e", e=64),
                            in1=s_[:, 5, tt, gg * 8:(gg + 1) * 8].unsqueeze(2).to_broadcast([128, 8, 64]),
                            op=ALU.mult), reads=[psb[PY_], bs], writes=[b_t1[tb]])
                        yield
                        for half in range(2):
                            for j4 in range(4):
                                j = half * 4 + j4
                                hh = gg * 8 + j
                                cs = slice(j4 * 128, (j4 + 1) * 128)
                                sc.op("pe", lambda: T.matmul(
                                    ps[PB_][:, cs], lhsT=s_[:, 0, tt, hh:hh + 1].to_broadcast([128, 128]), rhs=tri_f,
                                    start=True, stop=False), reads=[bs, b_cst], writes=[psb[PB_]])
                                sc.op("pe", lambda: T.matmul(ps[PB_][:, cs], lhsT=ident_f, rhs=negm_f,
                                                             start=False, stop=True),
                                      reads=[b_cst], writes=[psb[PB_]])
                            for j4 in range(4):
                                j = half * 4 + j4
                                hh = gg * 8 + j
                                cs = slice(j4 * 128, (j4 + 1) * 128)
                                sc.op("act", lambda: A.activation(
                                    out=LT[j][:], in_=ps[PB_][:, cs], func=AF.Exp, bias=s_[:, 2, tt, hh:hh + 1],
                                    scale=1.0), reads=[psb[PB_], bs], writes=[b_LT[j]])
                                sc.op("dve", lambda: V.tensor_tensor(out=MT[j][:], in0=LT[j][:],
                                                                     in1=ps[PM_][:, 256:384], op=ALU.mult),
                                      reads=[b_LT[j], psb[PM_]], writes=[b_MT[j]])
                            yield
                        for j in range(8):
                            hh = gg * 8 + j
                            sc.op("pe", lambda: T.matmul(
                                ps[PY_][:, j * 64:(j + 1) * 64], lhsT=MT[j][:], rhs=xd[:, hh * 64:(hh + 1) * 64],
                                start=True, stop=True), reads=[b_MT[j], b_xd], writes=[psb[PY_]])
                        sc.op("dve", lambda: V.tensor_tensor(out=t1[tb][:], in0=ps[PY_][:, :], in1=t1[tb][:],
                                                             op=ALU.add),
                              reads=[psb[PY_], b_t1[tb]], writes=[b_t1[tb]])
                        sc.op("pool", lambda: P.tensor_tensor(out=yy[:, gs], in0=t1[tb][:], in1=xsk[:, gs],
                                                              op=ALU.add),
                              reads=[b_t1[tb], b_xsk], writes=[b_yy])
                        sc.op("dve", lambda: V.tensor_tensor(
                            out=Hs[:, gg, :].rearrange("p (h e) -> p h e", e=64),
                            in0=Hs[:, gg, :].rearrange("p (h e) -> p h e", e=64),
                            in1=s_[:, 6, tt, gg * 8:(gg + 1) * 8].unsqueeze(2).to_broadcast([128, 8, 64]),
                            op=ALU.mult), reads=[b_Hs, bs], writes=[b_Hs])
                        sc.op("dve", lambda: V.tensor_tensor(out=Hs[:, gg, :], in0=ps[PX_][:, :], in1=Hs[:, gg, :],
                                                             op=ALU.add),
                              reads=[psb[PX_], b_Hs], writes=[b_Hs])
                        sc.op("act", lambda: A.copy(out=Hb[:, gg, :], in_=Hs[:, gg, :]), reads=[b_Hs],
                              writes=[b_Hb])
                        yield
                    n_ = nst[c % 2]
                    bn = b_nst[c % 2]
                    sc.op("dve", lambda: V.tensor_tensor(out=yy[:], in0=yy[:], in1=Zt[:], op=ALU.mult),
                          reads=[b_yy, b_Zt], writes=[b_yy])
                    for gg in range(2):
                        gs = slice(gg * 512, (gg + 1) * 512)
                        sc.op("dve", lambda: V.scalar_tensor_tensor(
                            out=junk[:], in0=yy[:, gs], scalar=1.0, in1=yy[:, gs], op0=ALU.mult, op1=ALU.mult,
                            accum_out=n_[:, gg:gg + 1]), reads=[b_yy], writes=[b_junk, bn])
                    emit_rstd(nc, sc, n_[:, 4:6], n_[:, 2:4], n_[:, 0:2], 1.0 / 512, 2, [bn], [bn])
                    w0 = SM["ssmw"][0]
                    for gg in range(2):
                        gs = slice(gg * 512, (gg + 1) * 512)
                        sc.op("dve", lambda: V.scalar_tensor_tensor(
                            out=ynb[:, gs], in0=yy[:, gs], scalar=n_[:, 4 + gg:5 + gg],
                            in1=sm[:, w0 + gg * 512:w0 + (gg + 1) * 512], op0=ALU.mult, op1=ALU.mult),
                              reads=[b_yy, bn, b_sm], writes=[b_ynb])
                    yield
                    pT = psbf(PX_)
                    for cc in range(8):
                        sc.op("pe", lambda: T.transpose(out=pT[:, cc * 128:(cc + 1) * 128],
                                                        in_=ynb[:, cc * 128:(cc + 1) * 128], identity=identb[:]),
                              reads=[b_ynb, b_identb], writes=[psb[PX_]])
                    sc.op("act", lambda: A.copy(out=YNst[:, :, tsl],
                                                in_=pT[:, :].rearrange("p (c s) -> p c s", c=8)),
                          reads=[psb[PX_]], writes=[b_YNst])
                    if g + 1 < NG:
                        conv_piece(g + 1, range(3 * tt, 3 * tt + 3))
                    yield
                sc.dma("pool", YNTd[:, g * 512:(g + 1) * 512].rearrange("(c p) s -> p c s", p=128), YNst[:],
                       reads=[b_YNst], writes=[b_YNT[g]])
                yield

        gB = gen_B()
        gC = gen_C()
        nB = sum(2 * (4 * qg + 4 + 1) for qg in range(NG)) * 4
        nC = 4 + NG * (2 + 4 * (2 + 2 * 4 + 2))
        ratio = BC_RATIO * nC / nB
        accf = 0.0
        doneB = doneC = False
        while not (doneB and doneC):
            if not doneB:
                try:
                    next(gB)
                except StopIteration:
                    doneB = True
            accf += ratio
            while (accf >= 1.0 or doneB) and not doneC:
                accf -= 1.0
                try:
                    next(gC)
                except StopIteration:
                    doneC = True
                if doneB:
                    continue
        _fence(nc, sc)


def _phase_D(nc, sc, l, last, ps, psb, psbf, x_src, b_xsrc, OGTd, YNTd, MGd, X1d, y_out, b_OGT, b_YNT, b_MG,
             b_X1, b_Y, watt_in, wssm_in, wout_in, gate_b, b_gate, fnw_in):
    V, A, T, P = nc.vector, nc.scalar, nc.tensor, nc.gpsimd
    with contextlib.ExitStack() as es2:
        def sb(name, shape, dt):
            return es2.enter_context(nc.sbuf_tensor(f"{name}_D{l}", list(shape), dt))

        Watt = sb("Watt", [128, 4, D], BF16)
        Wssm = sb("Wssm", [128, 8, D], BF16)
        Wout = sb("Wout", [128, 8, D], BF16)
        fnw = sb("fnw", [128, D], F32)
        b_Wd, b_fnw = Buf(), Buf()
        wst = [sb(f"wst{i}", [128, D], F32) for i in range(2)]
        b_wst = [Buf(), Buf()]
        k = 0
        for (src, dst, nk) in ((watt_in, Watt, 4), (wssm_in, Wssm, 8), (wout_in, Wout, 8)):
            for kc in range(nk):
                i = k % 2
                k += 1
                sc.dma("sp", wst[i][:], src[l, kc * 128:(kc + 1) * 128, :], writes=[b_wst[i]])
                sc.op("dve", lambda dst=dst, kc=kc, i=i: V.tensor_copy(out=dst[:, kc, :], in_=wst[i][:]),
                      reads=[b_wst[i]], writes=[b_Wd])
        if last:
            sc.dma("sp", fnw[:], fnw_in[:, :], writes=[b_fnw])

        OGt = [sb(f"OGt{i}", [128, 4, 512], BF16) for i in range(2)]
        YNt = [sb(f"YNt{i}", [128, 8, 512], BF16) for i in range(2)]
        mga = [sb(f"mga{i}", [128, 512], F32) for i in range(2)]
        mgs = [sb(f"mgs{i}", [128, 512], F32) for i in range(2)]
        b_mga, b_mgs = [Buf(), Buf()], [Buf(), Buf()]
        xt = [sb(f"xt{i}", [128, D], F32) for i in range(2)]
        m1 = [sb(f"m1{i}", [128, 512], F32) for i in range(2)]
        m2 = [sb(f"m2{i}", [128, 512], F32) for i in range(2)]
        mT = [sb(f"mT{i}", [128, 8, 512], BF16) for i in range(2)]
        xo = [sb(f"xo{i}", [128, D], F32) for i in range(2)]
        yo = [sb(f"yo{i}", [128, D], F32) for i in range(2)]
        junk = sb("junk", [128, D], F32)
        fst = [sb(f"fst{i}", [128, 4], F32) for i in range(2)]
        b_OGt, b_YNt, b_MGt_unused, b_xt, b_m1, b_m2, b_mT, b_xo, b_yo, b_fst = ([Buf(), Buf()] for _ in range(10))
        b_junk = Buf()
        for g in range(NG):
            gb = g % 2
            sc.dma("sp", OGt[gb][:], OGTd[:, g * 512:(g + 1) * 512].rearrange("(c p) s -> p c s", p=128),
                   reads=[b_OGT[h][g] for h in range(4)], writes=[b_OGt[gb]])
            sc.dma("sp", YNt[gb][:], YNTd[:, g * 512:(g + 1) * 512].rearrange("(c p) s -> p c s", p=128),
                   reads=[b_YNT[g]], writes=[b_YNt[gb]])
            for dc in range(8):
                db = dc % 2
                dsl = slice(dc * 128, (dc + 1) * 128)
                sc.dma("sp", mga[db][:], MGd[dc * 128:(dc + 1) * 128, g * 512:(g + 1) * 512], reads=[b_MG[g]],
                       writes=[b_mga[db]])
                sc.dma("sp", mgs[db][:], MGd[(8 + dc) * 128:(9 + dc) * 128, g * 512:(g + 1) * 512], reads=[b_MG[g]],
                       writes=[b_mgs[db]])
                for kc in range(4):
                    sc.op("pe", lambda kc=kc, db=db, dsl=dsl: T.matmul(ps[db][:, :], lhsT=Watt[:, kc, dsl],
                                                                       rhs=OGt[gb][:, kc, :],
                                                                       start=(kc == 0), stop=(kc == 3)),
                          reads=[b_Wd, b_OGt[gb]], writes=[psb[db]])
                for kc in range(8):
                    sc.op("pe", lambda kc=kc, db=db, dsl=dsl: T.matmul(ps[2 + db][:, :], lhsT=Wssm[:, kc, dsl],
                                                                       rhs=YNt[gb][:, kc, :],
                                                                       start=(kc == 0), stop=(kc == 7)),
                          reads=[b_Wd, b_YNt[gb]], writes=[psb[2 + db]])
                sc.op("dve", lambda db=db, dc=dc: V.tensor_tensor(out=m1[db][:], in0=ps[db][:, :],
                                                                  in1=mga[db][:], op=ALU.mult),
                      reads=[psb[db], b_mga[db]], writes=[b_m1[db]])
                sc.op("dve", lambda db=db, dc=dc: V.tensor_tensor(out=m2[db][:], in0=ps[2 + db][:, :],
                                                                  in1=mgs[db][:], op=ALU.mult),
                      reads=[psb[2 + db], b_mgs[db]], writes=[b_m2[db]])
                sc.op("pool", lambda db=db, dc=dc: P.tensor_tensor(out=mT[gb][:, dc, :], in0=m1[db][:],
                                                                   in1=m2[db][:], op=ALU.add),
                      reads=[b_m1[db], b_m2[db]], writes=[b_mT[gb]])
            for tt in range(4):
                t = 4 * g + tt
                tb = t % 2
                tsl = slice(tt * 128, (tt + 1) * 128)
                rd = [b_xsrc[t]] if b_xsrc is not None else []
                sc.dma("sp", xt[tb][:], x_src[t * 128:(t + 1) * 128, :], reads=rd, writes=[b_xt[tb]])
                for n in range(2):
                    ob = 4 + n
                    ns = slice(n * 512, (n + 1) * 512)
                    for dc in range(8):
                        sc.op("pe", lambda dc=dc, ob=ob, ns=ns: T.matmul(ps[ob][:, :], lhsT=mT[gb][:, dc, tsl],
                                                                         rhs=Wout[:, dc, ns],
                                                                         start=(dc == 0), stop=(dc == 7)),
                              reads=[b_mT[gb], b_Wd], writes=[psb[ob]])
                    sc.op("dve", lambda ob=ob, ns=ns: V.tensor_tensor(out=xo[tb][:, ns], in0=ps[ob][:, :],
                                                                      in1=gate_b[:, ns], op=ALU.mult),
                          reads=[psb[ob], b_gate], writes=[b_xo[tb]])
                    sc.op("pool", lambda ns=ns: P.tensor_tensor(out=xo[tb][:, ns], in0=xo[tb][:, ns],
                                                                in1=xt[tb][:, ns], op=ALU.add),
                          reads=[b_xo[tb], b_xt[tb]], writes=[b_xo[tb]])
                if not last:
                    sc.dma("pool", X1d[t * 128:(t + 1) * 128, :], xo[tb][:], reads=[b_xo[tb]], writes=[b_X1[t]])
                else:
                    f = fst[tb]
                    sc.op("dve", lambda f=f: V.scalar_tensor_tensor(out=junk[:], in0=xo[tb][:], scalar=1.0,
                                                                    in1=xo[tb][:], op0=ALU.mult, op1=ALU.mult,
                                                                    accum_out=f[:, 0:1]),
                          reads=[b_xo[tb]], writes=[b_junk, b_fst[tb]])
                    emit_rstd(nc, sc, f[:, 2:3], f[:, 1:2], f[:, 0:1], 1.0 / D, 1, [b_fst[tb]], [b_fst[tb]])
                    sc.op("dve", lambda f=f: V.scalar_tensor_tensor(out=yo[tb][:], in0=xo[tb][:], scalar=f[:, 2:3],
                                                                    in1=fnw[:], op0=ALU.mult, op1=ALU.mult),
                          reads=[b_xo[tb], b_fst[tb], b_fnw], writes=[b_yo[tb]])
                    sc.dma("pool", y_out[t * 128:(t + 1) * 128, :], yo[tb][:], reads=[b_yo[tb]], writes=[b_Y[t]])
        _fence(nc, sc)


def _consts():
    k = np.arange(128)
    ident = np.eye(128, dtype=np.float32)
    tri = (k[:, None] <= k[None, :]).astype(np.float32)
    negm = np.where(k[None, :] >= k[:, None], 0.0, -30000.0).astype(np.float32)
    ones = np.ones((128, 128), np.float32)
    return np.ascontiguousarray(np.stack([ident, tri, negm, ones], axis=1))


def _smalls(inp):
    out = np.zeros((DEPTH, 128, NS), np.float32)

    def rep(v):
        return np.broadcast_to(np.asarray(v, np.float32)[None, :], (128, len(v)))

    for l in range(DEPTH):
        b_ada = inp["b_ada"][l]
        o = out[l]
        a, b = SM["bada"]; o[:, a:b] = b_ada[:2048].reshape(16, 128).T
        a, b = SM["nw"]; o[:, a:b] = inp["norm_w"][l].reshape(8, 128).T
        a, b = SM["dtb"]; o[:, a:b] = rep(inp["dt_bias"][l])
        a, b = SM["alog"]; o[:, a:b] = rep(inp["a_log"][l])
        a, b = SM["dsk"]; o[:, a:b] = rep(inp["d_skip"][l])
        a, b = SM["cw"]; o[:, a:b] = inp["conv_w"][l].reshape(4, 12, 128).transpose(2, 1, 0).reshape(128, 48)
        a, b = SM["cb"]; o[:, a:b] = inp["conv_b"][l].reshape(12, 128).T
        a, b = SM["subw"]; o[:, a:b] = rep(inp["attn_subln_w"][l])
        a, b = SM["lq1"]; o[:, a:b] = rep(inp["lambda_q1"][l])
        a, b = SM["lk1"]; o[:, a:b] = rep(inp["lambda_k1"][l])
        a, b = SM["lq2"]; o[:, a:b] = rep(inp["lambda_q2"][l])
        a, b = SM["lk2"]; o[:, a:b] = rep(inp["lambda_k2"][l])
        a, b = SM["bgate"]; o[:, a:b] = rep(b_ada[2048:])
        a, b = SM["ssmw"]; o[:, a:b] = rep(inp["ssm_norm_w"][l])
    return out


def make_in_maps(inp, n_cores=8):
    inp = {k: np.asarray(v) for k, v in inp.items()}
    cst = _consts()
    smalls = _smalls(inp)
    invf = (np.float32(ROPE_THETA) ** (-np.arange(0, 16, 2, dtype=np.float32) / np.float32(16))).astype(np.float32)
    invf = np.ascontiguousarray(np.broadcast_to(invf[None, :], (128, 8)))
    fnw = np.ascontiguousarray(np.broadcast_to(inp["final_norm_w"].astype(np.float32)[None, :], (128, D)))
    shared = dict(cst=cst, smalls=smalls, invf=invf, fnw=fnw,
                  w_ada=np.ascontiguousarray(inp["w_ada"], dtype=np.float32),
                  w_in=np.ascontiguousarray(inp["w_in"], dtype=np.float32),
                  w_att=np.ascontiguousarray(inp["w_att_branch"], dtype=np.float32),
                  w_ssm=np.ascontiguousarray(inp["w_ssm_branch"], dtype=np.float32),
                  w_out=np.ascontiguousarray(inp["w_out"], dtype=np.float32))
    maps = []
    for core in range(n_cores):
        b = core % 4
        c = inp["c"][b].astype(np.float32)
        cpk = c.reshape(8, 128).T
        m = dict(shared)
        m["x"] = np.ascontiguousarray(inp["x"][b], dtype=np.float32)
        m["pos"] = np.ascontiguousarray(inp["positions"][b].astype(np.int32).reshape(NT, 128).T)
        m["cT2"] = np.ascontiguousarray(np.repeat(cpk[:, :, None], 2, axis=2))
        m["cB"] = np.ascontiguousarray(np.repeat(cpk[:, :, None], 128, axis=2))
        maps.append(m)
    return maps


_NC_CACHE = {}


def kernel(**inputs):
    if "nc" not in _NC_CACHE:
        _NC_CACHE["nc"] = build_program()
    nc = _NC_CACHE["nc"]
    maps = make_in_maps(inputs, 8)
    res = run_bass_kernel_spmd(nc, maps, core_ids=list(range(8)))
    out = np.stack([np.asarray(res.results[b]["y"], dtype=np.float32) for b in range(4)], axis=0)
    return out
```

```python
import contextlib
import math

import numpy as np

import concourse.bass as bass
import concourse.mybir as mybir
from concourse.alu_op_type import AluOpType as ALU
from concourse.bass_utils import run_bass_kernel_spmd

F32 = mybir.dt.float32
BF16 = mybir.dt.bfloat16
I32 = mybir.dt.int32
AF = mybir.ActivationFunctionType

D = 1024
S = 8192
NT = S // 128
NG = S // 512
import os as _os
NGL = int(_os.environ.get('KDBG_NG', NG))
DEPTH = 2
IN_COLS = 6672
O_Q, O_K, O_V, O_G, O_Z, O_X, O_DT, O_MA, O_MS = 0, 512, 1024, 1536, 2048, 3072, 4608, 4624, 5648
EPS = 1e-5
ROPE_THETA = 500000.0

SM = {}
_o = 0
for _n, _w in [("bada", 16), ("nw", 8), ("dtb", 16), ("alog", 16), ("dsk", 16), ("cw", 48), ("cb", 12),
               ("subw", 128), ("lq1", 64), ("lk1", 64), ("lq2", 64), ("lk2", 64), ("bgate", 1024),
               ("ssmw", 1024)]:
    SM[_n] = (_o, _o + _w)
    _o += _w
NS = _o


class _Stop(Exception):
    pass


KSTOP = int(_os.environ.get('KDBG_STOP', -1))


_DEAD = [False]


def dbg_stop(k):
    if k == KSTOP:
        _DEAD[0] = True


class Buf:
    __slots__ = ("w", "r", "rd", "name", "psum")

    def __init__(self, name="", psum=False):
        self.psum = psum
        self.w = None
        self.r = {}
        self.rd = []
        self.name = name


_G = {}
INTERLEAVE_BC = _os.environ.get('KDBG_NOBC') is None
BC_RATIO = float(_os.environ.get('KDBG_BCR', 1.0))


def emit_rstd(nc, sc, out, tmp, in_, scale, n, reads, writes):
    V, P = nc.vector, nc.gpsimd
    sc.op("dve", lambda: V.tensor_scalar(out=tmp, in0=in_, scalar1=scale, scalar2=EPS, op0=ALU.mult, op1=ALU.add),
          reads=reads, writes=writes)
    sc.op("pool", lambda: P.tensor_tensor(out=out, in0=tmp, in1=_G["negh"][:, 0:n], op=ALU.pow),
          reads=list(writes) + [_G["b_negh"]], writes=writes)


class Sched:
    LIM = 30000
    NQ = 16

    def __init__(self, nc, es):
        self.nc = nc
        self.es = es
        self.engs = {"pe": nc.tensor, "act": nc.scalar, "dve": nc.vector, "pool": nc.gpsimd, "sp": nc.sync}
        self.cnt = {"pe": 0, "act": 0, "dve": 0, "pool": 0}
        self.csem = {e: [] for e in self.cnt}
        self.dcnt = {"sp": 0, "pool": 0}
        self.dsem = {q: [es.enter_context(nc.semaphore(f"d_{q}_{i}")) for i in range(self.NQ)] for q in self.dcnt}
        self.seen = {e: {} for e in self.engs}

    def _csem(self, e, gen):
        while len(self.csem[e]) <= gen:
            self.csem[e].append(self.es.enter_context(self.nc.semaphore(f"c_{e}_{len(self.csem[e])}")))
        return self.csem[e][gen]

    def _wait(self, E, ev):
        kind, src, n = ev
        if kind == "c":
            if src == "pe" and E == "pe":
                return
            key = ("c", src)
            if self.seen[E].get(key, 0) >= n:
                return
            gen, val = (n - 1) // self.LIM, (n - 1) % self.LIM + 1
            self.engs[E].wait_ge(self._csem(src, gen), val)
            self.seen[E][key] = n
        else:
            slot, val = n % self.NQ, 16 * (n // self.NQ + 1)
            key = ("d", src, slot)
            if self.seen[E].get(key, 0) >= val:
                return
            self.engs[E].wait_ge(self.dsem[src][slot], val)
            self.seen[E][key] = val

    def _deps(self, E, reads, writes):
        for b in reads:
            if b.w is not None:
                self._wait(E, b.w)
            if b.psum:
                for e2, ev in list(b.r.items()):
                    if e2 != E:
                        self._wait(E, ev)
        for b in writes:
            if b.w is not None:
                self._wait(E, b.w)
            for ev in b.r.values():
                self._wait(E, ev)
            for ev in b.rd:
                self._wait(E, ev)

    def op(self, E, inst_fn, reads=(), writes=()):
        if _DEAD[0]:
            return None
        self._deps(E, reads, writes)
        inst = inst_fn()
        self.cnt[E] += 1
        n = self.cnt[E]
        inst.then_inc(self._csem(E, (n - 1) // self.LIM), 1)
        ev = ("c", E, n)
        for b in reads:
            b.r[E] = ev
        for b in writes:
            b.w = ev
            b.r = {}
            b.rd = []
        return ev

    def dma(self, q, out, in_, reads=(), writes=()):
        if _DEAD[0]:
            return None
        n = self.dcnt[q]
        if n >= self.NQ:
            self._wait(q, ("d", q, n - self.NQ))
        self._deps(q, reads, writes)
        inst = self.engs[q].dma_start(out=out, in_=in_)
        inst.then_inc(self.dsem[q][n % self.NQ], 16)
        ev = ("d", q, n)
        self.dcnt[q] += 1
        for b in reads:
            b.rd.append(ev)
        for b in writes:
            b.w = ev
            b.r = {}
            b.rd = []
        return ev

    def wait_all(self, E, bufs):
        for b in bufs:
            if b.w is not None:
                self._wait(E, b.w)


_DUMPS = []


def dbg_dump(nc, sc, name, ap, shape, dt, reads):
    d = nc.dram_tensor("dbg_" + name, list(shape), dt, kind="ExternalOutput").ap()
    b = Buf()
    sc.dma("pool", d, ap, reads=reads, writes=[b])
    _DUMPS.append(b)


def build_program(n_layers=DEPTH, phases="ABCD", debug=False):
    nc = bass.Bass("TRN2", target_bir_lowering=False)
    es = contextlib.ExitStack()
    with es:
        _emit(nc, es, n_layers, phases, debug)
    return nc


def _emit(nc, es, n_layers, phases, debug):
    sc = Sched(nc, es)
    del _DUMPS[:]
    V, A, T, P = nc.vector, nc.scalar, nc.tensor, nc.gpsimd

    def din(name, shape, dt=F32):
        return nc.dram_tensor(name, list(shape), dt, kind="ExternalInput").ap()

    def dscr(name, shape, dt):
        kind = "ExternalOutput" if debug else "Internal"
        return nc.dram_tensor(name, list(shape), dt, kind=kind).ap()

    x_in = din("x", [S, D])
    pos_in = din("pos", [128, NT], I32)
    invf_in = din("invf", [128, 8])
    cst_in = din("cst", [128, 4, 128])
    cT2_in = din("cT2", [128, 8, 2])
    cB_in = din("cB", [128, 8, 128])
    sm_in = din("smalls", [DEPTH, 128, NS])
    fnw_in = din("fnw", [128, D])
    wada_in = din("w_ada", [DEPTH, D, 3 * D])
    win_in = din("w_in", [DEPTH, D, IN_COLS])
    watt_in = din("w_att", [DEPTH, 512, D])
    wssm_in = din("w_ssm", [DEPTH, D, D])
    wout_in = din("w_out", [DEPTH, D, D])
    y_out = nc.dram_tensor("y", [S, D], F32, kind="ExternalOutput").ap()

    QTd = dscr("QTd", [4, 128, S], BF16)
    KTd = dscr("KTd", [4, 128, S], BF16)
    Vd = dscr("Vd", [S, 512], BF16)
    Gd = dscr("Gd", [S, 512], F32)
    Zd = dscr("Zd", [S, D], F32)
    DTd = dscr("DTd", [S, 16], F32)
    UTd = dscr("UTd", [1536, S], F32)
    MGd = dscr("MGd", [2048, S], F32)
    OGTd = dscr("OGTd", [512, S], BF16)
    YNTd = dscr("YNTd", [D, S], BF16)
    X1d = dscr("X1d", [S, D], F32)

    b_QK = [Buf() for _ in range(NG)]
    b_V = [Buf() for _ in range(NT)]
    b_G = [Buf() for _ in range(NT)]
    b_Z = [Buf() for _ in range(NT)]
    b_DT = [Buf() for _ in range(NT)]
    b_UT = [Buf() for _ in range(NG)]
    b_MG = [Buf() for _ in range(NG)]
    b_OGT = [[Buf() for _ in range(NG)] for _ in range(4)]
    b_YNT = [Buf() for _ in range(NG)]
    b_X1 = [Buf() for _ in range(NT)]
    b_Y = [Buf() for _ in range(NT)]

    _uid = [0]

    def sb(name, shape, dt):
        _uid[0] += 1
        return es2.enter_context(nc.sbuf_tensor(f"sb_{name}_{_uid[0]}", list(shape), dt))

    ps = [es.enter_context(nc.psum_tensor(f"ps{i}", [128, 512], F32)) for i in range(8)]
    psb = [Buf(f"ps{i}", psum=True) for i in range(8)]

    def psbf(i):
        return ps[i][:].bitcast(BF16)

    es2 = es
    cst = sb("cst", [128, 4, 128], F32)
    identb = sb("identb", [128, 128], BF16)
    maskb = sb("maskb", [128, 128], BF16)
    cosT = sb("cosT", [128, NT, 8], F32)
    sinT = sb("sinT", [128, NT, 8], F32)
    cT2 = sb("cT2", [128, 8, 2], F32)
    cB = sb("cB", [128, 8, 128], F32)
    sm = sb("sm", [128, NS], F32)
    A_pk = sb("A_pk", [128, 8], F32)
    Sh_pk = sb("Sh_pk", [128, 8], F32)
    gate_b = sb("gate_b", [128, D], F32)
    b_cst, b_identb, b_maskb, b_cs, b_cT, b_sm, b_mod, b_gate = (Buf() for _ in range(8))
    negh = sb("negh", [128, 8], F32)
    b_negh = Buf()
    sc.op("dve", lambda: V.memset(negh[:], -0.5), writes=[b_negh])
    _G["negh"] = negh
    _G["b_negh"] = b_negh

    ident_f = cst[:, 0, :]
    tri_f = cst[:, 1, :]
    negm_f = cst[:, 2, :]
    ones_f = cst[:, 3, :]

    sc.dma("sp", cst[:], cst_in[:, :, :], writes=[b_cst])
    sc.dma("sp", cT2[:], cT2_in[:, :, :], writes=[b_cT])
    sc.dma("sp", cB[:], cB_in[:, :, :], writes=[b_cT])
    sc.op("dve", lambda: V.tensor_copy(out=identb[:], in_=ident_f), reads=[b_cst], writes=[b_identb])
    sc.op("dve", lambda: V.tensor_copy(out=maskb[:], in_=tri_f), reads=[b_cst], writes=[b_maskb])

    with contextlib.ExitStack() as es2:
        posi = sb("posi", [128, NT], I32)
        posf = sb("posf", [128, NT], F32)
        invf = sb("invf", [128, 8], F32)
        ang = sb("ang", [128, NT, 8], F32)
        kk = sb("kk", [128, NT, 8], F32)
        ki = sb("ki", [128, NT, 8], I32)
        r1 = sb("r1", [128, NT, 8], F32)
        r2 = sb("r2", [128, NT, 8], F32)
        bt = Buf()
        TWO_PI = 2.0 * math.pi
        sc.dma("sp", posi[:], pos_in[:, :], writes=[bt])
        sc.dma("sp", invf[:], invf_in[:, :], writes=[bt])
        sc.op("dve", lambda: V.tensor_copy(out=posf[:], in_=posi[:]), reads=[bt], writes=[bt])
        sc.op("dve", lambda: V.tensor_tensor(out=ang[:], in0=posf[:].unsqueeze(2).to_broadcast([128, NT, 8]),
                                             in1=invf[:].unsqueeze(1).to_broadcast([128, NT, 8]), op=ALU.mult),
              reads=[bt], writes=[bt])

        def reduce_to_pi(src, dst):
            sc.op("dve", lambda: V.tensor_scalar(out=kk[:], in0=src[:], scalar1=1.0 / TWO_PI, scalar2=None,
                                                 op0=ALU.mult), reads=[bt], writes=[bt])
            sc.op("dve", lambda: V.tensor_copy(out=ki[:], in_=kk[:]), reads=[bt], writes=[bt])
            sc.op("dve", lambda: V.tensor_copy(out=kk[:], in_=ki[:]), reads=[bt], writes=[bt])
            sc.op("dve", lambda: V.scalar_tensor_tensor(out=dst[:], in0=kk[:], scalar=-TWO_PI, in1=src[:],
                                                        op0=ALU.mult, op1=ALU.add), reads=[bt], writes=[bt])
            sc.op("dve", lambda: V.tensor_scalar(out=kk[:], in0=dst[:], scalar1=math.pi, scalar2=-TWO_PI,
                                                 op0=ALU.is_gt, op1=ALU.mult), reads=[bt], writes=[bt])
            sc.op("dve", lambda: V.tensor_tensor(out=dst[:], in0=dst[:], in1=kk[:], op=ALU.add),
                  reads=[bt], writes=[bt])
            sc.op("dve", lambda: V.tensor_scalar(out=kk[:], in0=dst[:], scalar1=-math.pi, scalar2=TWO_PI,
                                                 op0=ALU.is_lt, op1=ALU.mult), reads=[bt], writes=[bt])
            sc.op("dve", lambda: V.tensor_tensor(out=dst[:], in0=dst[:], in1=kk[:], op=ALU.add),
                  reads=[bt], writes=[bt])
            sc.op("dve", lambda: V.tensor_scalar(out=dst[:], in0=dst[:], scalar1=math.pi, scalar2=-math.pi,
                                                 op0=ALU.min, op1=ALU.max), reads=[bt], writes=[bt])

        if debug:
            dbg_dump(nc, sc, "invf", invf[:], [128, 8], F32, [bt])
            dbg_dump(nc, sc, "cB", cB[:], [128, 8, 128], F32, [b_cT])
            dbg_dump(nc, sc, "cT2", cT2[:], [128, 8, 2], F32, [b_cT])
            dbg_dump(nc, sc, "posf", posf[:], [128, NT], F32, [bt])
            dbg_dump(nc, sc, "ang", ang[:], [128, NT, 8], F32, [bt])
        reduce_to_pi(ang, r1)
        if debug:
            dbg_dump(nc, sc, "r1", r1[:], [128, NT, 8], F32, [bt])
            dbg_dump(nc, sc, "kk", kk[:], [128, NT, 8], F32, [bt])
        sc.op("act", lambda: A.activation(out=sinT[:], in_=r1[:], func=AF.Sin), reads=[bt], writes=[b_cs])
        sc.op("dve", lambda: V.tensor_scalar(out=r2[:], in0=r1[:], scalar1=math.pi / 2, scalar2=None, op0=ALU.add),
              reads=[bt], writes=[bt])
        reduce_to_pi(r2, r1)
        sc.op("act", lambda: A.activation(out=cosT[:], in_=r1[:], func=AF.Sin), reads=[bt], writes=[b_cs])
        _fence(nc, sc)

    def smv(name):
        a, b = SM[name]
        return sm[:, a:b]

    for l in range(n_layers):
        lambda_init = 0.8 - 0.6 * math.exp(-0.3 * l)
        last = (l == n_layers - 1)
        x_src = x_in if l == 0 else X1d

        sc.dma("sp", sm[:], sm_in[l, :, :], writes=[b_sm])
        with contextlib.ExitStack() as es2:
            wa = sb("wa", [128, 8, 3 * D], F32)
            modsb = sb("modsb", [128, 16], F32)
            b_wa = [Buf() for _ in range(8)]
            bm = Buf()
            for kc in range(8):
                sc.dma("sp", wa[:, kc, :], wada_in[l, kc * 128:(kc + 1) * 128, :], writes=[b_wa[kc]])
            for j in range(16):
                for kc in range(8):
                    sc.op("pe", lambda j=j, kc=kc: T.matmul(ps[0][:, 2 * j:2 * j + 2],
                                                            lhsT=wa[:, kc, j * 128:(j + 1) * 128],
                                                            rhs=cT2[:, kc, :], start=(kc == 0), stop=(kc == 7)),
                          reads=[b_wa[kc], b_cT], writes=[psb[0]])
            for n in range(2):
                for kc in range(8):
                    sc.op("pe", lambda n=n, kc=kc: T.matmul(ps[1 + n][:, :], lhsT=cB[:, kc, :],
                                                            rhs=wa[:, kc, 2 * D + n * 512:2 * D + (n + 1) * 512],
                                                            start=(kc == 0), stop=(kc == 7)),
                          reads=[b_wa[kc], b_cT], writes=[psb[1 + n]])
            sc.op("dve", lambda: V.tensor_tensor(out=modsb[:], in0=ps[0][:, 0:32].rearrange("p (j two) -> p j two", two=2)[:, :, 0], in1=smv("bada"), op=ALU.add),
                  reads=[psb[0], b_sm], writes=[bm])
            sc.op("dve", lambda: V.scalar_tensor_tensor(out=A_pk[:], in0=modsb[:, 8:16], scalar=1.0, in1=smv("nw"),
                                                        op0=ALU.add, op1=ALU.mult),
                  reads=[bm, b_sm], writes=[b_mod])
            sc.op("dve", lambda: V.tensor_copy(out=Sh_pk[:], in_=modsb[:, 0:8]), reads=[bm], writes=[b_mod])
            a0 = SM["bgate"][0]
            for n in range(2):
                sc.op("dve", lambda n=n: V.tensor_tensor(out=gate_b[:, n * 512:(n + 1) * 512], in0=ps[1 + n][:, :],
                                                         in1=sm[:, a0 + n * 512:a0 + (n + 1) * 512], op=ALU.add),
                      reads=[psb[1 + n], b_sm], writes=[b_gate])
            sc.wait_all("dve", [b_mod, b_gate])
            if debug and l == 0:
                dbg_dump(nc, sc, "A_pk", A_pk[:], [128, 8], F32, [b_mod])
                dbg_dump(nc, sc, "Sh_pk", Sh_pk[:], [128, 8], F32, [b_mod])
                dbg_dump(nc, sc, "gate_b", gate_b[:], [128, D], F32, [b_gate])
                dbg_dump(nc, sc, "cos", cosT[:], [128, NT, 8], F32, [b_cs])
                dbg_dump(nc, sc, "sin", sinT[:], [128, NT, 8], F32, [b_cs])
                dbg_dump(nc, sc, "modsb", modsb[:], [128, 16], F32, [bm])
            _fence(nc, sc)

        if "A" in phases:
          try:
            _phase_A(nc, sc, l, x_src, (b_X1 if l > 0 else None), win_in, ps, psb, psbf,
                     dict(QTd=QTd, KTd=KTd, Vd=Vd, Gd=Gd, Zd=Zd, DTd=DTd, UTd=UTd, MGd=MGd),
                     dict(QK=b_QK, V=b_V, G=b_G, Z=b_Z, DT=b_DT, UT=b_UT, MG=b_MG),
                     sm, b_sm, A_pk, Sh_pk, b_mod, identb, b_identb, cosT, sinT, b_cs)
          except _Stop:
            _fence(nc, sc)
        if INTERLEAVE_BC and "B" in phases and "C" in phases:
            _phase_BC(nc, sc, l, lambda_init, ps, psb, psbf, QTd, KTd, Vd, Gd, OGTd, b_QK, b_V, b_G, b_OGT,
                      UTd, DTd, Zd, YNTd, b_UT, b_DT, b_Z, b_YNT,
                      sm, b_sm, identb, b_identb, maskb, b_maskb, cst, b_cst)
        else:
            if "B" in phases:
                _phase_B(nc, sc, l, lambda_init, ps, psb, psbf, QTd, KTd, Vd, Gd, OGTd,
                         b_QK, b_V, b_G, b_OGT, sm, b_sm, identb, b_identb, maskb, b_maskb)
            if "C" in phases:
                _phase_C(nc, sc, l, ps, psb, psbf, UTd, DTd, Zd, YNTd, b_UT, b_DT, b_Z, b_YNT,
                         sm, b_sm, identb, b_identb, cst, b_cst)
        if "D" in phases:
            _phase_D(nc, sc, l, last, ps, psb, psbf, x_src, (b_X1 if l > 0 else None), OGTd, YNTd, MGd, X1d, y_out,
                     b_OGT, b_YNT, b_MG, b_X1, b_Y, watt_in, wssm_in, wout_in, gate_b, b_gate, fnw_in)

    allb = list(_DUMPS) + b_QK + b_V + b_G + b_Z + b_DT + b_UT + b_MG + b_YNT + b_X1 + b_Y
    for h in range(4):
        allb += b_OGT[h]
    sc.wait_all("sp", allb)
    for e in ("pe", "act", "dve", "pool"):
        if sc.cnt[e] > 0:
            sc._wait("sp", ("c", e, sc.cnt[e]))


def _phase_A(nc, sc, l, x_src, b_xsrc, win_in, ps, psb, psbf, dr, bd, sm, b_sm, A_pk, Sh_pk, b_mod,
             identb, b_identb, cosT, sinT, b_cs):
    V, A, T, P = nc.vector, nc.scalar, nc.tensor, nc.gpsimd
    with contextlib.ExitStack() as es2:
        def sb(name, shape, dt):
            return es2.enter_context(nc.sbuf_tensor(f"{name}_A{l}", list(shape), dt))

        Wb = sb("Wb", [128, 8, IN_COLS], BF16)
        b_W = [Buf() for _ in range(8)]
        with contextlib.ExitStack() as es3:
            wst = [es3.enter_context(nc.sbuf_tensor(f"wst{i}_A{l}", [128, IN_COLS], F32)) for i in range(2)]
            b_wst = [Buf(), Buf()]
            H1 = 3328
            for kc in range(8):
                i = kc % 2
                sc.dma("sp", wst[i][:], win_in[l, kc * 128:(kc + 1) * 128, :], writes=[b_wst[i]])
                sc.op("act", lambda kc=kc, i=i: A.copy(out=Wb[:, kc, 0:H1], in_=wst[i][:, 0:H1]),
                      reads=[b_wst[i]], writes=[b_W[kc]])
                sc.op("dve", lambda kc=kc, i=i: V.tensor_copy(out=Wb[:, kc, H1:IN_COLS], in_=wst[i][:, H1:IN_COLS]),
                      reads=[b_wst[i]], writes=[b_W[kc]])
            _fence(nc, sc)
        b_Wall = Buf()
        sc.op("act", lambda: A.copy(out=Wb[:, 0, 0:1], in_=Wb[:, 0, 0:1]), reads=b_W, writes=[b_Wall])
        sc.op("dve", lambda: V.tensor_copy(out=Wb[:, 0, 1:2], in_=Wb[:, 0, 1:2]), reads=b_W + [b_Wall],
              writes=[b_Wall])

        dbg_stop(1)
        xt = [sb(f"xt{i}", [128, D], F32) for i in range(2)]
        junk = sb("junk", [128, D], BF16)
        xn = [sb(f"xn{i}", [128, D], BF16) for i in range(2)]
        hT = [sb(f"hT{i}", [128, 8, 512], BF16) for i in range(2)]
        stat = [sb(f"stat{i}", [128, 4], F32) for i in range(2)]
        qk_sb = [sb(f"qksb{i}", [128, 1024], BF16) for i in range(2)]
        rp = [sb(f"rp{i}", [128, 4, 16, 8], F32) for i in range(2)]
        QKst = [sb("QKst0", [128, 8, 512], BF16)] * 2
        v_st = [sb(f"vst{i}", [128, 512], BF16) for i in range(2)]
        g_st = [sb(f"gst{i}", [128, 512], F32) for i in range(2)]
        z_st = [sb(f"zst{i}", [128, D], F32) for i in range(2)]
        dt4 = [sb(f"dt4{i}", [128, 4, 16], F32) for i in range(2)]
        dt4s = [sb(f"dt4s{i}", [128, 4, 16], F32) for i in range(2)]
        b_dt4, b_dt4s = [Buf(), Buf()], [Buf(), Buf()]
        fm_st = [sb(f"fmst{i}", [128, 512], F32) for i in range(3)]
        b_xt, b_xn, b_hT, b_stat, b_qk, b_rp, b_QKst_unused, b_vst, b_gst, b_zst, b_dtst, b_dttmp, b_hTa = (
            [Buf(), Buf()] for _ in range(13))
        b_junk = Buf()
        _bq = Buf()
        b_QKst = [_bq, _bq]
        b_fm = [Buf() for _ in range(3)]
        tm_banks = [2, 3, 6, 7]
        fm_banks = [4, 5]
        tmc = [0]
        fmc = [0]

        def a_sl(name):
            a, b = SM[name]
            return sm[:, a:b]

        for g in range(NGL):
            hb = g % 2
            for tt in range(4):
                t = 4 * g + tt
                i2 = t % 2
                rd = [b_xsrc[t]] if b_xsrc is not None else []
                sc.dma("sp", xt[i2][:], x_src[t * 128:(t + 1) * 128, :], reads=rd, writes=[b_xt[i2]])
                sc.op("dve", lambda i2=i2: V.scalar_tensor_tensor(out=junk[:], in0=xt[i2][:], scalar=1.0,
                                                                  in1=xt[i2][:], op0=ALU.mult, op1=ALU.mult,
                                                                  accum_out=stat[i2][:, 0:1]),
                      reads=[b_xt[i2]], writes=[b_junk, b_stat[i2]])
                emit_rstd(nc, sc, stat[i2][:, 2:3], stat[i2][:, 1:2], stat[i2][:, 0:1], 1.0 / D, 1,
                          [b_stat[i2]], [b_stat[i2]])
                sc.op("dve", lambda i2=i2: V.tensor_scalar(out=xn[i2][:], in0=xt[i2][:], scalar1=stat[i2][:, 2:3],
                                                           scalar2=None, op0=ALU.mult),
                      reads=[b_xt[i2], b_stat[i2]], writes=[b_xn[i2]])
                dbg_stop(11)
                pT = psbf(0)
                for kc in range(8):
                    sc.op("pe", lambda kc=kc, i2=i2: T.transpose(out=pT[:, kc * 128:(kc + 1) * 128],
                                                                 in_=xn[i2][:, kc * 128:(kc + 1) * 128],
                                                                 identity=identb[:]),
                          reads=[b_xn[i2], b_identb], writes=[psb[0]])
                dbg_stop(12)
                for kc in range(8):
                    eng = _os.environ.get("KDBG_EVAC") or ("dve" if kc % 2 == 0 else "act")
                    if eng == "dve":
                        sc.op("dve", lambda kc=kc, tt=tt, hb=hb: V.tensor_scalar(
                            out=hT[hb][:, kc, tt * 128:(tt + 1) * 128], in0=pT[:, kc * 128:(kc + 1) * 128],
                            scalar1=A_pk[:, kc:kc + 1], scalar2=Sh_pk[:, kc:kc + 1], op0=ALU.mult, op1=ALU.add),
                              reads=[psb[0], b_mod], writes=[b_hT[hb]])
                    else:
                        sc.op("act", lambda kc=kc, tt=tt, hb=hb: A.activation(
                            out=hT[hb][:, kc, tt * 128:(tt + 1) * 128], in_=pT[:, kc * 128:(kc + 1) * 128],
                            func=AF.Identity, scale=A_pk[:, kc:kc + 1], bias=Sh_pk[:, kc:kc + 1]),
                              reads=[psb[0], b_mod], writes=[b_hTa[hb]])

                dbg_stop(2)

                def tm_mm(c0, ncols):
                    bk = tm_banks[tmc[0] % 4]
                    tmc[0] += 1
                    for kc in range(8):
                        sc.op("pe", lambda kc=kc, bk=bk: T.matmul(ps[bk][:, 0:ncols],
                                                                  lhsT=hT[hb][:, kc, tt * 128:(tt + 1) * 128],
                                                                  rhs=Wb[:, kc, c0:c0 + ncols],
                                                                  start=(kc == 0), stop=(kc == 7)),
                              reads=[b_hT[hb], b_hTa[hb], b_Wall], writes=[psb[bk]])
                    return bk

                for half in range(2):
                    bk = tm_mm(half * 512, 512)
                    pv = ps[bk][:, :].rearrange("p (s d) -> p s d", d=64)
                    qv = qk_sb[i2][:, half * 512:(half + 1) * 512].rearrange("p (s d) -> p s d", d=64)
                    cb_ = cosT[:, t, :].unsqueeze(1).to_broadcast([128, 8, 8])
                    sb_ = sinT[:, t, :].unsqueeze(1).to_broadcast([128, 8, 8])
                    r = rp[i2]
                    o8 = half * 8
                    t1 = pv[:, :, 0:8]
                    t2 = pv[:, :, 8:16]
                    sc.op("dve", lambda: V.tensor_tensor(out=r[:, 0, o8:o8 + 8, :], in0=t1, in1=cb_, op=ALU.mult),
                          reads=[psb[bk], b_cs], writes=[b_rp[i2]])
                    sc.op("dve", lambda: V.tensor_tensor(out=r[:, 1, o8:o8 + 8, :], in0=t2, in1=sb_, op=ALU.mult),
                          reads=[psb[bk], b_cs], writes=[b_rp[i2]])
                    sc.op("dve", lambda: V.tensor_tensor(out=r[:, 2, o8:o8 + 8, :], in0=t2, in1=cb_, op=ALU.mult),
                          reads=[psb[bk], b_cs], writes=[b_rp[i2]])
                    sc.op("dve", lambda: V.tensor_tensor(out=r[:, 3, o8:o8 + 8, :], in0=t1, in1=sb_, op=ALU.mult),
                          reads=[psb[bk], b_cs], writes=[b_rp[i2]])
                    sc.op("dve", lambda: V.tensor_tensor(out=qv[:, :, 0:8], in0=r[:, 0, o8:o8 + 8, :],
                                                         in1=r[:, 1, o8:o8 + 8, :], op=ALU.subtract),
                          reads=[b_rp[i2]], writes=[b_qk[i2]])
                    sc.op("dve", lambda: V.tensor_tensor(out=qv[:, :, 8:16], in0=r[:, 2, o8:o8 + 8, :],
                                                         in1=r[:, 3, o8:o8 + 8, :], op=ALU.add),
                          reads=[b_rp[i2]], writes=[b_qk[i2]])
                    sc.op("act", lambda: A.copy(out=qv[:, :, 16:64], in_=pv[:, :, 16:64]),
                          reads=[psb[bk]], writes=[b_qk[i2]])
                    sc.op("dve", lambda: V.tensor_copy(out=r[:, 0, o8, 0:1], in_=r[:, 0, o8, 1:2]),
                          reads=[b_qk[i2], b_rp[i2]], writes=[b_qk[i2], b_rp[i2]])
                dbg_stop(4)
                bk = tm_mm(O_V, 512)
                sc.op("act", lambda bk=bk, i2=i2: A.copy(out=v_st[i2][:], in_=ps[bk][:, :]),
                      reads=[psb[bk]], writes=[b_vst[i2]])
                sc.dma("pool", dr["Vd"][t * 128:(t + 1) * 128, :], v_st[i2][:], reads=[b_vst[i2]], writes=[bd["V"][t]])
                dbg_stop(5)
                bk = tm_mm(O_G, 512)
                sc.op("act", lambda bk=bk, i2=i2: A.activation(out=g_st[i2][:], in_=ps[bk][:, :], func=AF.Silu),
                      reads=[psb[bk]], writes=[b_gst[i2]])
                sc.dma("pool", dr["Gd"][t * 128:(t + 1) * 128, :], g_st[i2][:], reads=[b_gst[i2]], writes=[bd["G"][t]])
                for half in range(2):
                    bk = tm_mm(O_Z + half * 512, 512)
                    sc.op("act", lambda bk=bk, i2=i2, half=half: A.activation(
                        out=z_st[i2][:, half * 512:(half + 1) * 512], in_=ps[bk][:, :], func=AF.Silu),
                          reads=[psb[bk]], writes=[b_zst[i2]])
                sc.dma("pool", dr["Zd"][t * 128:(t + 1) * 128, :], z_st[i2][:], reads=[b_zst[i2]], writes=[bd["Z"][t]])
                dbg_stop(6)
                bk = tm_mm(O_DT, 16)
                sc.op("dve", lambda bk=bk: V.tensor_tensor(out=dt4[hb][:, tt, :], in0=ps[bk][:, 0:16],
                                                           in1=a_sl("dtb"), op=ALU.add),
                      reads=[psb[bk], b_sm], writes=[b_dt4[hb]])
                dbg_stop(3)
                pT2 = psbf(1)
                for c in range(8):
                    sc.op("pe", lambda c=c, i2=i2: T.transpose(out=pT2[:, c * 128:(c + 1) * 128],
                                                               in_=qk_sb[i2][:, c * 128:(c + 1) * 128],
                                                               identity=identb[:]),
                          reads=[b_qk[i2], b_identb], writes=[psb[1]])
                sc.op("act", lambda hb=hb, tt=tt: A.copy(
                    out=QKst[hb][:, :, tt * 128:(tt + 1) * 128],
                    in_=pT2[:, :].rearrange("p (c s) -> p c s", c=8)),
                      reads=[psb[1]], writes=[b_QKst[hb]])
                dbg_stop(7)
            sc.op("act", lambda: A.activation(out=dt4[hb][:], in_=dt4[hb][:], func=AF.Exp),
                  reads=[b_dt4[hb]], writes=[b_dt4[hb]])
            sc.op("act", lambda: A.activation(out=dt4s[hb][:], in_=dt4[hb][:], func=AF.Ln, bias=1.0),
                  reads=[b_dt4[hb]], writes=[b_dt4s[hb]])
            sc.dma("pool", dr["DTd"][g * 512:(g + 1) * 512, :].rearrange("(i p) e -> p i e", p=128), dt4s[hb][:],
                   reads=[b_dt4s[hb]], writes=[bd["DT"][4 * g + i] for i in range(4)])
            sc.dma("pool", dr["QTd"][:, :, g * 512:(g + 1) * 512].rearrange("h p s -> p h s"), QKst[hb][:, 0:4, :],
                   reads=[b_QKst[hb]], writes=[bd["QK"][g]])
            sc.dma("pool", dr["KTd"][:, :, g * 512:(g + 1) * 512].rearrange("h p s -> p h s"), QKst[hb][:, 4:8, :],
                   reads=[b_QKst[hb]], writes=[bd["QK"][g]])
            dbg_stop(8)
            for cc in range(28):
                if cc < 12:
                    c0 = O_X + cc * 128
                else:
                    c0 = O_MA + (cc - 12) * 128
                bk = fm_banks[fmc[0] % 2]
                fi = fmc[0] % 3
                fmc[0] += 1
                for kc in range(8):
                    sc.op("pe", lambda kc=kc, bk=bk, c0=c0: T.matmul(ps[bk][:, :], lhsT=Wb[:, kc, c0:c0 + 128],
                                                                     rhs=hT[hb][:, kc, :],
                                                                     start=(kc == 0), stop=(kc == 7)),
                          reads=[b_hT[hb], b_hTa[hb], b_Wall], writes=[psb[bk]])
                if cc < 12:
                    sc.op("dve", lambda bk=bk, fi=fi: V.tensor_copy(out=fm_st[fi][:], in_=ps[bk][:, :]),
                          reads=[psb[bk]], writes=[b_fm[fi]])
                    sc.dma("pool", dr["UTd"][cc * 128:(cc + 1) * 128, g * 512:(g + 1) * 512], fm_st[fi][:],
                           reads=[b_fm[fi]], writes=[bd["UT"][g]] if cc == 11 else [])
                else:
                    sc.op("act", lambda bk=bk, fi=fi: A.activation(out=fm_st[fi][:], in_=ps[bk][:, :],
                                                                   func=AF.Sigmoid),
                          reads=[psb[bk]], writes=[b_fm[fi]])
                    sc.dma("pool", dr["MGd"][(cc - 12) * 128:(cc - 11) * 128, g * 512:(g + 1) * 512], fm_st[fi][:],
                           reads=[b_fm[fi]], writes=[bd["MG"][g]] if cc == 27 else [])
        _fence(nc, sc)


def _fence(nc, sc):
    _DEAD[0] = False
    for E in ("pe", "act", "dve", "pool", "sp"):
        for e in ("pe", "act", "dve", "pool"):
            if sc.cnt[e] > 0 and not (E == "pe" and e == "pe"):
                sc._wait(E, ("c", e, sc.cnt[e]))
        for q in ("sp", "pool"):
            n = sc.dcnt[q]
            for k in range(max(0, n - sc.NQ), n):
                sc._wait(E, ("d", q, k))


def _phase_B(nc, sc, l, lambda_init, ps, psb, psbf, QTd, KTd, Vd, Gd, OGTd, b_QK, b_V, b_G, b_OGT,
             sm, b_sm, identb, b_identb, maskb, b_maskb):
    V, A, T, P = nc.vector, nc.scalar, nc.tensor, nc.gpsimd
    with contextlib.ExitStack() as es2:
        def sb(name, shape, dt):
            return es2.enter_context(nc.sbuf_tensor(f"{name}_B{l}", list(shape), dt))

        KT = [sb(f"KT{i}", [128, S], BF16) for i in range(2)]
        Va = [sb(f"Va{i}", [128, NT, 130], BF16) for i in range(2)]
        QZ = [[sb(f"QZ{i}_{t}", [128, 512], BF16) for t in range(2)] for i in range(2)]
        Et = [sb(f"E{i}", [128, 512], BF16) for i in range(4)]
        Gt = [sb(f"Gt{i}", [128, 4, 128], F32) for i in range(2)]
        gw = [sb(f"gw{i}", [128, 4, 128], F32) for i in range(2)]
        o1 = sb("o1", [128, 4, 128], F32)
        od = [sb(f"od{i}", [128, 128], F32) for i in range(2)]
        junk = sb("junk", [128, 128], F32)
        st = [sb(f"st{i}", [128, 8], F32) for i in range(2)]
        ogb = [sb(f"ogb{i}", [128, 128], BF16) for i in range(2)]
        ogT = [sb(f"ogT{i}", [128, 512], BF16) for i in range(2)]
        wsub = sb("wsub", [128, 128], F32)
        lam = sb("lam", [128, 8], F32)
        ljunk = sb("ljunk", [128, 64], F32)
        b_KT, b_Va, b_QT_unused, b_Gt, b_gw, b_od, b_st, b_ogb, b_ogT = ([Buf(), Buf()] for _ in range(9))
        b_E = [Buf() for _ in range(4)]
        b_o1, b_junk, b_wsub, b_lam = Buf(), Buf(), Buf(), Buf()
        b_QZ = [[Buf(), Buf()], [Buf(), Buf()]]
        for i in range(2):
            sc.op("dve", lambda i=i: V.memset(QZ[i][0][64:128, :], 0.0), writes=[b_QZ[i][0]])
            sc.op("dve", lambda i=i: V.memset(QZ[i][1][0:64, :], 0.0), writes=[b_QZ[i][1]])

        def smv(name):
            a, b = SM[name]
            return sm[:, a:b]

        sc.op("dve", lambda: V.scalar_tensor_tensor(out=ljunk[:], in0=smv("lq1"), scalar=1.0, in1=smv("lk1"),
                                                    op0=ALU.mult, op1=ALU.mult, accum_out=lam[:, 0:1]),
              reads=[b_sm], writes=[b_lam])
        sc.op("dve", lambda: V.scalar_tensor_tensor(out=ljunk[:], in0=smv("lq2"), scalar=1.0, in1=smv("lk2"),
                                                    op0=ALU.mult, op1=ALU.mult, accum_out=lam[:, 1:2]),
              reads=[b_sm, b_lam], writes=[b_lam])
        sc.op("act", lambda: A.activation(out=lam[:, 2:4], in_=lam[:, 0:2], func=AF.Exp), reads=[b_lam],
              writes=[b_lam])
        sc.op("dve", lambda: V.tensor_tensor(out=lam[:, 4:5], in0=lam[:, 2:3], in1=lam[:, 3:4], op=ALU.subtract),
              reads=[b_lam], writes=[b_lam])
        sc.op("dve", lambda: V.tensor_scalar(out=lam[:, 5:6], in0=lam[:, 4:5], scalar1=lambda_init, scalar2=-1.0,
                                             op0=ALU.add, op1=ALU.mult), reads=[b_lam], writes=[b_lam])
        sc.op("dve", lambda: V.tensor_scalar(out=wsub[:], in0=smv("subw"), scalar1=(1.0 - lambda_init), scalar2=None,
                                             op0=ALU.mult), reads=[b_sm], writes=[b_wsub])
        for i in range(2):
            sc.op("dve", lambda i=i: V.memset(Va[i][:, :, 128:130], 1.0), writes=[b_Va[i]])

        sbanks = [0, 1, 7]
        obanks = [2, 3, 4, 5]
        scnt = [0]
        for h in range(4):
            hb = h % 2
            sc.dma("sp", KT[hb][:], KTd[h, :, :], reads=b_QK, writes=[b_KT[hb]])
            for q4 in range(4):
                sc.dma("sp", Va[hb][:, q4 * 16:(q4 + 1) * 16, 0:128],
                       Vd[q4 * 2048:(q4 + 1) * 2048, h * 128:(h + 1) * 128].rearrange("(t p) e -> p t e", p=128),
                       reads=b_V[q4 * 16:(q4 + 1) * 16], writes=[b_Va[hb]])
            for qg in range(NG):
                qb = qg % 2
                sc.dma("sp", QZ[qb][0][0:64, :], QTd[h, 0:64, qg * 512:(qg + 1) * 512], reads=[b_QK[qg]],
                       writes=[b_QZ[qb][0]])
                sc.dma("sp", QZ[qb][1][64:128, :], QTd[h, 64:128, qg * 512:(qg + 1) * 512], reads=[b_QK[qg]],
                       writes=[b_QZ[qb][1]])
                sc.dma("sp", Gt[qb][:], Gd[qg * 512:(qg + 1) * 512, h * 128:(h + 1) * 128].rearrange(
                    "(i p) e -> p i e", p=128), reads=b_G[qg * 4:qg * 4 + 4], writes=[b_Gt[qb]])
                sc.op("pool", lambda qb=qb: P.tensor_tensor(out=gw[qb][:], in0=Gt[qb][:],
                                                            in1=wsub[:].unsqueeze(1).to_broadcast([128, 4, 128]),
                                                            op=ALU.mult),
                      reads=[b_Gt[qb], b_wsub], writes=[b_gw[qb]])
                for t in range(2):
                    nj = 4 * qg + 4
                    slots = {}

                    def emit_score(j, t=t, qg=qg, qb=qb, hb=hb):
                        c0 = max(0, j - 4 * qg) * 128
                        sbk = sbanks[scnt[0] % 3]
                        ei = scnt[0] % 4
                        scnt[0] += 1
                        slots[j] = (c0, ei)
                        sc.op("pe", lambda: T.matmul(
                            ps[sbk][:, c0:512], lhsT=KT[hb][:, j * 128:(j + 1) * 128],
                            rhs=QZ[qb][t][:, c0:512], start=True, stop=True),
                              reads=[b_KT[hb], b_QZ[qb][t]], writes=[psb[sbk]])
                        sc.op("act", lambda: A.activation(
                            out=Et[ei][:, c0:512], in_=ps[sbk][:, c0:512], func=AF.Exp, scale=0.125),
                              reads=[psb[sbk]], writes=[b_E[ei]])
                        if j >= 4 * qg:
                            sc.op("dve", lambda: V.tensor_tensor(
                                out=Et[ei][:, c0:c0 + 128], in0=Et[ei][:, c0:c0 + 128], in1=maskb[:], op=ALU.mult),
                                  reads=[b_E[ei], b_maskb], writes=[b_E[ei]])

                    def emit_pv(j, qg=qg, hb=hb):
                        c0, ei = slots[j]
                        for i in range(max(j, 4 * qg), 4 * qg + 4):
                            ii = i - 4 * qg
                            ob = obanks[ii]
                            sc.op("pe", lambda: T.matmul(
                                ps[ob][:, 0:129], lhsT=Et[ei][:, ii * 128:(ii + 1) * 128],
                                rhs=Va[hb][:, j, 0:129], start=(j == 0), stop=(j == i)),
                                  reads=[b_E[ei], b_Va[hb]], writes=[psb[ob]])

                    LOOK = 2
                    for j in range(min(LOOK, nj)):
                        emit_score(j)
                    for j in range(nj):
                        if j + LOOK < nj:
                            emit_score(j + LOOK)
                        emit_pv(j)
                    for ii in range(4):
                        ob = obanks[ii]
                        s2 = st[ii % 2]
                        bs2 = b_st[ii % 2]
                        if t == 0:
                            sc.op("dve", lambda ob=ob, s2=s2: V.reciprocal(out=s2[:, 0:1], in_=ps[ob][:, 128:129]),
                                  reads=[psb[ob]], writes=[bs2])
                            sc.op("dve", lambda ob=ob, s2=s2, ii=ii: V.tensor_scalar(
                                out=o1[:, ii, :], in0=ps[ob][:, 0:128], scalar1=s2[:, 0:1], scalar2=None,
                                op0=ALU.mult), reads=[psb[ob], bs2], writes=[b_o1])
                        else:
                            oi = ii % 2
                            sc.op("dve", lambda ob=ob, s2=s2: V.reciprocal(out=s2[:, 0:1], in_=ps[ob][:, 128:129]),
                                  reads=[psb[ob]], writes=[bs2])
                            sc.op("dve", lambda s2=s2: V.tensor_tensor(out=s2[:, 1:2], in0=s2[:, 0:1],
                                                                       in1=lam[:, 5:6], op=ALU.mult),
                                  reads=[bs2, b_lam], writes=[bs2])
                            sc.op("dve", lambda ob=ob, s2=s2, ii=ii, oi=oi: V.scalar_tensor_tensor(
                                out=od[oi][:], in0=ps[ob][:, 0:128], scalar=s2[:, 1:2], in1=o1[:, ii, :],
                                op0=ALU.mult, op1=ALU.add), reads=[psb[ob], bs2, b_o1], writes=[b_od[oi]])
                            sc.op("dve", lambda s2=s2, oi=oi: V.scalar_tensor_tensor(
                                out=junk[:], in0=od[oi][:], scalar=1.0, in1=od[oi][:], op0=ALU.mult, op1=ALU.mult,
                                accum_out=s2[:, 2:3]), reads=[b_od[oi]], writes=[b_junk, bs2])
                            emit_rstd(nc, sc, s2[:, 4:5], s2[:, 3:4], s2[:, 2:3], 1.0 / 128, 1, [bs2], [bs2])
                            sc.op("dve", lambda s2=s2, oi=oi, ii=ii, qb=qb: V.scalar_tensor_tensor(
                                out=ogb[oi][:], in0=od[oi][:], scalar=s2[:, 4:5], in1=gw[qb][:, ii, :],
                                op0=ALU.mult, op1=ALU.mult), reads=[b_od[oi], bs2, b_gw[qb]], writes=[b_ogb[oi]])
                            pT = psbf(6)
                            sc.op("pe", lambda oi=oi, ii=ii: T.transpose(out=pT[:, ii * 128:(ii + 1) * 128],
                                                                         in_=ogb[oi][:], identity=identb[:]),
                                  reads=[b_ogb[oi], b_identb], writes=[psb[6]])
                    if t == 1:
                        pT = psbf(6)
                        sc.op("act", lambda qb=qb: A.copy(out=ogT[qb][:], in_=pT[:, 0:512]),
                              reads=[psb[6]], writes=[b_ogT[qb]])
                        sc.dma("pool", OGTd[h * 128:(h + 1) * 128, qg * 512:(qg + 1) * 512], ogT[qb][:],
                               reads=[b_ogT[qb]], writes=[b_OGT[h][qg]])
        _fence(nc, sc)


def _phase_C(nc, sc, l, ps, psb, psbf, UTd, DTd, Zd, YNTd, b_UT, b_DT, b_Z, b_YNT, sm, b_sm,
             identb, b_identb, cst, b_cst):
    V, A, T, P = nc.vector, nc.scalar, nc.tensor, nc.gpsimd
    ident_f = cst[:, 0, :]
    tri_f = cst[:, 1, :]
    negm_f = cst[:, 2, :]
    ones_f = cst[:, 3, :]
    with contextlib.ExitStack() as es2:
        def sb(name, shape, dt):
            return es2.enter_context(nc.sbuf_tensor(f"{name}_C{l}", list(shape), dt))

        U = [sb(f"U{i}", [128, 12, 515], F32) for i in range(2)]
        acc = [sb(f"acc{i}", [128, 512], F32) for i in range(2)]
        XC = [sb(f"XC{i}", [128, 12, 512], BF16) for i in range(2)]
        DTg = [sb(f"DTg{i}", [128, 4, 16], F32) for i in range(2)]
        Zt = [sb(f"Zt{i}", [128, D], F32) for i in range(2)]
        xd = [sb(f"xd{i}", [128, D], BF16) for i in range(2)]
        xdd = [sb(f"xdd{i}", [128, D], BF16) for i in range(2)]
        xsk = [sb(f"xsk{i}", [128, D], F32) for i in range(2)]
        Btm = [sb(f"Btm{i}", [128, 256], BF16) for i in range(2)]
        sml = [sb(f"sml{i}", [128, 8, 4, 16], F32) for i in range(2)]
        LT = [sb(f"LT{i}", [128, 128], F32) for i in range(8)]
        MT = [sb(f"MT{i}", [128, 128], BF16) for i in range(8)]
        t1 = [sb(f"t1{i}", [128, 512], F32) for i in range(2)]
        yy = [sb(f"yy{i}", [128, D], F32) for i in range(2)]
        Hs = sb("Hs", [128, 2, 512], F32)
        Hb = sb("Hb", [128, 2, 512], BF16)
        a_row = sb("a_row", [128, 16], F32)
        nst = [sb(f"nst{i}", [128, 8], F32) for i in range(2)]
        junk = sb("junk", [128, 512], F32)
        ynb = [sb(f"ynb{i}", [128, D], BF16) for i in range(2)]
        YNst = [sb(f"YNst{i}", [128, 8, 512], BF16) for i in range(2)]
        (b_U, b_acc, b_XC, b_DTg, b_Zt, b_xd, b_xdd, b_xsk, b_Btm, b_sml, b_t1, b_yy, b_nst, b_ynb,
         b_YNst) = ([Buf(), Buf()] for _ in range(15))
        b_LT = [Buf() for _ in range(8)]
        b_MT = [Buf() for _ in range(8)]
        b_Hs, b_Hb, b_arow, b_junk = Buf(), Buf(), Buf(), Buf()

        def smv(name):
            a, b = SM[name]
            return sm[:, a:b]

        sc.op("act", lambda: A.activation(out=a_row[:], in_=smv("alog"), func=AF.Exp), reads=[b_sm], writes=[b_arow])
        sc.op("dve", lambda: V.tensor_scalar(out=a_row[:], in0=a_row[:], scalar1=-1.0, scalar2=None, op0=ALU.mult),
              reads=[b_arow], writes=[b_arow])
        sc.op("dve", lambda: V.memset(Hs[:], 0.0), writes=[b_Hs])
        sc.op("dve", lambda: V.memset(Hb[:], 0.0), writes=[b_Hb])
        sc.op("dve", lambda: V.memset(U[0][:, :, 0:3], 0.0), writes=[b_U[0]])

        cw0 = SM["cw"][0]
        cb0 = SM["cb"][0]

        def load_group(g):
            ub = g % 2
            if g == 0:
                sc.dma("sp", U[ub][:, :, 3:515], UTd[:, 0:512].rearrange("(c p) s -> p c s", p=128),
                       reads=[b_UT[0]], writes=[b_U[ub]])
            else:
                sc.dma("sp", U[ub][:, :, :], UTd[:, g * 512 - 3:g * 512 + 512].rearrange("(c p) s -> p c s", p=128),
                       reads=[b_UT[g - 1], b_UT[g]], writes=[b_U[ub]])
            sc.dma("sp", DTg[ub][:], DTd[g * 512:(g + 1) * 512, :].rearrange("(i p) e -> p i e", p=128),
                   reads=b_DT[4 * g:4 * g + 4], writes=[b_DTg[ub]])

        def conv_piece(g, ccs):
            ub = g % 2
            for cc in ccs:
                ab = cc % 2
                w = lambda k, cc=cc: sm[:, cw0 + cc * 4 + k:cw0 + cc * 4 + k + 1]
                sc.op("dve", lambda: V.tensor_scalar(
                    out=acc[ab][:], in0=U[ub][:, cc, 3:515], scalar1=w(3), scalar2=sm[:, cb0 + cc:cb0 + cc + 1],
                    op0=ALU.mult, op1=ALU.add), reads=[b_U[ub], b_sm], writes=[b_acc[ab]])
                for k in (2, 1, 0):
                    sc.op("dve", lambda: V.scalar_tensor_tensor(
                        out=acc[ab][:], in0=U[ub][:, cc, k:k + 512], scalar=w(k), in1=acc[ab][:],
                        op0=ALU.mult, op1=ALU.add), reads=[b_U[ub], b_sm, b_acc[ab]], writes=[b_acc[ab]])
                sc.op("act", lambda: A.activation(out=XC[ub][:, cc, :], in_=acc[ab][:], func=AF.Silu),
                      reads=[b_acc[ab]], writes=[b_XC[ub]])

        load_group(0)
        conv_piece(0, range(12))
        for g in range(NG):
            ub = g % 2
            if g + 1 < NG:
                load_group(g + 1)
            s_ = sml[ub]
            bs = b_sml[ub]
            row = lambda r: s_[:, r, :, :]
            row2 = lambda r: s_[:, r, :, :].rearrange("p t h -> p (t h)")
            sc.op("dve", lambda: V.tensor_tensor(out=row(0), in0=DTg[ub][:],
                                                 in1=a_row[:].unsqueeze(1).to_broadcast([128, 4, 16]), op=ALU.mult),
                  reads=[b_DTg[ub], b_arow], writes=[bs])
            sc.op("pe", lambda: T.matmul(ps[1][:, 256:320], lhsT=tri_f, rhs=row2(0), start=True, stop=True),
                  reads=[bs, b_cst], writes=[psb[1]])
            sc.op("pe", lambda: T.matmul(ps[1][:, 320:384], lhsT=ones_f, rhs=row2(0), start=True, stop=True),
                  reads=[bs, b_cst], writes=[psb[1]])
            sc.op("dve", lambda: V.tensor_copy(out=row2(1), in_=ps[1][:, 256:320]), reads=[psb[1]], writes=[bs])
            sc.op("dve", lambda: V.tensor_scalar(out=row2(2), in0=ps[1][:, 256:320], scalar1=-1.0,
                                                 scalar2=None, op0=ALU.mult), reads=[psb[1]], writes=[bs])
            sc.op("dve", lambda: V.tensor_tensor(out=row2(3), in0=ps[1][:, 320:384], in1=row2(1),
                                                 op=ALU.subtract), reads=[psb[1], bs], writes=[bs])
            sc.op("act", lambda: A.activation(out=row2(4), in_=row2(3), func=AF.Exp), reads=[bs], writes=[bs])
            sc.op("act", lambda: A.activation(out=row2(5), in_=row2(1), func=AF.Exp), reads=[bs], writes=[bs])
            sc.op("act", lambda: A.activation(out=row2(6), in_=ps[1][:, 320:384], func=AF.Exp),
                  reads=[psb[1], bs], writes=[bs])
            sc.op("dve", lambda: V.tensor_copy(out=s_[:, 7, 0, 0:1], in_=s_[:, 6, 0, 0:1]), reads=[bs], writes=[bs])

            for tt in range(4):
                c = 4 * g + tt
                cb = c % 2
                tsl = slice(tt * 128, (tt + 1) * 128)
                sc.dma("sp", Zt[cb][:], Zd[c * 128:(c + 1) * 128, :], reads=[b_Z[c]], writes=[b_Zt[cb]])
                pX = psbf(0)
                for cc in range(8):
                    sc.op("pe", lambda cc=cc: T.transpose(out=pX[:, cc * 128:(cc + 1) * 128], in_=XC[ub][:, cc, tsl],
                                                          identity=identb[:]),
                          reads=[b_XC[ub], b_identb], writes=[psb[0]])
                pB = psbf(1)
                for k in range(2):
                    sc.op("pe", lambda k=k: T.transpose(out=pB[:, k * 128:(k + 1) * 128], in_=XC[ub][:, 8 + k, tsl],
                                                        identity=identb[:]),
                          reads=[b_XC[ub], b_identb], writes=[psb[1]])
                pX3 = pX[:, :].rearrange("p (h e) -> p h e", e=64)
                bc = lambda r: s_[:, r, tt, :].unsqueeze(2).to_broadcast([128, 16, 64])
                sc.op("dve", lambda: V.tensor_tensor(out=xd[cb][:].rearrange("p (h e) -> p h e", e=64), in0=pX3,
                                                     in1=DTg[ub][:, tt, :].unsqueeze(2).to_broadcast([128, 16, 64]),
                                                     op=ALU.mult), reads=[psb[0], b_DTg[ub]], writes=[b_xd[cb]])
                sc.op("dve", lambda: V.tensor_tensor(out=xdd[cb][:].rearrange("p (h e) -> p h e", e=64),
                                                     in0=xd[cb][:].rearrange("p (h e) -> p h e", e=64), in1=bc(4),
                                                     op=ALU.mult), reads=[b_xd[cb], bs], writes=[b_xdd[cb]])
                sc.op("dve", lambda: V.tensor_tensor(out=xsk[cb][:].rearrange("p (h e) -> p h e", e=64), in0=pX3,
                                                     in1=smv("dsk").unsqueeze(2).to_broadcast([128, 16, 64]),
                                                     op=ALU.mult), reads=[psb[0], b_sm], writes=[b_xsk[cb]])
                sc.op("act", lambda: A.copy(out=Btm[cb][:], in_=pB[:, 0:256]), reads=[psb[1]], writes=[b_Btm[cb]])
                for gg in range(2):
                    gs = slice(gg * 512, (gg + 1) * 512)
                    sc.op("pe", lambda: T.matmul(ps[2][:, 0:128], lhsT=XC[ub][:, 8 + gg, tsl],
                                                 rhs=XC[ub][:, 10 + gg, tsl], start=True, stop=True),
                          reads=[b_XC[ub]], writes=[psb[2]])
                    sc.op("pe", lambda: T.matmul(ps[3][:, :], lhsT=XC[ub][:, 10 + gg, tsl], rhs=Hb[:, gg, :],
                                                 start=True, stop=True),
                          reads=[b_XC[ub], b_Hb], writes=[psb[3]])
                    sc.op("pe", lambda: T.matmul(ps[4][:, :], lhsT=Btm[cb][:, gg * 128:(gg + 1) * 128],
                                                 rhs=xdd[cb][:, gs], start=True, stop=True),
                          reads=[b_Btm[cb], b_xdd[cb]], writes=[psb[4]])
                    for j in range(8):
                        hh = gg * 8 + j
                        pb = 6 + j // 4
                        cs = slice((j % 4) * 128, (j % 4 + 1) * 128)
                        sc.op("pe", lambda: T.matmul(
                            ps[pb][:, cs], lhsT=s_[:, 0, tt, hh:hh + 1].to_broadcast([128, 128]), rhs=tri_f,
                            start=True, stop=False), reads=[bs, b_cst], writes=[psb[pb]])
                        sc.op("pe", lambda: T.matmul(ps[pb][:, cs], lhsT=ident_f, rhs=negm_f,
                                                     start=False, stop=True),
                              reads=[b_cst], writes=[psb[pb]])
                    for j in range(8):
                        hh = gg * 8 + j
                        pb = 6 + j // 4
                        cs = slice((j % 4) * 128, (j % 4 + 1) * 128)
                        sc.op("act", lambda: A.activation(
                            out=LT[j][:], in_=ps[pb][:, cs], func=AF.Exp, bias=s_[:, 2, tt, hh:hh + 1], scale=1.0),
                              reads=[psb[pb], bs], writes=[b_LT[j]])
                        sc.op("dve", lambda: V.tensor_tensor(out=MT[j][:], in0=LT[j][:], in1=ps[2][:, 0:128],
                                                             op=ALU.mult),
                              reads=[b_LT[j], psb[2]], writes=[b_MT[j]])
                    for j in range(8):
                        hh = gg * 8 + j
                        sc.op("pe", lambda: T.matmul(
                            ps[5][:, j * 64:(j + 1) * 64], lhsT=MT[j][:], rhs=xd[cb][:, hh * 64:(hh + 1) * 64],
                            start=True, stop=True), reads=[b_MT[j], b_xd[cb]], writes=[psb[5]])
                    tb = gg
                    sc.op("dve", lambda: V.tensor_tensor(
                        out=t1[tb][:].rearrange("p (h e) -> p h e", e=64),
                        in0=ps[3][:, :].rearrange("p (h e) -> p h e", e=64),
                        in1=s_[:, 5, tt, gg * 8:(gg + 1) * 8].unsqueeze(2).to_broadcast([128, 8, 64]), op=ALU.mult),
                          reads=[psb[3], bs], writes=[b_t1[tb]])
                    sc.op("dve", lambda: V.tensor_tensor(out=t1[tb][:], in0=ps[5][:, :], in1=t1[tb][:],
                                                         op=ALU.add),
                          reads=[psb[5], b_t1[tb]], writes=[b_t1[tb]])
                    sc.op("pool", lambda: P.tensor_tensor(out=yy[cb][:, gs], in0=t1[tb][:],
                                                          in1=xsk[cb][:, gs], op=ALU.add),
                          reads=[b_t1[tb], b_xsk[cb]], writes=[b_yy[cb]])
                    sc.op("dve", lambda: V.tensor_tensor(
                        out=Hs[:, gg, :].rearrange("p (h e) -> p h e", e=64),
                        in0=Hs[:, gg, :].rearrange("p (h e) -> p h e", e=64),
                        in1=s_[:, 6, tt, gg * 8:(gg + 1) * 8].unsqueeze(2).to_broadcast([128, 8, 64]), op=ALU.mult),
                          reads=[b_Hs, bs], writes=[b_Hs])
                    sc.op("dve", lambda: V.tensor_tensor(out=Hs[:, gg, :], in0=ps[4][:, :], in1=Hs[:, gg, :],
                                                         op=ALU.add),
                          reads=[psb[4], b_Hs], writes=[b_Hs])
                    sc.op("act", lambda: A.copy(out=Hb[:, gg, :], in_=Hs[:, gg, :]), reads=[b_Hs],
                          writes=[b_Hb])
                n_ = nst[cb]
                bn = b_nst[cb]
                sc.op("dve", lambda: V.tensor_tensor(out=yy[cb][:], in0=yy[cb][:], in1=Zt[cb][:], op=ALU.mult),
                      reads=[b_yy[cb], b_Zt[cb]], writes=[b_yy[cb]])
                for gg in range(2):
                    gs = slice(gg * 512, (gg + 1) * 512)
                    sc.op("dve", lambda: V.scalar_tensor_tensor(
                        out=junk[:], in0=yy[cb][:, gs], scalar=1.0, in1=yy[cb][:, gs], op0=ALU.mult, op1=ALU.mult,
                        accum_out=n_[:, gg:gg + 1]), reads=[b_yy[cb]], writes=[b_junk, bn])
                emit_rstd(nc, sc, n_[:, 4:6], n_[:, 2:4], n_[:, 0:2], 1.0 / 512, 2, [bn], [bn])
                w0 = SM["ssmw"][0]
                for gg in range(2):
                    gs = slice(gg * 512, (gg + 1) * 512)
                    sc.op("dve", lambda: V.scalar_tensor_tensor(
                        out=ynb[cb][:, gs], in0=yy[cb][:, gs], scalar=n_[:, 4 + gg:5 + gg],
                        in1=sm[:, w0 + gg * 512:w0 + (gg + 1) * 512], op0=ALU.mult, op1=ALU.mult),
                          reads=[b_yy[cb], bn, b_sm], writes=[b_ynb[cb]])
                pT = psbf(0)
                for cc in range(8):
                    sc.op("pe", lambda cc=cc: T.transpose(out=pT[:, cc * 128:(cc + 1) * 128],
                                                          in_=ynb[cb][:, cc * 128:(cc + 1) * 128],
                                                          identity=identb[:]),
                          reads=[b_ynb[cb], b_identb], writes=[psb[0]])
                sc.op("act", lambda: A.copy(out=YNst[ub][:, :, tsl], in_=pT[:, :].rearrange("p (c s) -> p c s", c=8)),
                      reads=[psb[0]], writes=[b_YNst[ub]])
                if g + 1 < NG:
                    conv_piece(g + 1, range(3 * tt, 3 * tt + 3))
            sc.dma("pool", YNTd[:, g * 512:(g + 1) * 512].rearrange("(c p) s -> p c s", p=128), YNst[ub][:],
                   reads=[b_YNst[ub]], writes=[b_YNT[g]])
        _fence(nc, sc)


def _phase_BC(nc, sc, l, lambda_init, ps, psb, psbf, QTd, KTd, Vd, Gd, OGTd, b_QK, b_V, b_G, b_OGT,
              UTd, DTd, Zd, YNTd, b_UT, b_DT, b_Z, b_YNT,
              sm, b_sm, identb, b_identb, maskb, b_maskb, cst, b_cst):
    V, A, T, P = nc.vector, nc.scalar, nc.tensor, nc.gpsimd
    ident_f = cst[:, 0, :]
    tri_f = cst[:, 1, :]
    negm_f = cst[:, 2, :]
    ones_f = cst[:, 3, :]

    def smv(name):
        a, b = SM[name]
        return sm[:, a:b]

    with contextlib.ExitStack() as es2:
        def sb(name, shape, dt):
            return es2.enter_context(nc.sbuf_tensor(f"{name}_BC{l}", list(shape), dt))

        KT = [sb(f"KT{i}", [128, S], BF16) for i in range(2)]
        Va = sb("Va", [128, NT, 130], BF16)
        QZ = [[sb(f"QZ{i}_{t}", [128, 512], BF16) for t in range(2)] for i in range(2)]
        Et = [sb(f"E{i}", [128, 512], BF16) for i in range(4)]
        Gt = [sb(f"Gt{i}", [128, 4, 128], F32) for i in range(2)]
        gw = [sb(f"gw{i}", [128, 4, 128], F32) for i in range(2)]
        o1 = sb("o1", [128, 4, 128], F32)
        od = [sb(f"od{i}", [128, 128], F32) for i in range(2)]
        junkb = sb("junkb", [128, 128], F32)
        st = [sb(f"st{i}", [128, 8], F32) for i in range(2)]
        ogb = [sb(f"ogb{i}", [128, 128], BF16) for i in range(4)]
        ogT = [sb(f"ogT{i}", [128, 512], BF16) for i in range(2)]
        wsub = sb("wsub", [128, 128], F32)
        lam = sb("lam", [128, 8], F32)
        ljunk = sb("ljunk", [128, 64], F32)
        b_KT, b_Gt, b_gw, b_od, b_st, b_ogT = ([Buf(), Buf()] for _ in range(6))
        b_Va = Buf()
        b_ogb = [Buf() for _ in range(4)]
        b_E = [Buf() for _ in range(4)]
        b_o1, b_junkb, b_wsub, b_lam = Buf(), Buf(), Buf(), Buf()
        b_QZ = [[Buf(), Buf()], [Buf(), Buf()]]
        U = sb("U", [128, 12, 515], F32)
        acc = [sb(f"acc{i}", [128, 512], F32) for i in range(2)]
        XC = [sb(f"XC{i}", [128, 12, 512], BF16) for i in range(2)]
        DTg = [sb(f"DTg{i}", [128, 4, 16], F32) for i in range(2)]
        Zt = sb("Zt", [128, D], F32)
        xd = sb("xd", [128, D], BF16)
        xdd = sb("xdd", [128, D], BF16)
        xsk = sb("xsk", [128, D], F32)
        Btm = sb("Btm", [128, 256], BF16)
        sml = [sb(f"sml{i}", [128, 8, 4, 16], F32) for i in range(2)]
        LT = [sb(f"LT{i}", [128, 128], F32) for i in range(8)]
        MT = [sb(f"MT{i}", [128, 128], BF16) for i in range(8)]
        t1 = [sb(f"t1{i}", [128, 512], F32) for i in range(2)]
        yy = sb("yy", [128, D], F32)
        Hs = sb("Hs", [128, 2, 512], F32)
        Hb = sb("Hb", [128, 2, 512], BF16)
        a_row = sb("a_row", [128, 16], F32)
        nst = [sb(f"nst{i}", [128, 8], F32) for i in range(2)]
        junk = sb("junk", [128, 512], F32)
        ynb = sb("ynb", [128, D], BF16)
        YNst = sb("YNst", [128, 8, 512], BF16)
        b_acc, b_XC, b_DTg, b_sml, b_t1, b_nst = ([Buf(), Buf()] for _ in range(6))
        b_U, b_Zt, b_xd, b_xdd, b_xsk, b_Btm, b_yy, b_ynb, b_YNst = (Buf() for _ in range(9))
        b_LT = [Buf() for _ in range(8)]
        b_MT = [Buf() for _ in range(8)]
        b_Hs, b_Hb, b_arow, b_junk = Buf(), Buf(), Buf(), Buf()

        def gen_B():
            sc.op("dve", lambda: V.scalar_tensor_tensor(out=ljunk[:], in0=smv("lq1"), scalar=1.0, in1=smv("lk1"),
                                                        op0=ALU.mult, op1=ALU.mult, accum_out=lam[:, 0:1]),
                  reads=[b_sm], writes=[b_lam])
            sc.op("dve", lambda: V.scalar_tensor_tensor(out=ljunk[:], in0=smv("lq2"), scalar=1.0, in1=smv("lk2"),
                                                        op0=ALU.mult, op1=ALU.mult, accum_out=lam[:, 1:2]),
                  reads=[b_sm, b_lam], writes=[b_lam])
            sc.op("act", lambda: A.activation(out=lam[:, 2:4], in_=lam[:, 0:2], func=AF.Exp), reads=[b_lam],
                  writes=[b_lam])
            sc.op("dve", lambda: V.tensor_tensor(out=lam[:, 4:5], in0=lam[:, 2:3], in1=lam[:, 3:4], op=ALU.subtract),
                  reads=[b_lam], writes=[b_lam])
            sc.op("dve", lambda: V.tensor_scalar(out=lam[:, 5:6], in0=lam[:, 4:5], scalar1=lambda_init, scalar2=-1.0,
                                                 op0=ALU.add, op1=ALU.mult), reads=[b_lam], writes=[b_lam])
            sc.op("dve", lambda: V.tensor_scalar(out=wsub[:], in0=smv("subw"), scalar1=(1.0 - lambda_init),
                                                 scalar2=None, op0=ALU.mult), reads=[b_sm], writes=[b_wsub])
            sc.op("dve", lambda: V.memset(Va[:, :, 128:130], 1.0), writes=[b_Va])
            for i in range(2):
                sc.op("dve", lambda i=i: V.memset(QZ[i][0][64:128, :], 0.0), writes=[b_QZ[i][0]])
                sc.op("dve", lambda i=i: V.memset(QZ[i][1][0:64, :], 0.0), writes=[b_QZ[i][1]])
            sbanks = [0, 1]
            scnt = [0]
            for h in range(4):
                hb = h % 2
                sc.dma("sp", KT[hb][:], KTd[h, :, :], reads=b_QK, writes=[b_KT[hb]])
                for q4 in range(4):
                    sc.dma("sp", Va[:, q4 * 16:(q4 + 1) * 16, 0:128],
                           Vd[q4 * 2048:(q4 + 1) * 2048, h * 128:(h + 1) * 128].rearrange("(t p) e -> p t e", p=128),
                           reads=b_V[q4 * 16:(q4 + 1) * 16], writes=[b_Va])
                for qg in range(NG):
                    qb = qg % 2
                    sc.dma("sp", QZ[qb][0][0:64, :], QTd[h, 0:64, qg * 512:(qg + 1) * 512], reads=[b_QK[qg]],
                           writes=[b_QZ[qb][0]])
                    sc.dma("sp", QZ[qb][1][64:128, :], QTd[h, 64:128, qg * 512:(qg + 1) * 512], reads=[b_QK[qg]],
                           writes=[b_QZ[qb][1]])
                    sc.dma("sp", Gt[qb][:], Gd[qg * 512:(qg + 1) * 512, h * 128:(h + 1) * 128].rearrange(
                        "(i p) e -> p i e", p=128), reads=b_G[qg * 4:qg * 4 + 4], writes=[b_Gt[qb]])
                    sc.op("pool", lambda: P.tensor_tensor(out=gw[qb][:], in0=Gt[qb][:],
                                                          in1=wsub[:].unsqueeze(1).to_broadcast([128, 4, 128]),
                                                          op=ALU.mult),
                          reads=[b_Gt[qb], b_wsub], writes=[b_gw[qb]])
                    for t in range(2):
                        nj = 4 * qg + 4
                        slots = {}

                        def emit_score(j):
                            c0 = max(0, j - 4 * qg) * 128
                            sbk = sbanks[scnt[0] % 2]
                            ei = scnt[0] % 4
                            scnt[0] += 1
                            slots[j] = (c0, ei)
                            sc.op("pe", lambda: T.matmul(
                                ps[sbk][:, c0:512], lhsT=KT[hb][:, j * 128:(j + 1) * 128],
                                rhs=QZ[qb][t][:, c0:512], start=True, stop=True),
                                  reads=[b_KT[hb], b_QZ[qb][t]], writes=[psb[sbk]])
                            sc.op("act", lambda: A.activation(
                                out=Et[ei][:, c0:512], in_=ps[sbk][:, c0:512], func=AF.Exp, scale=0.125),
                                  reads=[psb[sbk]], writes=[b_E[ei]])
                            if j >= 4 * qg:
                                sc.op("dve", lambda: V.tensor_tensor(
                                    out=Et[ei][:, c0:c0 + 128], in0=Et[ei][:, c0:c0 + 128], in1=maskb[:],
                                    op=ALU.mult), reads=[b_E[ei], b_maskb], writes=[b_E[ei]])

                        def emit_pv(j):
                            c0, ei = slots[j]
                            for i in range(max(j, 4 * qg), 4 * qg + 4):
                                ii = i - 4 * qg
                                ob = 2 + ii // 2
                                oc = (ii % 2) * 256
                                sc.op("pe", lambda: T.matmul(
                                    ps[ob][:, oc:oc + 129], lhsT=Et[ei][:, ii * 128:(ii + 1) * 128],
                                    rhs=Va[:, j, 0:129], start=(j == 0 and ii % 2 == 0), stop=(j == i),
                                    skip_group_check=True),
                                      reads=[b_E[ei], b_Va], writes=[psb[ob]])

                        emit_score(0)
                        for j in range(nj):
                            if j + 1 < nj:
                                emit_score(j + 1)
                            emit_pv(j)
                            yield
                        for ii in range(4):
                            ob = 2 + ii // 2
                            oc = (ii % 2) * 256
                            s2 = st[ii % 2]
                            bs2 = b_st[ii % 2]
                            sc.op("dve", lambda: V.reciprocal(out=s2[:, 0:1], in_=ps[ob][:, oc + 128:oc + 129]),
                                  reads=[psb[ob]], writes=[bs2])
                            if t == 0:
                                sc.op("dve", lambda: V.tensor_scalar(
                                    out=o1[:, ii, :], in0=ps[ob][:, oc:oc + 128], scalar1=s2[:, 0:1], scalar2=None,
                                    op0=ALU.mult), reads=[psb[ob], bs2], writes=[b_o1])
                            else:
                                oi = ii % 2
                                sc.op("dve", lambda: V.tensor_tensor(out=s2[:, 1:2], in0=s2[:, 0:1],
                                                                     in1=lam[:, 5:6], op=ALU.mult),
                                      reads=[bs2, b_lam], writes=[bs2])
                                sc.op("dve", lambda: V.scalar_tensor_tensor(
                                    out=od[oi][:], in0=ps[ob][:, oc:oc + 128], scalar=s2[:, 1:2], in1=o1[:, ii, :],
                                    op0=ALU.mult, op1=ALU.add), reads=[psb[ob], bs2, b_o1], writes=[b_od[oi]])
                                sc.op("dve", lambda: V.scalar_tensor_tensor(
                                    out=junkb[:], in0=od[oi][:], scalar=1.0, in1=od[oi][:], op0=ALU.mult,
                                    op1=ALU.mult, accum_out=s2[:, 2:3]), reads=[b_od[oi]], writes=[b_junkb, bs2])
                                emit_rstd(nc, sc, s2[:, 4:5], s2[:, 3:4], s2[:, 2:3], 1.0 / 128, 1, [bs2], [bs2])
                                sc.op("dve", lambda: V.scalar_tensor_tensor(
                                    out=ogb[ii][:], in0=od[oi][:], scalar=s2[:, 4:5], in1=gw[qb][:, ii, :],
                                    op0=ALU.mult, op1=ALU.mult), reads=[b_od[oi], bs2, b_gw[qb]],
                                      writes=[b_ogb[ii]])
                        if t == 1:
                            sbk = sbanks[scnt[0] % 2]
                            scnt[0] += 1
                            pT = psbf(sbk)
                            for ii in range(4):
                                sc.op("pe", lambda: T.transpose(out=pT[:, ii * 128:(ii + 1) * 128],
                                                                in_=ogb[ii][:], identity=identb[:]),
                                      reads=[b_ogb[ii], b_identb], writes=[psb[sbk]])
                            sc.op("act", lambda: A.copy(out=ogT[qb][:], in_=pT[:, 0:512]),
                                  reads=[psb[sbk]], writes=[b_ogT[qb]])
                            sc.dma("pool", OGTd[h * 128:(h + 1) * 128, qg * 512:(qg + 1) * 512], ogT[qb][:],
                                   reads=[b_ogT[qb]], writes=[b_OGT[h][qg]])
                        yield

        cw0 = SM["cw"][0]
        cb0 = SM["cb"][0]
        PB_, PX_, PM_, PY_ = 4, 5, 6, 7

        def load_group(g):
            ub = g % 2
            if g == 0:
                sc.op("dve", lambda: V.memset(U[:, :, 0:3], 0.0), writes=[b_U])
                sc.dma("sp", U[:, :, 3:515], UTd[:, 0:512].rearrange("(c p) s -> p c s", p=128),
                       reads=[b_UT[0]], writes=[b_U])
            else:
                sc.dma("sp", U[:, :, :], UTd[:, g * 512 - 3:g * 512 + 512].rearrange("(c p) s -> p c s", p=128),
                       reads=[b_UT[g - 1], b_UT[g]], writes=[b_U])
            sc.dma("sp", DTg[ub][:], DTd[g * 512:(g + 1) * 512, :].rearrange("(i p) e -> p i e", p=128),
                   reads=b_DT[4 * g:4 * g + 4], writes=[b_DTg[ub]])

        def conv_piece(g, ccs):
            ub = g % 2
            for cc in ccs:
                ab = cc % 2
                w = lambda k, cc=cc: sm[:, cw0 + cc * 4 + k:cw0 + cc * 4 + k + 1]
                sc.op("dve", lambda: V.tensor_scalar(
                    out=acc[ab][:], in0=U[:, cc, 3:515], scalar1=w(3), scalar2=sm[:, cb0 + cc:cb0 + cc + 1],
                    op0=ALU.mult, op1=ALU.add), reads=[b_U, b_sm], writes=[b_acc[ab]])
                for k in (2, 1, 0):
                    sc.op("dve", lambda: V.scalar_tensor_tensor(
                        out=acc[ab][:], in0=U[:, cc, k:k + 512], scalar=w(k), in1=acc[ab][:],
                        op0=ALU.mult, op1=ALU.add), reads=[b_U, b_sm, b_acc[ab]], writes=[b_acc[ab]])
                sc.op("act", lambda: A.activation(out=XC[ub][:, cc, :], in_=acc[ab][:], func=AF.Silu),
                      reads=[b_acc[ab]], writes=[b_XC[ub]])

        def conv_dve(g, cc):
            ab = cc % 2
            w = lambda k: sm[:, cw0 + cc * 4 + k:cw0 + cc * 4 + k + 1]
            sc.op("dve", lambda: V.tensor_scalar(
                out=acc[ab][:], in0=U[:, cc, 3:515], scalar1=w(3), scalar2=sm[:, cb0 + cc:cb0 + cc + 1],
                op0=ALU.mult, op1=ALU.add), reads=[b_U, b_sm], writes=[b_acc[ab]])
            for k in (2, 1, 0):
                sc.op("dve", lambda: V.scalar_tensor_tensor(
                    out=acc[ab][:], in0=U[:, cc, k:k + 512], scalar=w(k), in1=acc[ab][:],
                    op0=ALU.mult, op1=ALU.add), reads=[b_U, b_sm, b_acc[ab]], writes=[b_acc[ab]])

        def conv_act(g, cc):
            ab = cc % 2
            sc.op("act", lambda: A.activation(out=XC[g % 2][:, cc, :], in_=acc[ab][:], func=AF.Silu),
                  reads=[b_acc[ab]], writes=[b_XC[g % 2]])

        def conv_skewed(g, ccs):
            ccs = list(ccs)
            for i, cc in enumerate(ccs):
                conv_dve(g, cc)
                if i > 0:
                    conv_act(g, ccs[i - 1])
                yield
            conv_act(g, ccs[-1])
            yield

        def gen_C():
            sc.op("act", lambda: A.activation(out=a_row[:], in_=smv("alog"), func=AF.Exp), reads=[b_sm],
                  writes=[b_arow])
            sc.op("dve", lambda: V.tensor_scalar(out=a_row[:], in0=a_row[:], scalar1=-1.0, scalar2=None,
                                                 op0=ALU.mult), reads=[b_arow], writes=[b_arow])
            sc.op("dve", lambda: V.memset(Hs[:], 0.0), writes=[b_Hs])
            sc.op("dve", lambda: V.memset(Hb[:], 0.0), writes=[b_Hb])
            load_group(0)
            yield from conv_skewed(0, range(12))
            for g in range(NG):
                ub = g % 2
                if g + 1 < NG:
                    load_group(g + 1)
                s_ = sml[ub]
                bs = b_sml[ub]
                row = lambda r: s_[:, r, :, :]
                row2 = lambda r: s_[:, r, :, :].rearrange("p t h -> p (t h)")
                sc.op("dve", lambda: V.tensor_tensor(out=row(0), in0=DTg[ub][:],
                                                     in1=a_row[:].unsqueeze(1).to_broadcast([128, 4, 16]),
                                                     op=ALU.mult), reads=[b_DTg[ub], b_arow], writes=[bs])
                yield
                sc.op("pe", lambda: T.matmul(ps[PM_][:, 128:192], lhsT=tri_f, rhs=row2(0), start=True, stop=True),
                      reads=[bs, b_cst], writes=[psb[PM_]])
                sc.op("pe", lambda: T.matmul(ps[PM_][:, 192:256], lhsT=ones_f, rhs=row2(0), start=True, stop=True),
                      reads=[bs, b_cst], writes=[psb[PM_]])
                yield
                sc.op("dve", lambda: V.tensor_copy(out=row2(1), in_=ps[PM_][:, 128:192]), reads=[psb[PM_]],
                      writes=[bs])
                sc.op("dve", lambda: V.tensor_scalar(out=row2(2), in0=ps[PM_][:, 128:192], scalar1=-1.0,
                                                     scalar2=None, op0=ALU.mult), reads=[psb[PM_]], writes=[bs])
                sc.op("dve", lambda: V.tensor_tensor(out=row2(3), in0=ps[PM_][:, 192:256], in1=row2(1),
                                                     op=ALU.subtract), reads=[psb[PM_], bs], writes=[bs])
                yield
                sc.op("act", lambda: A.activation(out=row2(4), in_=row2(3), func=AF.Exp), reads=[bs], writes=[bs])
                sc.op("act", lambda: A.activation(out=row2(5), in_=row2(1), func=AF.Exp), reads=[bs], writes=[bs])
                sc.op("act", lambda: A.activation(out=row2(6), in_=ps[PM_][:, 192:256], func=AF.Exp),
                      reads=[psb[PM_], bs], writes=[bs])
                yield
                sc.op("dve", lambda: V.tensor_copy(out=s_[:, 7, 0, 0:1], in_=s_[:, 6, 0, 0:1]), reads=[bs],
                      writes=[bs])
                yield
                for tt in range(4):
                    c = 4 * g + tt
                    tsl = slice(tt * 128, (tt + 1) * 128)
                    sc.dma("sp", Zt[:], Zd[c * 128:(c + 1) * 128, :], reads=[b_Z[c]], writes=[b_Zt])
                    pX = psbf(PX_)
                    for cc in range(8):
                        sc.op("pe", lambda: T.transpose(out=pX[:, cc * 128:(cc + 1) * 128], in_=XC[ub][:, cc, tsl],
                                                        identity=identb[:]),
                              reads=[b_XC[ub], b_identb], writes=[psb[PX_]])
                    pB = psbf(PM_)
                    for k in range(2):
                        sc.op("pe", lambda: T.transpose(out=pB[:, k * 128:(k + 1) * 128], in_=XC[ub][:, 8 + k, tsl],
                                                        identity=identb[:]),
                              reads=[b_XC[ub], b_identb], writes=[psb[PM_]])
                    yield
                    pX3 = pX[:, :].rearrange("p (h e) -> p h e", e=64)
                    bc = lambda r: s_[:, r, tt, :].unsqueeze(2).to_broadcast([128, 16, 64])
                    v3 = lambda tl: tl[:].rearrange("p (h e) -> p h e", e=64)
                    sc.op("dve", lambda: V.tensor_tensor(out=v3(xd), in0=pX3,
                                                         in1=DTg[ub][:, tt, :].unsqueeze(2).to_broadcast(
                                                             [128, 16, 64]), op=ALU.mult),
                          reads=[psb[PX_], b_DTg[ub]], writes=[b_xd])
                    sc.op("dve", lambda: V.tensor_tensor(out=v3(xsk), in0=pX3,
                                                         in1=smv("dsk").unsqueeze(2).to_broadcast([128, 16, 64]),
                                                         op=ALU.mult), reads=[psb[PX_], b_sm], writes=[b_xsk])
                    sc.op("act", lambda: A.copy(out=Btm[:], in_=pB[:, 0:256]), reads=[psb[PM_]], writes=[b_Btm])
                    yield
                    sc.op("dve", lambda: V.tensor_tensor(out=v3(xdd), in0=v3(xd), in1=bc(4), op=ALU.mult),
                          reads=[b_xd, bs], writes=[b_xdd])
                    yield
                    for gg in range(2):
                        gs = slice(gg * 512, (gg + 1) * 512)
                        sc.op("pe", lambda: T.matmul(ps[PM_][:, 256:384], lhsT=XC[ub][:, 8 + gg, tsl],
                                                     rhs=XC[ub][:, 10 + gg, tsl], start=True, stop=True),
                              reads=[b_XC[ub]], writes=[psb[PM_]])
                        sc.op("pe", lambda: T.matmul(ps[PY_][:, :], lhsT=XC[ub][:, 10 + gg, tsl], rhs=Hb[:, gg, :],
                                                     start=True, stop=True),
                              reads=[b_XC[ub], b_Hb], writes=[psb[PY_]])
                        sc.op("pe", lambda: T.matmul(ps[PX_][:, :], lhsT=Btm[:, gg * 128:(gg + 1) * 128],
                                                     rhs=xdd[:, gs], start=True, stop=True),
                              reads=[b_Btm, b_xdd], writes=[psb[PX_]])
                        yield
                        tb = gg
                        sc.op("dve", lambda: V.tensor_tensor(
                            out=t1[tb][:].rearrange("p (h e) -> p h e", e=64),
                            in0=ps[PY_][:, :].rearrange("p (h e) -> p h e", e=64),
                            in1=s_[:, 5, tt, gg * 8:(gg + 1) * 8].unsqueeze(2).to_broadcast([128, 8, 64]),
                            op=ALU.mult), reads=[psb[PY_], bs], writes=[b_t1[tb]])
                        sc.op("dve", lambda: V.tensor_tensor(
                            out=Hs[:, gg, :].rearrange("p (h e) -> p h e", e=64),
                            in0=Hs[:, gg, :].rearrange("p (h e) -> p h e", e=64),
                            in1=s_[:, 6, tt, gg * 8:(gg + 1) * 8].unsqueeze(2).to_broadcast([128, 8, 64]),
                            op=ALU.mult), reads=[b_Hs, bs], writes=[b_Hs])
                        sc.op("dve", lambda: V.tensor_tensor(out=Hs[:, gg, :], in0=ps[PX_][:, :], in1=Hs[:, gg, :],
                                                             op=ALU.add),
                              reads=[psb[PX_], b_Hs], writes=[b_Hs])
                        yield
                        sc.op("act", lambda: A.copy(out=Hb[:, gg, :], in_=Hs[:, gg, :]), reads=[b_Hs],
                              writes=[b_Hb])
                        for half in range(2):
                            for j4 in range(4):
                                j = half * 4 + j4
                                hh = gg * 8 + j
                                cs = slice(j4 * 128, (j4 + 1) * 128)
                                sc.op("pe", lambda: T.matmul(
                                    ps[PB_][:, cs], lhsT=s_[:, 0, tt, hh:hh + 1].to_broadcast([128, 128]), rhs=tri_f,
                                    start=True, stop=False), reads=[bs, b_cst], writes=[psb[PB_]])
                                sc.op("pe", lambda: T.matmul(ps[PB_][:, cs], lhsT=ident_f, rhs=negm_f,
                                                             start=False, stop=True),
                                      reads=[b_cst], writes=[psb[PB_]])
                            yield
                            for j4 in range(4):
                                j = half * 4 + j4
                                hh = gg * 8 + j
                                cs = slice(j4 * 128, (j4 + 1) * 128)
                                sc.op("act", lambda: A.activation(
                                    out=LT[j][:], in_=ps[PB_][:, cs], func=AF.Exp, bias=s_[:, 2, tt, hh:hh + 1],
                                    scale=1.0), reads=[psb[PB_], bs], writes=[b_LT[j]])
                            yield
                            for j4 in range(4):
                                j = half * 4 + j4
                                sc.op("dve", lambda: V.tensor_tensor(out=MT[j][:], in0=LT[j][:],
                                                                     in1=ps[PM_][:, 256:384], op=ALU.mult),
                                      reads=[b_LT[j], psb[PM_]], writes=[b_MT[j]])
                            yield
                        for j in range(8):
                            hh = gg * 8 + j
                            sc.op("pe", lambda: T.matmul(
                                ps[PY_][:, j * 64:(j + 1) * 64], lhsT=MT[j][:], rhs=xd[:, hh * 64:(hh + 1) * 64],
                                start=True, stop=True), reads=[b_MT[j], b_xd], writes=[psb[PY_]])
                        yield
                        sc.op("dve", lambda: V.tensor_tensor(out=t1[tb][:], in0=ps[PY_][:, :], in1=t1[tb][:],
                                                             op=ALU.add),
                              reads=[psb[PY_], b_t1[tb]], writes=[b_t1[tb]])
                        yield
                        sc.op("pool", lambda: P.tensor_tensor(out=yy[:, gs], in0=t1[tb][:], in1=xsk[:, gs],
                                                              op=ALU.add),
                              reads=[b_t1[tb], b_xsk], writes=[b_yy])
                        yield
                    n_ = nst[c % 2]
                    bn = b_nst[c % 2]
                    sc.op("dve", lambda: V.tensor_tensor(out=yy[:], in0=yy[:], in1=Zt[:], op=ALU.mult),
                          reads=[b_yy, b_Zt], writes=[b_yy])
                    for gg in range(2):
                        gs = slice(gg * 512, (gg + 1) * 512)
                        sc.op("dve", lambda: V.scalar_tensor_tensor(
                            out=junk[:], in0=yy[:, gs], scalar=1.0, in1=yy[:, gs], op0=ALU.mult, op1=ALU.mult,
                            accum_out=n_[:, gg:gg + 1]), reads=[b_yy], writes=[b_junk, bn])
                    sc.op("dve", lambda: V.tensor_scalar(out=n_[:, 2:4], in0=n_[:, 0:2], scalar1=1.0 / 512,
                                                         scalar2=EPS, op0=ALU.mult, op1=ALU.add),
                          reads=[bn], writes=[bn])
                    yield
                    sc.op("pool", lambda: P.tensor_tensor(out=n_[:, 4:6], in0=n_[:, 2:4], in1=_G["negh"][:, 0:2],
                                                          op=ALU.pow), reads=[bn, _G["b_negh"]], writes=[bn])
                    yield
                    w0 = SM["ssmw"][0]
                    for gg in range(2):
                        gs = slice(gg * 512, (gg + 1) * 512)
                        sc.op("dve", lambda: V.scalar_tensor_tensor(
                            out=ynb[:, gs], in0=yy[:, gs], scalar=n_[:, 4 + gg:5 + gg],
                            in1=sm[:, w0 + gg * 512:w0 + (gg + 1) * 512], op0=ALU.mult, op1=ALU.mult),
                              reads=[b_yy, bn, b_sm], writes=[b_ynb])
                    yield
                    pT = psbf(PX_)
                    for cc in range(8):
                        sc.op("pe", lambda: T.transpose(out=pT[:, cc * 128:(cc + 1) * 128],
                                                        in_=ynb[:, cc * 128:(cc + 1) * 128], identity=identb[:]),
                              reads=[b_ynb, b_identb], writes=[psb[PX_]])
                    yield
                    sc.op("act", lambda: A.copy(out=YNst[:, :, tsl],
                                                in_=pT[:, :].rearrange("p (c s) -> p c s", c=8)),
                          reads=[psb[PX_]], writes=[b_YNst])
                    if g + 1 < NG:
                        yield from conv_skewed(g + 1, range(3 * tt, 3 * tt + 3))
                    else:
                        yield
                sc.dma("pool", YNTd[:, g * 512:(g + 1) * 512].rearrange("(c p) s -> p c s", p=128), YNst[:],
                       reads=[b_YNst], writes=[b_YNT[g]])
                yield

        gB = gen_B()
        gC = gen_C()
        nB = sum(2 * (4 * qg + 4 + 1) for qg in range(NG)) * 4
        nC = 13 + NG * (6 + 4 * (3 + 2 * 12 + 9))
        ratio = BC_RATIO * nC / nB
        accf = 0.0
        doneB = doneC = False
        while not (doneB and doneC):
            if not doneB:
                try:
                    next(gB)
                except StopIteration:
                    doneB = True
            accf += ratio
            while (accf >= 1.0 or doneB) and not doneC:
                accf -= 1.0
                try:
                    next(gC)
                except StopIteration:
                    doneC = True
                if doneB:
                    continue
        _fence(nc, sc)


def _phase_D(nc, sc, l, last, ps, psb, psbf, x_src, b_xsrc, OGTd, YNTd, MGd, X1d, y_out, b_OGT, b_YNT, b_MG,
             b_X1, b_Y, watt_in, wssm_in, wout_in, gate_b, b_gate, fnw_in):
    V, A, T, P = nc.vector, nc.scalar, nc.tensor, nc.gpsimd
    with contextlib.ExitStack() as es2:
        def sb(name, shape, dt):
            return es2.enter_context(nc.sbuf_tensor(f"{name}_D{l}", list(shape), dt))

        Watt = sb("Watt", [128, 4, D], BF16)
        Wssm = sb("Wssm", [128, 8, D], BF16)
        Wout = sb("Wout", [128, 8, D], BF16)
        fnw = sb("fnw", [128, D], F32)
        b_Wd, b_fnw = Buf(), Buf()
        wst = [sb(f"wst{i}", [128, D], F32) for i in range(2)]
        b_wst = [Buf(), Buf()]
        k = 0
        for (src, dst, nk) in ((watt_in, Watt, 4), (wssm_in, Wssm, 8), (wout_in, Wout, 8)):
            for kc in range(nk):
                i = k % 2
                k += 1
                sc.dma("sp", wst[i][:], src[l, kc * 128:(kc + 1) * 128, :], writes=[b_wst[i]])
                sc.op("dve", lambda dst=dst, kc=kc, i=i: V.tensor_copy(out=dst[:, kc, :], in_=wst[i][:]),
                      reads=[b_wst[i]], writes=[b_Wd])
        if last:
            sc.dma("sp", fnw[:], fnw_in[:, :], writes=[b_fnw])

        OGt = [sb(f"OGt{i}", [128, 4, 512], BF16) for i in range(2)]
        YNt = [sb(f"YNt{i}", [128, 8, 512], BF16) for i in range(2)]
        mga = [sb(f"mga{i}", [128, 512], F32) for i in range(2)]
        mgs = [sb(f"mgs{i}", [128, 512], F32) for i in range(2)]
        b_mga, b_mgs = [Buf(), Buf()], [Buf(), Buf()]
        xt = [sb(f"xt{i}", [128, D], F32) for i in range(2)]
        m1 = [sb(f"m1{i}", [128, 512], F32) for i in range(2)]
        m2 = [sb(f"m2{i}", [128, 512], F32) for i in range(2)]
        mT = [sb(f"mT{i}", [128, 8, 512], BF16) for i in range(2)]
        xo = [sb(f"xo{i}", [128, D], F32) for i in range(2)]
        yo = [sb(f"yo{i}", [128, D], F32) for i in range(2)]
        junk = sb("junk", [128, D], F32)
        fst = [sb(f"fst{i}", [128, 4], F32) for i in range(2)]
        b_OGt, b_YNt, b_MGt_unused, b_xt, b_m1, b_m2, b_mT, b_xo, b_yo, b_fst = ([Buf(), Buf()] for _ in range(10))
        b_junk = Buf()
        for g in range(NG):
            gb = g % 2
            sc.dma("sp", OGt[gb][:], OGTd[:, g * 512:(g + 1) * 512].rearrange("(c p) s -> p c s", p=128),
                   reads=[b_OGT[h][g] for h in range(4)], writes=[b_OGt[gb]])
            sc.dma("sp", YNt[gb][:], YNTd[:, g * 512:(g + 1) * 512].rearrange("(c p) s -> p c s", p=128),
                   reads=[b_YNT[g]], writes=[b_YNt[gb]])
            for dc in range(8):
                db = dc % 2
                dsl = slice(dc * 128, (dc + 1) * 128)
                sc.dma("sp", mga[db][:], MGd[dc * 128:(dc + 1) * 128, g * 512:(g + 1) * 512], reads=[b_MG[g]],
                       writes=[b_mga[db]])
                sc.dma("sp", mgs[db][:], MGd[(8 + dc) * 128:(9 + dc) * 128, g * 512:(g + 1) * 512], reads=[b_MG[g]],
                       writes=[b_mgs[db]])
                for kc in range(4):
                    sc.op("pe", lambda kc=kc, db=db, dsl=dsl: T.matmul(ps[db][:, :], lhsT=Watt[:, kc, dsl],
                                                                       rhs=OGt[gb][:, kc, :],
                                                                       start=(kc == 0), stop=(kc == 3)),
                          reads=[b_Wd, b_OGt[gb]], writes=[psb[db]])
                for kc in range(8):
                    sc.op("pe", lambda kc=kc, db=db, dsl=dsl: T.matmul(ps[2 + db][:, :], lhsT=Wssm[:, kc, dsl],
                                                                       rhs=YNt[gb][:, kc, :],
                                                                       start=(kc == 0), stop=(kc == 7)),
                          reads=[b_Wd, b_YNt[gb]], writes=[psb[2 + db]])
                sc.op("dve", lambda db=db, dc=dc: V.tensor_tensor(out=m1[db][:], in0=ps[db][:, :],
                                                                  in1=mga[db][:], op=ALU.mult),
                      reads=[psb[db], b_mga[db]], writes=[b_m1[db]])
                sc.op("dve", lambda db=db, dc=dc: V.tensor_tensor(out=m2[db][:], in0=ps[2 + db][:, :],
                                                                  in1=mgs[db][:], op=ALU.mult),
                      reads=[psb[2 + db], b_mgs[db]], writes=[b_m2[db]])
                sc.op("pool", lambda db=db, dc=dc: P.tensor_tensor(out=mT[gb][:, dc, :], in0=m1[db][:],
                                                                   in1=m2[db][:], op=ALU.add),
                      reads=[b_m1[db], b_m2[db]], writes=[b_mT[gb]])
            for tt in range(4):
                t = 4 * g + tt
                tb = t % 2
                tsl = slice(tt * 128, (tt + 1) * 128)
                rd = [b_xsrc[t]] if b_xsrc is not None else []
                sc.dma("sp", xt[tb][:], x_src[t * 128:(t + 1) * 128, :], reads=rd, writes=[b_xt[tb]])
                for n in range(2):
                    ob = 4 + n
                    ns = slice(n * 512, (n + 1) * 512)
                    for dc in range(8):
                        sc.op("pe", lambda dc=dc, ob=ob, ns=ns: T.matmul(ps[ob][:, :], lhsT=mT[gb][:, dc, tsl],
                                                                         rhs=Wout[:, dc, ns],
                                                                         start=(dc == 0), stop=(dc == 7)),
                              reads=[b_mT[gb], b_Wd], writes=[psb[ob]])
                    sc.op("dve", lambda ob=ob, ns=ns: V.tensor_tensor(out=xo[tb][:, ns], in0=ps[ob][:, :],
                                                                      in1=gate_b[:, ns], op=ALU.mult),
                          reads=[psb[ob], b_gate], writes=[b_xo[tb]])
                    sc.op("pool", lambda ns=ns: P.tensor_tensor(out=xo[tb][:, ns], in0=xo[tb][:, ns],
                                                                in1=xt[tb][:, ns], op=ALU.add),
                          reads=[b_xo[tb], b_xt[tb]], writes=[b_xo[tb]])
                if not last:
                    sc.dma("pool", X1d[t * 128:(t + 1) * 128, :], xo[tb][:], reads=[b_xo[tb]], writes=[b_X1[t]])
                else:
                    f = fst[tb]
                    sc.op("dve", lambda f=f: V.scalar_tensor_tensor(out=junk[:], in0=xo[tb][:], scalar=1.0,
                                                                    in1=xo[tb][:], op0=ALU.mult, op1=ALU.mult,
                                                                    accum_out=f[:, 0:1]),
                          reads=[b_xo[tb]], writes=[b_junk, b_fst[tb]])
                    emit_rstd(nc, sc, f[:, 2:3], f[:, 1:2], f[:, 0:1], 1.0 / D, 1, [b_fst[tb]], [b_fst[tb]])
                    sc.op("dve", lambda f=f: V.scalar_tensor_tensor(out=yo[tb][:], in0=xo[tb][:], scalar=f[:, 2:3],
                                                                    in1=fnw[:], op0=ALU.mult, op1=ALU.mult),
                          reads=[b_xo[tb], b_fst[tb], b_fnw], writes=[b_yo[tb]])
                    sc.dma("pool", y_out[t * 128:(t + 1) * 128, :], yo[tb][:], reads=[b_yo[tb]], writes=[b_Y[t]])
        _fence(nc, sc)


def _consts():
    k = np.arange(128)
    ident = np.eye(128, dtype=np.float32)
    tri = (k[:, None] <= k[None, :]).astype(np.float32)
    negm = np.where(k[None, :] >= k[:, None], 0.0, -30000.0).astype(np.float32)
    ones = np.ones((128, 128), np.float32)
    return np.ascontiguousarray(np.stack([ident, tri, negm, ones], axis=1))


def _smalls(inp):
    out = np.zeros((DEPTH, 128, NS), np.float32)

    def rep(v):
        return np.broadcast_to(np.asarray(v, np.float32)[None, :], (128, len(v)))

    for l in range(DEPTH):
        b_ada = inp["b_ada"][l]
        o = out[l]
        a, b = SM["bada"]; o[:, a:b] = b_ada[:2048].reshape(16, 128).T
        a, b = SM["nw"]; o[:, a:b] = inp["norm_w"][l].reshape(8, 128).T
        a, b = SM["dtb"]; o[:, a:b] = rep(inp["dt_bias"][l])
        a, b = SM["alog"]; o[:, a:b] = rep(inp["a_log"][l])
        a, b = SM["dsk"]; o[:, a:b] = rep(inp["d_skip"][l])
        a, b = SM["cw"]; o[:, a:b] = inp["conv_w"][l].reshape(4, 12, 128).transpose(2, 1, 0).reshape(128, 48)
        a, b = SM["cb"]; o[:, a:b] = inp["conv_b"][l].reshape(12, 128).T
        a, b = SM["subw"]; o[:, a:b] = rep(inp["attn_subln_w"][l])
        a, b = SM["lq1"]; o[:, a:b] = rep(inp["lambda_q1"][l])
        a, b = SM["lk1"]; o[:, a:b] = rep(inp["lambda_k1"][l])
        a, b = SM["lq2"]; o[:, a:b] = rep(inp["lambda_q2"][l])
        a, b = SM["lk2"]; o[:, a:b] = rep(inp["lambda_k2"][l])
        a, b = SM["bgate"]; o[:, a:b] = rep(b_ada[2048:])
        a, b = SM["ssmw"]; o[:, a:b] = rep(inp["ssm_norm_w"][l])
    return out


def make_in_maps(inp, n_cores=8):
    inp = {k: np.asarray(v) for k, v in inp.items()}
    cst = _consts()
    smalls = _smalls(inp)
    invf = (np.float32(ROPE_THETA) ** (-np.arange(0, 16, 2, dtype=np.float32) / np.float32(16))).astype(np.float32)
    invf = np.ascontiguousarray(np.broadcast_to(invf[None, :], (128, 8)))
    fnw = np.ascontiguousarray(np.broadcast_to(inp["final_norm_w"].astype(np.float32)[None, :], (128, D)))
    shared = dict(cst=cst, smalls=smalls, invf=invf, fnw=fnw,
                  w_ada=np.ascontiguousarray(inp["w_ada"], dtype=np.float32),
                  w_in=np.ascontiguousarray(inp["w_in"], dtype=np.float32),
                  w_att=np.ascontiguousarray(inp["w_att_branch"], dtype=np.float32),
                  w_ssm=np.ascontiguousarray(inp["w_ssm_branch"], dtype=np.float32),
                  w_out=np.ascontiguousarray(inp["w_out"], dtype=np.float32))
    maps = []
    for core in range(n_cores):
        b = core % 4
        c = inp["c"][b].astype(np.float32)
        cpk = c.reshape(8, 128).T
        m = dict(shared)
        m["x"] = np.ascontiguousarray(inp["x"][b], dtype=np.float32)
        m["pos"] = np.ascontiguousarray(inp["positions"][b].astype(np.int32).reshape(NT, 128).T)
        m["cT2"] = np.ascontiguousarray(np.repeat(cpk[:, :, None], 2, axis=2))
        m["cB"] = np.ascontiguousarray(np.repeat(cpk[:, :, None], 128, axis=2))
        maps.append(m)
    return maps


_NC_CACHE = {}


def kernel(**inputs):
    if "nc" not in _NC_CACHE:
        _NC_CACHE["nc"] = build_program()
    nc = _NC_CACHE["nc"]
    maps = make_in_maps(inputs, 8)
    res = run_bass_kernel_spmd(nc, maps, core_ids=list(range(8)))
    out = np.stack([np.asarray(res.results[b]["y"], dtype=np.float32) for b in range(4)], axis=0)
    return out
```
